# Optimizing a Trainium2 kernel written in Bass

```python
import math
import jax, jax.numpy as jnp
from jax import lax
import numpy as np

D_MODEL = 1024
BATCH = 8
SEQ = 2048
DEPTH = 4
DEC_BATCH = 128
DEC_SEQ = 1
PAST_LEN = 16384
PAGE_SIZE = 128

HEAD_DIM = 64
A_WIDTH = 3 * D_MODEL // 8
B_WIDTH = 3 * D_MODEL // 8
C_WIDTH = D_MODEL - A_WIDTH - B_WIDTH
H_A = A_WIDTH // HEAD_DIM
H_B = B_WIDTH // HEAD_DIM
H_C = C_WIDTH // HEAD_DIM
CONV_W = 4
CHUNK = 64
W_LORA = 64
A_LORA = 64
G_LORA = 128
V_LORA = 32
A_COLS = 4 * A_WIDTH + 2 * H_A
B_COLS = 3 * B_WIDTH + W_LORA + A_LORA + G_LORA
C_COLS = 4 * C_WIDTH
IN_COLS = A_COLS + B_COLS + C_COLS
D_FF = -(-8 * D_MODEL // (3 * 256)) * 256
RMS_EPS = 1e-6
RWKV_LN_EPS = 64e-5

kernel_name = 'hybrid_gdn_rwkv7_hgrn2_decode_step'


def rmsnorm(x, w):
    xf = x.astype(jnp.float32)
    y = xf * lax.rsqrt(jnp.mean(xf * xf, -1, keepdims=True) + RMS_EPS)
    return (y * w.astype(jnp.float32)).astype(x.dtype)


def gated_rmsnorm(o, w, z):
    return o * lax.rsqrt(jnp.mean(o * o, -1, keepdims=True) + RMS_EPS) * w.astype(jnp.float32) * jax.nn.silu(z)


def l2norm(x):
    return x * lax.rsqrt(jnp.sum(x * x, -1, keepdims=True) + 1e-6)


def split_heads(x, n):
    return x.reshape(x.shape[:-1] + (n, x.shape[-1] // n))


def merge_heads(x):
    return x.reshape(x.shape[:-2] + (x.shape[-2] * x.shape[-1],))


def masked_exp(mask, d):
    return jnp.where(mask, jnp.exp(jnp.where(mask, d, 0.0)), 0.0)


def causal_conv(x, buf, w):
    t = x.shape[1]
    xc = jnp.concatenate([buf, x], axis=1)
    y = xc[:, 0:t] * w[0]
    for j in range(1, CONV_W):
        y = y + xc[:, j:j + t] * w[j]
    return y, xc[:, t:]


def to_chunks(x, c):
    b, t = x.shape[:2]
    n = -(-t // c)
    x = jnp.pad(x, [(0, 0), (0, n * c - t)] + [(0, 0)] * (x.ndim - 2))
    x = x.reshape((b, n, c) + x.shape[2:])
    return jnp.swapaxes(jnp.moveaxis(x, 1, 0), 2, 3)


def from_chunks(o, t):
    o = jnp.moveaxis(jnp.swapaxes(o, 2, 3), 0, 1)
    return o.reshape((o.shape[0], o.shape[1] * o.shape[2]) + o.shape[3:])[:, :t]


def gdn_chunked(q, k, v, g, beta, s0):
    t = q.shape[1]
    c = min(CHUNK, t)
    dv = v.shape[-1]
    qc, kc, vc, gc, bc = (to_chunks(a, c) for a in (q, k, v, g, beta))
    gcum = jnp.cumsum(gc, axis=-1)
    causal = jnp.tril(jnp.ones((c, c), bool))
    strict = jnp.tril(jnp.ones((c, c), bool), -1)
    decay_in = masked_exp(causal, gcum[..., :, None] - gcum[..., None, :])
    m = jnp.where(strict, bc[..., :, None] * jnp.einsum('nbhid,nbhjd->nbhij', kc, kc) * decay_in, 0.0)
    a_mat = m + jnp.eye(c, dtype=m.dtype)
    rhs = jnp.concatenate([vc * bc[..., None], kc * (bc * jnp.exp(gcum))[..., None]], axis=-1)
    sol = lax.linalg.triangular_solve(a_mat, rhs, left_side=True, lower=True, unit_diagonal=True)
    u, w = sol[..., :dv], sol[..., dv:]
    qk = jnp.einsum('nbhid,nbhjd->nbhij', qc, kc) * decay_in
    q_dec = qc * jnp.exp(gcum)[..., None]
    k_dec = kc * jnp.exp(gcum[..., -1:] - gcum)[..., None]
    g_last = jnp.exp(gcum[..., -1])

    def step(s, xs):
        u_i, w_i, qk_i, qd_i, kd_i, gl_i = xs
        v_new = u_i - jnp.einsum('bhck,bhkv->bhcv', w_i, s)
        o = jnp.einsum('bhck,bhkv->bhcv', qd_i, s) + jnp.einsum('bhij,bhjv->bhiv', qk_i, v_new)
        s = s * gl_i[..., None, None] + jnp.einsum('bhck,bhcv->bhkv', kd_i, v_new)
        return s, o

    s, o = lax.scan(step, s0, (u, w, qk, q_dec, k_dec, g_last))
    return from_chunks(o, t), s


def hgrn_chunked(q, k, i, log_f, s0):
    t = q.shape[1]
    c = min(CHUNK, t)
    qc, kc, ic, fc = (to_chunks(a, c) for a in (q, k, i, log_f))
    bcum = jnp.cumsum(fc, axis=-2)
    causal = jnp.tril(jnp.ones((c, c), bool))[..., None]

    def step(s, xs):
        q_i, k_i, v_i, b_i = xs
        dec = masked_exp(causal, b_i[..., :, None, :] - b_i[..., None, :, :])
        att = jnp.einsum('bhid,bhjd,bhijd->bhij', q_i, k_i, dec)
        o = jnp.einsum('bhid,bhdv->bhiv', q_i * jnp.exp(b_i), s) + jnp.einsum('bhij,bhjv->bhiv', att, v_i)
        bl = b_i[..., -1, :]
        s = jnp.exp(bl)[..., None] * s + jnp.einsum('bhcd,bhcv->bhdv', k_i * jnp.exp(bl[..., None, :] - b_i), v_i)
        return s, o

    s, o = lax.scan(step, s0, (qc, kc, ic, bcum))
    return from_chunks(o, t), s


def rwkv7_scan(r, decay, k, v, kk, a, s0):
    def step(s, xs):
        r_t, w_t, k_t, v_t, kk_t, a_t = xs
        sa = jnp.einsum('bhvk,bhk->bhv', s, -kk_t)
        s = s * w_t[:, :, None, :] + sa[..., None] * (kk_t * a_t)[:, :, None, :] + v_t[..., None] * k_t[:, :, None, :]
        return s, jnp.einsum('bhvk,bhk->bhv', s, r_t)

    xs = tuple(jnp.moveaxis(a_, 1, 0) for a_ in (r, decay, k, v, kk, a))
    s, y = lax.scan(step, s0, xs)
    return jnp.moveaxis(y, 0, 1), s


def gdn_mixer(p, s0, conv_buf, P, l):
    qkv, z, b, a = jnp.split(p, [3 * A_WIDTH, 4 * A_WIDTH, 4 * A_WIDTH + H_A], axis=-1)
    qkv, new_buf = causal_conv(qkv, conv_buf.astype(jnp.float32), P['gdn_conv_w'][l].astype(jnp.float32))
    qkv = jax.nn.silu(qkv)
    q, k, v = (split_heads(t_, H_A) for t_ in jnp.split(qkv, 3, axis=-1))
    q = l2norm(q) * HEAD_DIM ** -0.5
    k = l2norm(k)
    beta = jax.nn.sigmoid(b)
    g = -jnp.exp(P['gdn_a_log'][l].astype(jnp.float32)) * jax.nn.softplus(a + P['gdn_dt_bias'][l])
    o, s = gdn_chunked(q, k, v, g, beta, s0.astype(jnp.float32))
    o = gated_rmsnorm(o, P['gdn_norm_w'][l], split_heads(z, H_A))
    return merge_heads(o), s, new_buf


def rwkv_mixer(p, s0, shift_buf, v_first, P, l):
    prev = jnp.concatenate([shift_buf.astype(jnp.float32)[:, None], p[:, :-1]], axis=1)
    xs = p + (prev - p) * P['rwkv_mu'][l]
    new_shift = p[:, -1]
    r, k, v, wd, ad, gd = jnp.split(
        xs, [B_WIDTH, 2 * B_WIDTH, 3 * B_WIDTH, 3 * B_WIDTH + W_LORA, 3 * B_WIDTH + W_LORA + A_LORA], axis=-1)
    w_log = -jax.nn.softplus(-(P['rwkv_w0'][l] + jnp.einsum('btr,rc->btc', jnp.tanh(wd), P['rwkv_w2'][l]))) - 0.5
    decay = jnp.exp(-jnp.exp(w_log))
    a = jax.nn.sigmoid(P['rwkv_a0'][l] + jnp.einsum('btr,rc->btc', ad, P['rwkv_a2'][l]))
    g = jnp.einsum('btr,rc->btc', jax.nn.sigmoid(gd), P['rwkv_g2'][l])
    if l == 0:
        v_first = v
    else:
        mixv = jax.nn.sigmoid(P['rwkv_v0'][l - 1] + jnp.einsum(
            'btr,rc->btc', jnp.einsum('btc,cr->btr', v, P['rwkv_v1'][l - 1]), P['rwkv_v2'][l - 1]))
        v = v + (v_first - v) * mixv
    kk = l2norm(split_heads(k * P['rwkv_k_k'][l], H_B))
    k = k * (1.0 + (a - 1.0) * P['rwkv_k_a'][l])
    rh, kh, vh, ah, dh = (split_heads(t_, H_B) for t_ in (r, k, v, a, decay))
    y, s = rwkv7_scan(rh, dh, kh, vh, kk, ah, s0.astype(jnp.float32))
    mu = jnp.mean(y, -1, keepdims=True)
    yc = y - mu
    y = yc * lax.rsqrt(jnp.mean(yc * yc, -1, keepdims=True) + RWKV_LN_EPS)
    y = merge_heads(y) * P['rwkv_ln_w'][l] + P['rwkv_ln_b'][l]
    bonus = jnp.sum(rh * kh * P['rwkv_r_k'][l], -1, keepdims=True) * vh
    y = (y + merge_heads(bonus)) * g
    return y, s, new_shift, v_first


def hgrn_mixer(p, s0, lb, P, l):
    q, f, i, g = jnp.split(p, 4, axis=-1)
    q = jax.nn.silu(q) * HEAD_DIM ** -0.5
    k = (1.0 - lb) * jax.nn.sigmoid(-f)
    log_f = jnp.log1p(-k)
    o, s = hgrn_chunked(split_heads(q, H_C), split_heads(k, H_C), split_heads(i, H_C),
                        split_heads(log_f, H_C), s0.astype(jnp.float32))
    o = gated_rmsnorm(o, P['hgrn_norm_w'][l], split_heads(g, H_C))
    return merge_heads(o), s


def trunk(x, states, P, lb):
    st_gdn, st_conv, st_rwkv, st_shift, st_hgrn = states
    new = ([], [], [], [], [])
    v_first = None
    for l in range(DEPTH):
        h = rmsnorm(x, P['norm_mix'][l])
        p = jnp.einsum('btd,de->bte', h, P['w_in'][l]).astype(jnp.float32)
        p_a = p[..., :A_COLS]
        p_b = p[..., A_COLS:A_COLS + B_COLS]
        p_c = p[..., A_COLS + B_COLS:]
        y_a, s_a, c_a = gdn_mixer(p_a, st_gdn[l], st_conv[l], P, l)
        y_b, s_b, sh_b, v_first = rwkv_mixer(p_b, st_rwkv[l], st_shift[l], v_first, P, l)
        y_c, s_c = hgrn_mixer(p_c, st_hgrn[l], lb[l], P, l)
        mix = jnp.concatenate([y_a, y_b, y_c], axis=-1).astype(x.dtype)
        x = x + jnp.einsum('bte,ed->btd', mix, P['w_out'][l])
        h = rmsnorm(x, P['norm_ffn'][l])
        u = jax.nn.silu(jnp.einsum('btd,df->btf', h, P['w_gate'][l])) * jnp.einsum('btd,df->btf', h, P['w_up'][l])
        x = x + jnp.einsum('btf,fd->btd', u, P['w_down'][l])
        for lst, s in zip(new, (s_a, c_a, s_b, sh_b, s_c)):
            lst.append(s)
    y = rmsnorm(x, P['norm_final'])
    return y, tuple(jnp.stack(lst).astype(x.dtype) for lst in new)


def setup_inputs(seed: int = 0) -> dict:
    key = jax.random.key(seed)
    ks = jax.random.split(key, 40)
    cnt = iter(range(40))

    def nrm(shape, s):
        return jax.random.normal(ks[next(cnt)], shape, jnp.float32) * s

    def uni(shape, lo, hi):
        return jax.random.uniform(ks[next(cnt)], shape, jnp.float32, lo, hi)

    L = DEPTH
    dt = jnp.exp(uni((L, H_A), math.log(1e-3), math.log(1e-1)))
    return {
        'x_prompt': nrm((BATCH, SEQ, D_MODEL), 1.0),
        'x_sample': nrm((DEC_BATCH, DEC_SEQ, D_MODEL), 1.0),
        'state_gdn': nrm((L, DEC_BATCH, H_A, HEAD_DIM, HEAD_DIM), 0.3),
        'state_gdn_conv': nrm((L, DEC_BATCH, CONV_W - 1, 3 * A_WIDTH), 1.0),
        'state_rwkv': nrm((L, DEC_BATCH, H_B, HEAD_DIM, HEAD_DIM), 0.3),
        'state_rwkv_shift': nrm((L, DEC_BATCH, B_COLS), 1.0),
        'state_hgrn': nrm((L, DEC_BATCH, H_C, HEAD_DIM, HEAD_DIM), 0.3),
        'norm_mix': 1.0 + nrm((L, D_MODEL), 0.02),
        'w_in': nrm((L, D_MODEL, IN_COLS), D_MODEL ** -0.5),
        'w_out': nrm((L, D_MODEL, D_MODEL), 0.5 * D_MODEL ** -0.5),
        'gdn_conv_w': nrm((L, CONV_W, 3 * A_WIDTH), CONV_W ** -0.5),
        'gdn_a_log': jnp.log(uni((L, H_A), 1.0, 16.0)),
        'gdn_dt_bias': dt + jnp.log(-jnp.expm1(-dt)),
        'gdn_norm_w': 1.0 + nrm((L, HEAD_DIM), 0.02),
        'rwkv_mu': uni((L, B_COLS), 0.0, 1.0),
        'rwkv_w0': uni((L, B_WIDTH), -6.0, -1.0),
        'rwkv_w2': nrm((L, W_LORA, B_WIDTH), 0.1),
        'rwkv_a0': nrm((L, B_WIDTH), 0.1),
        'rwkv_a2': nrm((L, A_LORA, B_WIDTH), 0.1),
        'rwkv_g2': nrm((L, G_LORA, B_WIDTH), G_LORA ** -0.5),
        'rwkv_v0': 1.0 + nrm((L - 1, B_WIDTH), 0.1),
        'rwkv_v1': nrm((L - 1, B_WIDTH, V_LORA), B_WIDTH ** -0.5),
        'rwkv_v2': nrm((L - 1, V_LORA, B_WIDTH), 0.1),
        'rwkv_k_k': 0.85 + nrm((L, B_WIDTH), 0.02),
        'rwkv_k_a': 1.0 + nrm((L, B_WIDTH), 0.02),
        'rwkv_r_k': nrm((L, H_B, HEAD_DIM), 0.1),
        'rwkv_ln_w': 1.0 + nrm((L, B_WIDTH), 0.02),
        'rwkv_ln_b': nrm((L, B_WIDTH), 0.01),
        'hgrn_lower_bounds': nrm((L, C_WIDTH), 1.0),
        'hgrn_norm_w': 1.0 + nrm((L, HEAD_DIM), 0.02),
        'norm_ffn': 1.0 + nrm((L, D_MODEL), 0.02),
        'w_gate': nrm((L, D_MODEL, D_FF), D_MODEL ** -0.5),
        'w_up': nrm((L, D_MODEL, D_FF), D_MODEL ** -0.5),
        'w_down': nrm((L, D_FF, D_MODEL), 0.5 * D_FF ** -0.5),
        'norm_final': 1.0 + nrm((D_MODEL,), 0.02),
    }


def reference(x_prompt, x_sample, state_gdn, state_gdn_conv, state_rwkv, state_rwkv_shift, state_hgrn,
              norm_mix, w_in, w_out, gdn_conv_w, gdn_a_log, gdn_dt_bias, gdn_norm_w,
              rwkv_mu, rwkv_w0, rwkv_w2, rwkv_a0, rwkv_a2, rwkv_g2, rwkv_v0, rwkv_v1, rwkv_v2,
              rwkv_k_k, rwkv_k_a, rwkv_r_k, rwkv_ln_w, rwkv_ln_b,
              hgrn_lower_bounds, hgrn_norm_w, norm_ffn, w_gate, w_up, w_down, norm_final):
    P = {
        'norm_mix': norm_mix, 'w_in': w_in, 'w_out': w_out,
        'gdn_conv_w': gdn_conv_w, 'gdn_a_log': gdn_a_log, 'gdn_dt_bias': gdn_dt_bias, 'gdn_norm_w': gdn_norm_w,
        'rwkv_mu': rwkv_mu, 'rwkv_w0': rwkv_w0, 'rwkv_w2': rwkv_w2, 'rwkv_a0': rwkv_a0, 'rwkv_a2': rwkv_a2,
        'rwkv_g2': rwkv_g2, 'rwkv_v0': rwkv_v0, 'rwkv_v1': rwkv_v1, 'rwkv_v2': rwkv_v2,
        'rwkv_k_k': rwkv_k_k, 'rwkv_k_a': rwkv_k_a, 'rwkv_r_k': rwkv_r_k,
        'rwkv_ln_w': rwkv_ln_w, 'rwkv_ln_b': rwkv_ln_b,
        'hgrn_norm_w': hgrn_norm_w, 'norm_ffn': norm_ffn,
        'w_gate': w_gate, 'w_up': w_up, 'w_down': w_down, 'norm_final': norm_final,
    }
    sm = jax.nn.softmax(hgrn_lower_bounds.astype(jnp.float32), axis=0)
    lb = jnp.cumsum(sm, axis=0) - sm[0:1]

    b = x_prompt.shape[0]
    init_prompt = (
        jnp.zeros((DEPTH, b, H_A, HEAD_DIM, HEAD_DIM), jnp.float32),
        jnp.zeros((DEPTH, b, CONV_W - 1, 3 * A_WIDTH), jnp.float32),
        jnp.zeros((DEPTH, b, H_B, HEAD_DIM, HEAD_DIM), jnp.float32),
        jnp.zeros((DEPTH, b, B_COLS), jnp.float32),
        jnp.zeros((DEPTH, b, H_C, HEAD_DIM, HEAD_DIM), jnp.float32),
    )
    y_prompt, (gdn_p, conv_p, rwkv_p, shift_p, hgrn_p) = trunk(x_prompt, init_prompt, P, lb)
    y_sample, (gdn_s, conv_s, rwkv_s, shift_s, hgrn_s) = trunk(
        x_sample, (state_gdn, state_gdn_conv, state_rwkv, state_rwkv_shift, state_hgrn), P, lb)
    return (y_prompt, y_sample, gdn_p, gdn_s, conv_p, conv_s, rwkv_p, rwkv_s, shift_p, shift_s, hgrn_p, hgrn_s)
```

```python
import contextlib
import numpy as np
import concourse.bass as bass
import concourse.mybir as mybir

F32 = mybir.dt.float32
BF16 = mybir.dt.bfloat16
AF = mybir.ActivationFunctionType
ALU = mybir.AluOpType
AX = mybir.AxisListType

import os as _os
SAME_ENGINE_SYNC = _os.environ.get("SES", "1") == "1"


class Buf:
    __slots__ = ("t", "lastw", "readers", "name", "excl")

    def __init__(self, t, name=""):
        self.t = t
        self.excl = False
        self.lastw = None
        self.readers = []
        self.name = name

    def __getitem__(self, idx):
        return Ref(self, self.t[idx])

    @property
    def a(self):
        return Ref(self, self.t[:])


class MK:
    ENGS = ("pe", "act", "dve", "pool", "sp")

    def __init__(self, nc, n_dma_sems=6):
        self.nc = nc
        self.stack = contextlib.ExitStack()
        self.prog = {e: [] for e in self.ENGS}
        self.cnt = {e: 0 for e in self.ENGS}
        self.sems = {}
        for e in self.ENGS:
            self.sems[e] = self.stack.enter_context(nc.semaphore("s_" + e))
        self.dsem = {}
        self.dcnt = {}
        self.dnext = {}
        for q in ("sp", "pool", "act"):
            self.dsem[q] = [self.stack.enter_context(nc.semaphore(f"d_{q}{i}")) for i in range(n_dma_sems)]
            self.dcnt[q] = [0] * n_dma_sems
            self.dnext[q] = 0
        self.seen = {e: {} for e in self.ENGS}
        self.semobj = {}
        for e in self.ENGS:
            self.semobj[("e", e)] = self.sems[e]
        for q in self.dsem:
            for i, s in enumerate(self.dsem[q]):
                self.semobj[("d", q, i)] = s
        self.nuniq = 0

    def sb(self, shape, dtype=F32, name=None):
        self.nuniq += 1
        name = f"{name or 'sb'}_{self.nuniq}"
        t = self.stack.enter_context(self.nc.sbuf_tensor(name, list(shape), dtype))
        return Buf(t, name)

    def ps(self, shape, dtype=F32, name=None):
        self.nuniq += 1
        name = f"{name or 'ps'}_{self.nuniq}"
        t = self.stack.enter_context(self.nc.psum_tensor(name, list(shape), dtype))
        b = Buf(t, name)
        b.excl = True
        return b

    def view(self, buf_or_ap, name=""):
        return Buf(buf_or_ap, name)

    def _need(self, eng, reads, writes):
        need = {}

        def add(ev):
            if ev is None:
                return
            key, val, en = ev
            if en == eng and key[0] == "e" and (eng == "pe" or not SAME_ENGINE_SYNC):
                return
            if need.get(key, 0) < val:
                need[key] = val

        for b in reads:
            add(b.lastw)
        for b in writes:
            add(b.lastw)
            for r in b.readers:
                add(r)
        out = []
        seen = self.seen[eng]
        for key, val in need.items():
            if seen.get(key, 0) < val:
                seen[key] = val
                out.append((key, val))
        return out

    def _commit(self, ev, reads, writes):
        for b in reads:
            b.readers.append(ev)
        for b in writes:
            b.lastw = ev
            b.readers = []

    def op(self, eng, fn, reads=(), writes=()):
        rx = [b for b in reads if b.excl]
        if rx:
            writes = list(writes) + rx
            reads = [b for b in reads if not b.excl]
        waits = self._need(eng, reads, writes)
        self.cnt[eng] += 1
        n = self.cnt[eng]
        ev = (("e", eng), n, eng)
        self._commit(ev, reads, writes)
        self.prog[eng].append((waits, fn, ("e", eng), 1))
        return ev

    def dma(self, q, out_ap, in_ap, reads=(), writes=(), **kw):
        i = self.dnext[q]
        self.dnext[q] = (i + 1) % len(self.dsem[q])
        key = ("d", q, i)
        prev = self.dcnt[q][i]
        waits = self._need(q, reads, writes)
        if prev > 0 and self.seen[q].get(key, 0) < prev:
            self.seen[q][key] = prev
            waits.append((key, prev))
        self.dcnt[q][i] = prev + 16
        ev = (key, prev + 16, "dma_" + q)
        self._commit(ev, reads, writes)

        def fn(e, out_ap=out_ap, in_ap=in_ap, kw=kw):
            return e.dma_start(out=out_ap, in_=in_ap, **kw)
        self.prog[q].append((waits, fn, key, 16))
        return ev

    def barrier(self):
        tgt = {}
        for e in self.ENGS:
            if self.cnt[e] > 0:
                tgt[("e", e)] = self.cnt[e]
        for q in self.dsem:
            for i in range(len(self.dsem[q])):
                if self.dcnt[q][i] > 0:
                    tgt[("d", q, i)] = self.dcnt[q][i]
        for e in self.ENGS:
            waits = []
            for k, v in tgt.items():
                if self.seen[e].get(k, 0) < v:
                    self.seen[e][k] = v
                    waits.append((k, v))
            if waits:
                self.prog[e].append((waits, None, None, 0))

    def all_dma_events(self):
        return [(("d", q, i), self.dcnt[q][i], "") for q in self.dsem
                for i in range(len(self.dsem[q])) if self.dcnt[q][i] > 0]

    def emit(self, final_events):
        nc = self.nc
        prog = self.prog
        semobj = self.semobj
        fw = {}
        for key, val, _ in final_events:
            fw[key] = max(fw.get(key, 0), val)

        def run(engname, e):
            for waits, fn, key, inc in prog[engname]:
                for k, v in waits:
                    e.wait_ge(semobj[k], v)
                if fn is not None:
                    fn(e).then_inc(semobj[key], inc)

        with nc.Block() as block:
            @block.tensor
            def _(e):
                run("pe", e)

            @block.scalar
            def _(e):
                run("act", e)

            @block.vector
            def _(e):
                run("dve", e)

            @block.gpsimd
            def _(e):
                run("pool", e)

            @block.sync
            def _(e):
                run("sp", e)
                for k, v in fw.items():
                    e.wait_ge(semobj[k], v)
        self.stack.close()


class Ref:
    __slots__ = ("buf", "ap")

    def __init__(self, buf, ap):
        self.buf = buf
        self.ap = ap

    def __getitem__(self, idx):
        return Ref(self.buf, self.ap[idx])

    @property
    def a(self):
        return self

    def rr(self, s, **kw):
        return Ref(self.buf, self.ap.rearrange(s, **kw))

    def bc(self, shape):
        return Ref(self.buf, self.ap.to_broadcast(list(shape)))

    def un(self, axis):
        return Ref(self.buf, self.ap.unsqueeze(axis))

    def bitcast(self, dt):
        return Ref(self.buf, self.ap.bitcast(dt))


def R_(buf, idx=None):
    return Ref(buf, buf.t[:] if idx is None else buf.t[idx])


from concourse.bass_utils import run_bass_kernel_spmd

D = 1024; T = 2048; NS = 16; NT = T + NS; NCH = 16; DEPTH = 4
HD = 64; HA = 6; HB = 6; HC = 4
A_COLS = 1548; B_COLS = 1408; C_COLS = 1024; IN_COLS = 3980; DFF = 2816
B0 = A_COLS; C0 = A_COLS + B_COLS
HOFF = 3


class Arena:
    def __init__(self, mk, words):
        self.mk = mk
        self.t = mk.stack.enter_context(mk.nc.sbuf_tensor("arena", [128, words], F32))
        self.words = words
        self.off = 0

    def _take(self, n):
        n = (n + 7) // 8 * 8
        o = self.off
        assert o + n <= self.words, f"arena overflow {o}+{n}>{self.words}"
        self.off = o + n
        return o

    def f32(self, shape, name=""):
        p = shape[0]; n = int(np.prod(shape[1:]))
        o = self._take(n)
        ap = self.t[0:p, o:o + n]
        if len(shape) == 3:
            ap = ap.rearrange("p (a b) -> p a b", a=shape[1])
        return Buf(ap, name)

    def bf16(self, shape, name=""):
        p = shape[0]; n = int(np.prod(shape[1:]))
        w = (n + 1) // 2
        o = self._take(w)
        ap = self.t[0:p, o:o + w].bitcast(BF16)[:, 0:n]
        if len(shape) == 3:
            ap = ap.rearrange("p (a b) -> p a b", a=shape[1])
        return Buf(ap, name)


class Pool_:
    def __init__(self, items):
        self.items = items
        self.i = 0

    def get(self):
        b = self.items[self.i]
        self.i = (self.i + 1) % len(self.items)
        return b


class K:
    def __init__(self, mk):
        self.mk = mk

    def act(self, out, in_, func, bias=None, scale=None, accum=None):
        rd = [in_.buf]; wr = [out.buf]
        kw = {}
        if bias is not None:
            if isinstance(bias, Ref):
                rd.append(bias.buf); kw["bias"] = bias.ap
            else:
                kw["bias"] = float(bias)
        if scale is not None:
            if isinstance(scale, Ref):
                rd.append(scale.buf); kw["scale"] = scale.ap
            else:
                kw["scale"] = float(scale)
        if accum is not None:
            wr.append(accum.buf); kw["accum_out"] = accum.ap
        self.mk.op("act", lambda e: e.activation(out=out.ap, in_=in_.ap, func=func, **kw), reads=rd, writes=wr)

    def tt(self, eng, out, a, b, op):
        self.mk.op(eng, lambda e: e.tensor_tensor(out=out.ap, in0=a.ap, in1=b.ap, op=op), reads=[a.buf, b.buf], writes=[out.buf])

    def ts(self, eng, out, a, s1, op0, s2=None, op1=None):
        rd = [a.buf]
        v1 = s1
        if isinstance(s1, Ref):
            rd.append(s1.buf); v1 = s1.ap
        v2 = s2
        if isinstance(s2, Ref):
            rd.append(s2.buf); v2 = s2.ap
        if op1 is None:
            self.mk.op(eng, lambda e: e.tensor_scalar(out=out.ap, in0=a.ap, scalar1=v1, scalar2=None, op0=op0), reads=rd, writes=[out.buf])
        else:
            self.mk.op(eng, lambda e: e.tensor_scalar(out=out.ap, in0=a.ap, scalar1=v1, scalar2=v2, op0=op0, op1=op1), reads=rd, writes=[out.buf])

    def stt(self, out, a, scalar, b, op0, op1):
        rd = [a.buf, b.buf]
        sv = scalar
        if isinstance(scalar, Ref):
            rd.append(scalar.buf); sv = scalar.ap
        self.mk.op("dve", lambda e: e.scalar_tensor_tensor(out=out.ap, in0=a.ap, scalar=sv, in1=b.ap, op0=op0, op1=op1), reads=rd, writes=[out.buf])

    def red(self, out, in_, op=ALU.add):
        self.mk.op("dve", lambda e: e.tensor_reduce(out=out.ap, in_=in_.ap, axis=AX.X, op=op), reads=[in_.buf], writes=[out.buf])

    def cp(self, eng, out, in_):
        if eng == "act":
            self.mk.op("act", lambda e: e.copy(out=out.ap, in_=in_.ap), reads=[in_.buf], writes=[out.buf])
        else:
            self.mk.op(eng, lambda e: e.tensor_copy(out=out.ap, in_=in_.ap), reads=[in_.buf], writes=[out.buf])

    def memset(self, eng, out, val):
        self.mk.op(eng, lambda e: e.memset(out.ap, val), writes=[out.buf])

    def recip(self, out, in_):
        self.mk.op("dve", lambda e: e.reciprocal(out=out.ap, in_=in_.ap), reads=[in_.buf], writes=[out.buf])

    def mm(self, out, lhsT, rhs, start=True, stop=True):
        rd = [lhsT.buf, rhs.buf]
        wr = [out.buf]
        if not start:
            rd.append(out.buf)
        self.mk.op("pe", lambda e: e.matmul(out.ap, lhsT=lhsT.ap, rhs=rhs.ap, start=start, stop=stop), reads=rd, writes=wr)

    def tr(self, out, in_, ident):
        self.mk.op("pe", lambda e: e.transpose(out.ap, in_.ap, ident.ap), reads=[in_.buf, ident.buf], writes=[out.buf])

    def dma(self, q, out, in_, **kw):
        return self.mk.dma(q, out.ap, in_.ap, reads=[in_.buf], writes=[out.buf], **kw)

    def asel(self, out, in_, pattern, cmp, fill, base, cm):
        self.mk.op("pool", lambda e: e.affine_select(out=out.ap, in_=in_.ap, pattern=pattern, compare_op=cmp, fill=fill, base=base, channel_multiplier=cm), reads=[in_.buf], writes=[out.buf])

    def sigmoid(self, out, in_, tmp, sign=1.0):
        self.act(tmp, in_, AF.Exp, scale=-sign)
        self.act(tmp, tmp, AF.Ln, bias=1.0)
        self.act(out, tmp, AF.Exp, scale=-1.0)

    def rsqrt_ln(self, out, in_, scale, eps):
        self.act(out, in_, AF.Ln, bias=eps, scale=scale)
        self.act(out, out, AF.Exp, scale=-0.5)


def build(depth=DEPTH, phases=("gdn", "rwkv", "hgrn", "ffn"), dbg=False):
    nc = bass.Bass("TRN2", target_bir_lowering=False)
    mk = MK(nc)
    k = K(mk)

    def din(name, shape):
        return Buf(nc.dram_tensor(name, list(shape), F32, kind="ExternalInput").ap(), name)

    def dout(name, shape):
        return Buf(nc.dram_tensor(name, list(shape), F32, kind="ExternalOutput").ap(), name)

    def dscr(name, shape):
        return Buf(nc.dram_tensor(name, list(shape), F32, kind="Internal").ap(), name)

    L = DEPTH
    I = {}
    for name, shape in [
        ("xin", (NT, D)), ("st_gdn", (L, NS, HA, HD, HD)), ("st_conv", (L, NS, 3, 1152)), ("st_rwkv", (L, NS, HB, HD, HD)),
        ("st_shift", (L, NS, B_COLS)), ("st_hgrn", (L, NS, HC, HD, HD)),
        ("norm_mix", (L, D)), ("w_in", (L, D, IN_COLS)), ("w_out", (L, D, D)), ("gdn_conv_w", (L, 4, 1152)),
        ("gdn_a_log", (L, HA)), ("gdn_dt_bias", (L, HA)), ("gdn_norm_w", (L, HD)), ("rwkv_mu", (L, B_COLS)),
        ("rwkv_w0", (L, 384)), ("rwkv_w2", (L, 64, 384)), ("rwkv_a0", (L, 384)), ("rwkv_a2", (L, 64, 384)),
        ("rwkv_g2", (L, 128, 384)), ("rwkv_v0", (L - 1, 384)), ("rwkv_v1", (L - 1, 384, 32)), ("rwkv_v2", (L - 1, 32, 384)),
        ("rwkv_k_k", (L, 384)), ("rwkv_k_a", (L, 384)), ("rwkv_r_k", (L, 384)), ("rwkv_ln_w", (L, 384)), ("rwkv_ln_b", (L, 384)),
        ("hgrn_lower_bounds", (L, 256)), ("hgrn_norm_w", (L, HD)), ("norm_ffn", (L, D)),
        ("w_gate", (L, D, DFF)), ("w_up", (L, D, DFF)), ("w_down", (L, DFF, D)), ("norm_final", (1, D)),
    ]:
        I[name] = din(name, shape)
    O = {}
    for name, shape in [
        ("y", (NT, D)), ("gdn_p", (L, HA, HD, HD)), ("gdn_s", (L, NS, HA, HD, HD)), ("conv_p", (L, 3, 1152)), ("conv_s", (L, NS, 3, 1152)),
        ("rwkv_p", (L, HB, HD, HD)), ("rwkv_s", (L, NS, HB, HD, HD)), ("shift_p", (L, B_COLS)), ("shift_s", (L, NS, B_COLS)),
        ("hgrn_p", (L, HC, HD, HD)), ("hgrn_s", (L, NS, HC, HD, HD)),
    ]:
        O[name] = dout(name, shape)
    if dbg:
        O["dbg_ya"] = dout("dbg_ya", (NT, 384))
        O["dbg_yb"] = dout("dbg_yb", (NT, 384))
        O["dbg_yc"] = dout("dbg_yc", (NT, 256))
        O["dbg_x1"] = dout("dbg_x1", (NT, D))
    xres = nc.dram_tensor("xres", [NT, D], F32, kind="Internal").ap()
    tiles = [(i * 128, 128) for i in range(NCH)] + [(T, NS)]
    xb = [Buf(xres[t0:t0 + n, :], f"xres{i}") for i, (t0, n) in enumerate(tiles)]
    vfirst = nc.dram_tensor("vfirst", [NT, 384], F32, kind="Internal").ap()
    vfb = [Buf(vfirst[t0:t0 + n, :], f"vf{i}") for i, (t0, n) in enumerate(tiles)]
    bounce = dscr("bounce", (NS, 6 * 200))
    bounceB = dscr("bounceB", (NS, 6 * 392))
    bounceC = dscr("bounceC", (NS, 4 * 256))
    bounce2 = dscr("bounce2", (NS * 6, 64))
    bounce2C = dscr("bounce2C", (NS * 4, 64))

    arena = Arena(mk, 206 * 256)
    banks = [mk.ps([128, 512], F32, name=f"bank{i}") for i in range(8)]

    def pview(b, c0, c1, p=128, name=""):
        return Buf(banks[b].t[0:p, c0:c1], name)

    identF = arena.f32([128, 128]); identB = arena.bf16([128, 128])
    Utri = arena.f32([128, 128]); ones = arena.f32([128, 128])
    negmaskT = arena.f32([128, 128]); posmaskS = arena.f32([128, 128])
    smT = arena.f32([128, 128]); sm = arena.f32([128, 128]); cmT = arena.f32([128, 128])
    sel = arena.f32([6, 6, 128])
    k.memset("pool", ones.a, 1.0)
    k.asel(identF.a, ones.a, [[-1, 128]], ALU.is_equal, 0.0, 0, 1)
    k.cp("pool", identB.a, identF.a)
    k.asel(Utri.a, ones.a, [[1, 128]], ALU.is_ge, 0.0, 0, -1)
    k.memset("pool", negmaskT.a, 0.0)
    k.asel(negmaskT.a, negmaskT.a, [[1, 128]], ALU.is_ge, -30000.0, 0, -1)
    k.memset("pool", posmaskS.a, 0.0)
    k.asel(posmaskS.a, posmaskS.a, [[-1, 128]], ALU.is_gt, 30000.0, 0, 1)
    k.asel(smT.a, ones.a, [[1, 128]], ALU.is_gt, 0.0, 0, -1)
    k.asel(sm.a, ones.a, [[-1, 128]], ALU.is_gt, 0.0, 0, 1)
    k.asel(cmT.a, ones.a, [[1, 128]], ALU.is_ge, 0.0, 0, -1)
    k.memset("pool", sel.a, 1.0)
    k.asel(sel.a, sel.a, [[-1, 6], [0, 128]], ALU.is_equal, 0.0, 0, 1)

    hT = arena.bf16([128, 8, HOFF + NT], "hT")
    k.memset("pool", hT[:, :, 0:HOFF], 0.0)

    for i, (t0, n) in enumerate(tiles):
        mk.dma("sp", xres[t0:t0 + n, :], I["xin"].t[t0:t0 + n, :], reads=[I["xin"]], writes=[xb[i]])

    persist_mark = arena.off

    def norm_phase(wname, l):
        m0 = arena.off
        wbc = arena.f32([128, D])
        mk.dma("sp", wbc.t, I[wname].t[l:l + 1, :].to_broadcast([128, D]), reads=[I[wname]], writes=[wbc])
        ss = arena.f32([128, 32])
        k.memset("pool", ss.a, 0.0)
        xts = [arena.f32([128, D]) for _ in range(len(tiles))]
        junk = arena.bf16([128, D])
        hb = [arena.bf16([128, D]) for _ in range(2)]
        trp = [Ref(banks[j], banks[j].t[:, :].bitcast(BF16)) for j in range(8)]
        for i, (t0, n) in enumerate(tiles):
            mk.dma("sp" if i % 2 == 0 else "act", xts[i].t[0:n, :], xres[t0:t0 + n, :], reads=[xb[i]], writes=[xts[i]])
        for i, (t0, n) in enumerate(tiles):
            k.act(junk[0:n, :], xts[i][0:n, :], AF.Square, accum=ss[0:n, i:i + 1])
        k.rsqrt_ln(ss.a, ss.a, 1.0 / D, 1e-6)
        for i, (t0, n) in enumerate(tiles):
            xt = xts[i]
            h = hb[i % 2]
            k.stt(h[0:n, :], xt[0:n, :], ss[0:n, i:i + 1], wbc[0:n, :], ALU.mult, ALU.mult)
            tp = trp[i % 8]
            for c in range(8):
                k.tr(tp[:, c * 128:c * 128 + n], h[0:n, c * 128:(c + 1) * 128], identB[0:n, 0:n])
            src = tp.a.rr("p (c t) -> p c t", c=8)[:, :, 0:n]
            eng = "act" if i % 2 == 0 else "dve"
            k.cp(eng, hT[:, :, HOFF + t0:HOFF + t0 + n], src)
        arena.off = m0
        mk.barrier()

    S = dict(arena=arena, banks=banks, pview=pview, I=I, O=O, xres=xres, xb=xb, tiles=tiles, hT=hT, vfirst=vfirst, vfb=vfb,
             identF=identF, identB=identB, Utri=Utri, ones=ones, negmaskT=negmaskT, posmaskS=posmaskS, smT=smT, sm=sm, cmT=cmT,
             sel=sel, bounce=bounce, bounce2=bounce2, bounceB=bounceB, bounceC=bounceC, bounce2C=bounce2C, dbg=dbg, mk=mk, k=k)

    for l in range(depth):
        norm_phase("norm_mix", l)
        if "gdn" in phases:
            gdn_phase(S, l)
        if "rwkv" in phases:
            rwkv_phase(S, l)
        if "hgrn" in phases:
            hgrn_phase(S, l)
        if dbg and l == 0:
            for i, (t0, n) in enumerate(tiles):
                mk.dma("sp", O["dbg_x1"].t[t0:t0 + n, :], xres[t0:t0 + n, :], reads=[xb[i]], writes=[O["dbg_x1"]])
        if "ffn" in phases:
            ffn_phase(S, l, norm_phase)
    final_phase(S)
    mk.emit(mk.all_dma_events())
    return nc


def load_w(S, dst, src_ap, src_buf):
    S["mk"].dma("pool", dst.t, src_ap.rearrange("(c p) n -> p c n", p=128), reads=[src_buf], writes=[dst])


def tri_inverse(S, X, XT, sbp, psp, dest):
    k = S["k"]
    TT = sbp.get()
    k.tt("pool", TT.a, XT.a, S["identF"].a, ALU.add)
    P, PT = X, XT
    for st in range(1, 7):
        psP = psp.get()
        k.mm(psP.a, PT.a, P.a)
        if st < 6:
            psPT = psp.get()
            k.mm(psPT.a, P.a, PT.a)
        Pn = sbp.get()
        k.cp("act", Pn.a, psP.a)
        if st < 6:
            PTn = sbp.get()
            k.cp("dve", PTn.a, psPT.a)
        psT = psp.get()
        k.mm(psT.a, Pn.a, TT.a)
        TTn = sbp.get() if st < 6 else dest
        k.tt("dve", TTn.a, TT.a, psT.a, ALU.add)
        TT = TTn
        P = Pn
        if st < 6:
            PT = PTn
    return TT


def out_proj(S, yb, n, Wo, ne, ti, dbgname=None, dbgw=0):
    k = S["k"]; mk = S["mk"]; pview = S["pview"]
    t0, _ = S["tiles"][ti]
    tp = Ref(S["banks"][6], S["banks"][6].t[:, :].bitcast(BF16))
    for e in range(ne):
        k.tr(tp[:, e * 128:e * 128 + n], yb[0:n, e * 128:(e + 1) * 128], S["identB"][0:n, 0:n])
    yT = S["yT"]
    k.cp("act", yT[:, 0:ne, 0:n], tp.a.rr("p (c t) -> p c t", c=8)[:, 0:ne, 0:n])
    xt = S["xt_pool"].get()
    mk.dma("sp", xt.t[0:n, :], S["xres"][t0:t0 + n, :], reads=[S["xb"][ti]], writes=[xt])
    for dh in range(2):
        ps = S["banks"][1 + dh].a
        for e in range(ne):
            k.mm(ps[0:n, :], yT[:, e, 0:n], Wo[:, e, dh * 512:(dh + 1) * 512], start=(e == 0), stop=(e == ne - 1))
        k.tt("dve", xt[0:n, dh * 512:(dh + 1) * 512], xt[0:n, dh * 512:(dh + 1) * 512], ps[0:n, :], ALU.add)
    mk.dma("sp", S["xres"][t0:t0 + n, :], xt.t[0:n, :], reads=[xt], writes=[S["xb"][ti]])


def interleave(gB, gA):
    while gB is not None or gA is not None:
        if gB is not None:
            try:
                next(gB)
            except StopIteration:
                gB = None
        if gA is not None:
            try:
                next(gA)
            except StopIteration:
                gA = None


def hv(T6, h):
    return T6[h // 3][:, h % 3, :]


def grp_mm(S, psp, pairs_fn, evac_fn, w=128, p=128, ngrp=2, per=3):
    k = S["k"]
    for g in range(ngrp):
        pb = psp.get(per * w, p)
        for j in range(per):
            h = g * per + j
            prs = pairs_fn(h)
            for i, (a, b) in enumerate(prs):
                k.mm(pb[:, j * w:(j + 1) * w], a, b, start=(i == 0), stop=(i == len(prs) - 1))
        evac_fn(g, pb.rr("p (h t) -> p h t", t=w))


def tri_inverse6(S, X6, XT6, pools, psp, dest6):
    k = S["k"]
    P6p, PT6p, TFp, TBp = pools
    identb = S["identF"].a.un(1).bc([128, 3, 128])
    TF = TFp.get(); TB = TBp.get()
    for g in range(2):
        k.tt("dve", TB[g].a, XT6[g].a, identb, ALU.add)
    P6, PT6 = X6, XT6
    Pn6 = P6p.get()
    grp_mm(S, psp, lambda h: [(hv(PT6, h), hv(P6, h))], lambda g, ps: k.cp("act", Pn6[g].a, ps))
    yield
    PTn6 = PT6p.get()
    grp_mm(S, psp, lambda h: [(hv(P6, h), hv(PT6, h))], lambda g, ps: k.cp("act", PTn6[g].a, ps))
    yield
    P6, PT6 = Pn6, PTn6
    for st in range(2, 7):
        TBn = TBp.get()

        def ev(g, ps, TB=TB, TBn=TBn):
            k.tt("dve", TBn[g].a, TB[g].a, ps, ALU.add)
        grp_mm(S, psp, lambda h: [(hv(P6, h), hv(TB, h))], ev)
        yield
        Pn6 = P6p.get()
        grp_mm(S, psp, lambda h: [(hv(PT6, h), hv(P6, h))], lambda g, ps: k.cp("act", Pn6[g].a, ps))
        yield
        if st < 6:
            PTn6 = PT6p.get()
            grp_mm(S, psp, lambda h: [(hv(P6, h), hv(PT6, h))], lambda g, ps: k.cp("act", PTn6[g].a, ps))
            yield
            PT6 = PTn6
        TB = TBn
        P6 = Pn6
    TBd = TBp.get()
    grp_mm(S, psp, lambda h: [(hv(P6, h), hv(TB, h))], lambda g, ps: k.tt("dve", TBd[g].a, TB[g].a, ps, ALU.add))
    yield
    Tn6 = PT6p.get(); Rt6 = P6p.get()
    for g in range(2):
        pb_ = psp.get(512)
        ptr = Ref(pb_.buf, pb_.buf.t[:, :].bitcast(BF16))
        for j in range(3):
            k.tr(ptr[:, j * 128:(j + 1) * 128], TBd[g][:, j, :], S["identB"].a)
        k.cp("act", Tn6[g].a, ptr[:, 0:384].rr("p (h t) -> p h t", t=128))

    def evr(g, ps):
        k.tt("dve", TF[g].a, ps, TBd[g].a, ALU.subtract)
        k.tt("dve", Rt6[g].a, TF[g].a, identb, ALU.add)
    grp_mm(S, psp, lambda h: [(hv(X6, h), hv(TBd, h))], evr)
    yield
    grp_mm(S, psp, lambda h: [(hv(Tn6, h), hv(Rt6, h))], lambda g, ps: k.tt("dve", dest6[g].a, TBd[g].a, ps, ALU.add))
    yield
    return


def gdn_phase(S, l):
    k = S["k"]; mk = S["mk"]; arena = S["arena"]; I = S["I"]; O = S["O"]; banks = S["banks"]; hT = S["hT"]
    m0 = arena.off
    identF = S["identF"]
    Wqkv = arena.bf16([128, 8, 1152]); Wz = arena.bf16([128, 8, 384]); Wba = arena.bf16([128, 8, 12]); Wo = arena.bf16([128, 3, 1024])
    load_w(S, Wqkv, I["w_in"].t[l, :, 0:1152], I["w_in"])
    load_w(S, Wz, I["w_in"].t[l, :, 1152:1536], I["w_in"])
    load_w(S, Wba, I["w_in"].t[l, :, 1536:1548], I["w_in"])
    load_w(S, Wo, I["w_out"].t[l, 0:384, :], I["w_out"])
    wc = arena.f32([128, 4, 1152])
    mk.dma("sp", wc.t, I["gdn_conv_w"].t[l:l + 1, :, :].to_broadcast([128, 4, 1152]), reads=[I["gdn_conv_w"]], writes=[wc])
    negA = arena.f32([128, 6]); dtb = arena.f32([128, 6]); gnw = arena.f32([128, 64])
    mk.dma("sp", negA.t, I["gdn_a_log"].t[l:l + 1, :].to_broadcast([128, 6]), reads=[I["gdn_a_log"]], writes=[negA])
    mk.dma("sp", dtb.t, I["gdn_dt_bias"].t[l:l + 1, :].to_broadcast([128, 6]), reads=[I["gdn_dt_bias"]], writes=[dtb])
    mk.dma("sp", gnw.t, I["gdn_norm_w"].t[l:l + 1, :].to_broadcast([128, 64]), reads=[I["gdn_norm_w"]], writes=[gnw])
    k.act(negA.a, negA.a, AF.Exp)
    k.ts("pool", negA.a, negA.a, -1.0, ALU.mult)
    St = arena.f32([64, 6, 64], "gdnS")
    k.memset("pool", St.a, 0.0)
    Sth = [Buf(St.t[:, h, :], f"gdnS{h}") for h in range(6)]
    S["yT"] = arena.bf16([128, 8, 128])
    S["xt_pool"] = Pool_([arena.f32([128, D]) for _ in range(2)])
    acc = arena.f32([128, 384]); tmp = arena.f32([128, 384])
    qkv = arena.f32([128, 1152]); zsA = [arena.f32([128, 384]) for _ in range(2)]; zt = arena.f32([128, 384])
    zs = zsA[0]
    sqB = arena.f32([128, 384]); rsB = arena.f32([128, 6])
    sm12 = arena.f32([128, 12]); beta = arena.f32([128, 6]); g = arena.f32([128, 6]); t6 = arena.f32([128, 6])
    gc = arena.f32([128, 6]); ngc = arena.f32([128, 6]); eg = arena.f32([128, 6]); egl = arena.f32([128, 6]); ekl = arena.f32([128, 6])
    gT = arena.f32([6, 128])
    sq = arena.f32([128, 768]); rs = arena.f32([128, 12])
    qkn = arena.f32([128, 768])
    o = arena.f32([128, 384]); yb = arena.bf16([128, 384]); y32 = arena.f32([128, 384])
    convo = arena.f32([128, 1152])
    mchunk = arena.off
    bk = arena.bf16([128, 384]); qd = arena.bf16([128, 384])
    bvA = [arena.bf16([128, 384]) for _ in range(2)]; bkeA = [arena.bf16([128, 384]) for _ in range(2)]; kdA = [arena.bf16([128, 384]) for _ in range(2)]
    knT = arena.bf16([64, 6, 128]); bkT = arena.bf16([64, 6, 128]); qnT = arena.bf16([64, 6, 128])
    qdTA = [arena.bf16([64, 6, 128]) for _ in range(2)]
    eglA = [arena.f32([128, 6]) for _ in range(2)]
    def mk6(dt=F32):
        return [(arena.f32 if dt == F32 else arena.bf16)([128, 3, 128]) for _ in range(2)]
    DTc6 = mk6(); Ds6 = mk6(); DTs6 = mk6(); AT6A = [mk6(BF16) for _ in range(2)]; TTd6 = mk6(BF16)
    XT6A = [mk6(BF16) for _ in range(2)]
    X6p = Pool_([mk6(BF16) for _ in range(2)]); XT6p = Pool_([mk6(BF16) for _ in range(3)])
    TFp = Pool_([mk6() for _ in range(3)]); TBp = Pool_([mk6(BF16) for _ in range(3)])
    X60A = [mk6(BF16) for _ in range(2)]
    nwT6 = arena.bf16([64, 6, 128]); vn6 = arena.bf16([128, 384])
    Stb = arena.bf16([64, 6, 64]); k.memset("pool", Stb.a, 0.0)
    qknb = arena.bf16([128, 768])
    SA = [arena.bf16([128, 128]) for _ in range(3)]; SB = [arena.bf16([128, 128]) for _ in range(3)]
    onesb = arena.bf16([128, 128]); k.cp("pool", onesb.a, S["ones"].a)
    for s_ in (1, 2, 3):
        k.asel(SA[s_ - 1].a, onesb.a, [[1, 128]], ALU.is_equal, 0.0, -s_, -1)
        k.asel(SB[s_ - 1].a, onesb.a, [[1, 128]], ALU.is_equal, 0.0, 128 - s_, -1)
    pbA = [arena.bf16([128, 1152]) for _ in range(2)]
    k.memset("pool", pbA[1].a, 0.0)
    pb_par = [0]
    negmaskTb = arena.bf16([128, 128]); posmaskSb = arena.bf16([128, 128])
    k.cp("pool", negmaskTb.a, S["negmaskT"].a); k.cp("pool", posmaskSb.a, S["posmaskS"].a)
    gcp = arena.f32([128, 38]); G38 = arena.f32([38, 128]); Rm = arena.f32([38, 128]); Lm = arena.f32([38, 6, 128]); selhi = arena.f32([38, 6, 128])
    k.memset("pool", gcp.a, 0.0); k.memset("pool", Rm.a, 0.0); k.memset("pool", Lm.a, 0.0)
    k.memset("pool", Rm[32:38, :], 1.0)
    mk.dma("sp", Lm.t[0:6, :, :], S["sel"].t, reads=[S["sel"]], writes=[Lm])
    mk.dma("sp", selhi.t[32:38, :, :], S["sel"].t, reads=[S["sel"]], writes=[selhi])
    pconv = [Ref(banks[j], banks[j].t[:, 0:384]) for j in range(4)]
    bankpool = Pool_([banks[i] for i in (4, 5, 6, 7, 0, 1, 2, 3)])

    class PSP:
        def get(self, w=128, p=128):
            b = bankpool.get()
            return Ref(b, b.t[0:p, 0:w])
    psp = PSP()

    def prologue(n, col0, is_sample, zs=None):
        zs = zs if zs is not None else zsA[0]
        import os
        PSTOP = int(os.environ.get("GDN_PSTOP", "9")) if is_sample else 9
        if not is_sample:
            pcur = pbA[pb_par[0] % 2]; pprev = pbA[(pb_par[0] + 1) % 2]
            pb_par[0] += 1
            for cg in range(3):
                cs = slice(cg * 384, (cg + 1) * 384)
                bb = banks[cg]
                for c in range(8):
                    k.mm(bb[0:n, 0:384], hT[:, c, col0:col0 + n], Wqkv[:, c, cs], start=(c == 0), stop=(c == 7))
                k.cp("act", pcur[:, cs], bb[:, 0:384])
                if col0 == HOFF + T - 128:
                    k.cp("act", convo[96:128, cs], bb[96:128, 0:384])
            for cg in range(3):
                cs = slice(cg * 384, (cg + 1) * 384)
                bb = banks[cg]
                for j in range(3):
                    sh_ = 3 - j
                    k.mm(banks[3 + j][:, 0:384], SA[sh_ - 1].a, pcur[:, cs], start=True, stop=False)
                    k.mm(banks[3 + j][:, 0:384], SB[sh_ - 1].a, pprev[:, cs], start=False, stop=True)
                k.tt("dve", acc[0:n, :], bb[0:n, 0:384], wc[0:n, 3, cs], ALU.mult)
                for j in range(3):
                    k.tt("dve", tmp[0:n, :], banks[3 + j][0:n, 0:384], wc[0:n, j, cs], ALU.mult)
                    k.tt("pool", acc[0:n, :], acc[0:n, :], tmp[0:n, :], ALU.add)
                k.sigmoid(tmp[0:n, :], acc[0:n, :], tmp[0:n, :])
                k.tt("pool", qkv[0:n, cs], acc[0:n, :], tmp[0:n, :], ALU.mult)
            yield
        for cg in range(3):
            if PSTOP <= 0 or not is_sample:
                break
            cs = slice(cg * 384, (cg + 1) * 384)
            for c in range(8):
                k.mm(pconv[3][0:n, :], hT[:, c, col0:col0 + n], Wqkv[:, c, cs], start=(c == 0), stop=(c == 7))
            k.cp("act", convo[0:n, cs], pconv[3][0:n, :])
            k.tt("dve", acc[0:n, :], pconv[3][0:n, :], wc[0:n, 3, cs], ALU.mult)
            for j in range(3):
                k.tt("pool", tmp[0:n, :], S["cbuf"][0:n, j, cs], wc[0:n, j, cs], ALU.mult)
                k.tt("pool", acc[0:n, :], acc[0:n, :], tmp[0:n, :], ALU.add)
            k.sigmoid(tmp[0:n, :], acc[0:n, :], tmp[0:n, :])
            k.tt("pool", qkv[0:n, cs], acc[0:n, :], tmp[0:n, :], ALU.mult)
            yield
        if PSTOP <= 1:
            return
        for c in range(8):
            k.mm(pconv[0][0:n, :], hT[:, c, col0:col0 + n], Wz[:, c, :], start=(c == 0), stop=(c == 7))
        k.cp("act", zt[0:n, :], pconv[0][0:n, :])
        k.sigmoid(zs[0:n, :], zt[0:n, :], zs[0:n, :])
        k.tt("pool", zs[0:n, :], zs[0:n, :], zt[0:n, :], ALU.mult)
        yield
        if PSTOP <= 2:
            return
        pba = psp.get(12)
        for c in range(8):
            k.mm(pba[0:n, :], hT[:, c, col0:col0 + n], Wba[:, c, :], start=(c == 0), stop=(c == 7))
        k.cp("dve", sm12[0:n, :], pba[0:n, :])
        k.sigmoid(beta[0:n, :], sm12[0:n, 0:6], t6[0:n, :])
        k.tt("dve", t6[0:n, :], sm12[0:n, 6:12], dtb[0:n, :], ALU.add)
        k.act(t6[0:n, :], t6[0:n, :], AF.Exp)
        k.act(t6[0:n, :], t6[0:n, :], AF.Ln, bias=1.0)
        k.tt("dve", g[0:n, :], t6[0:n, :], negA[0:n, :], ALU.mult)
        yield
        if PSTOP <= 3:
            return
        k.tt("pool", sq[0:n, :], qkv[0:n, 0:768], qkv[0:n, 0:768], ALU.mult)
        k.red(rs[0:n, :], sq[0:n, :].rr("p (h d) -> p h d", d=64))
        k.rsqrt_ln(rs[0:n, :], rs[0:n, :], 1.0, 1e-6)
        k.ts("dve", rs[0:n, 0:6], rs[0:n, 0:6], 0.125, ALU.mult)
        k.tt("dve", qkn[0:n, :].rr("p (h d) -> p h d", d=64), qkv[0:n, 0:768].rr("p (h d) -> p h d", d=64),
             rs[0:n, :].un(2).bc([n, 12, 64]), ALU.mult)
        yield

    def h3(ref, n):
        return ref[0:n, :].rr("p (h d) -> p h d", d=64)

    def b3(ref, n):
        return ref[0:n, :].un(2).bc([n, 6, 64])

    def epilogue(n, ti, row0, zs=None):
        zs = zs if zs is not None else zsA[0]
        k.tt("pool", sqB[0:n, 0:384], o[0:n, :], o[0:n, :], ALU.mult)
        k.red(rsB[0:n, 0:6], sqB[0:n, 0:384].rr("p (h d) -> p h d", d=64))
        k.rsqrt_ln(rsB[0:n, 0:6], rsB[0:n, 0:6], 1.0 / 64, 1e-6)
        k.tt("dve", h3(y32, n), h3(o, n), b3(rsB[:, 0:6], n), ALU.mult)
        k.tt("pool", h3(y32, n), h3(y32, n), gnw[0:n, :].un(1).bc([n, 6, 64]), ALU.mult)
        k.tt("dve", yb[0:n, :], y32[0:n, :], zs[0:n, :], ALU.mult)
        if S["dbg"] and l == 0:
            k.tt("pool", y32[0:n, :], y32[0:n, :], zs[0:n, :], ALU.mult)
            mk.dma("sp", O["dbg_ya"].t[row0:row0 + n, :], y32.t[0:n, :], reads=[y32], writes=[O["dbg_ya"]])
        out_proj(S, yb, n, Wo, 3, ti)

    def chunkA(ci):
        n = 128
        par = ci % 2
        zs_ = zsA[par]; bv = bvA[par]; bke = bkeA[par]; kd = kdA[par]; qdT = qdTA[par]; egl = eglA[par]
        X6 = X60A[par]; XT6 = XT6A[par]; AT6 = AT6A[par]
        yield from prologue(n, HOFF + ci * 128, False, zs_)
        pg = psp.get(12)
        k.mm(pg[:, 0:6], S["Utri"].a, g.a)
        k.mm(pg[:, 6:12], S["ones"].a, g.a)
        k.cp("dve", gc.a, pg[:, 0:6])
        k.ts("dve", ngc.a, gc.a, -1.0, ALU.mult)
        k.act(eg.a, pg[:, 0:6], AF.Exp)
        k.act(egl.a, pg[:, 6:12], AF.Exp)
        k.tt("dve", ekl.a, pg[:, 6:12], gc.a, ALU.subtract)
        k.act(ekl.a, ekl.a, AF.Exp)
        k.cp("act", gcp[:, 0:6], gc.a)
        k.cp("act", gcp[:, 32:38], ngc.a)
        pgT = psp.get(128, 38)
        k.tr(pgT.a, gcp.a, identF.a)
        k.cp("act", G38.a, pgT.a)
        k.cp("act", Rm[0:6, :], pgT[0:6, :])
        yield
        kn = qkn[:, 384:768]; qn = qkn[:, 0:384]; v = qkv[:, 768:1152]
        k.cp("act", qknb.a, qkn.a)
        k.tt("dve", h3(bk, n), kn.rr("p (h d) -> p h d", d=64), b3(beta, n), ALU.mult)
        k.tt("dve", h3(qd, n), qn.rr("p (h d) -> p h d", d=64), b3(eg, n), ALU.mult)
        k.tt("pool", h3(bv, n), v.rr("p (h d) -> p h d", d=64), b3(beta, n), ALU.mult)
        k.tt("dve", h3(bke, n), h3(bk, n), b3(eg, n), ALU.mult)
        k.tt("dve", h3(kd, n), kn.rr("p (h d) -> p h d", d=64), b3(ekl, n), ALU.mult)
        yield
        for (src, dst, eng) in ((qknb[:, 384:768], knT, "act"), (bk.a, bkT, "dve"), (qknb[:, 0:384], qnT, "act"), (qd.a, qdT, "dve")):
            pb_ = bankpool.get()
            ptr = Ref(pb_, pb_.t[:, :].bitcast(BF16))
            for h in range(6):
                k.tr(ptr[0:64, h * 128:(h + 1) * 128], src[:, h * 64:(h + 1) * 64], S["identB"].a)
            k.cp(eng, dst.a, ptr[0:64, 0:768].rr("p (h t) -> p h t", t=128))
            yield
        k.tt("pool", Lm[32:38, :, :], G38[32:38, :].un(1).bc([6, 6, 128]), selhi[32:38, :, :], ALU.mult)
        grp_mm(S, psp, lambda h: [(Lm[:, h, :], Rm.a), (S["identB"].a, negmaskTb.a)],
               lambda g, ps: k.act(DTc6[g].a, ps, AF.Exp))
        yield
        grp_mm(S, psp, lambda h: [(Lm[:, h, :], Rm.a), (S["identB"].a, posmaskSb.a)],
               lambda g, ps: k.act(Ds6[g].a, ps, AF.Exp, scale=-1.0))
        for gi_ in range(2):
            k.tt("dve", DTs6[gi_].a, DTc6[gi_].a, S["smT"].a.un(1).bc([128, 3, 128]), ALU.mult)
        yield
        grp_mm(S, psp, lambda h: [(bkT[:, h, :], knT[:, h, :])],
               lambda g, ps: k.stt(X6[g].a, ps, -1.0, Ds6[g].a, ALU.mult, ALU.mult))
        yield
        grp_mm(S, psp, lambda h: [(knT[:, h, :], bkT[:, h, :])],
               lambda g, ps: k.stt(XT6[g].a, ps, -1.0, DTs6[g].a, ALU.mult, ALU.mult))
        yield
        grp_mm(S, psp, lambda h: [(knT[:, h, :], qnT[:, h, :])],
               lambda g, ps: k.tt("dve", AT6[g].a, ps, DTc6[g].a, ALU.mult))
        yield

    def chunkB(ci):
        n = 128
        par = ci % 2
        zs_ = zsA[par]; bv = bvA[par]; bke = bkeA[par]; kd = kdA[par]; qdT = qdTA[par]; egl = eglA[par]
        X6 = X60A[par]; XT6 = XT6A[par]; AT6 = AT6A[par]
        yield from tri_inverse6(S, X6, XT6, (X6p, XT6p, TFp, TBp), psp, TTd6)
        TT6 = TTd6
        grp_mm(S, psp, lambda h: [(bke[:, h * 64:(h + 1) * 64], hv(TT6, h))],
               lambda g, ps: k.act(nwT6[:, g * 3:(g + 1) * 3, :], ps, AF.Copy, scale=-1.0), p=64)
        yield
        grp_mm(S, psp, lambda h: [(nwT6[:, h, :], Stb[:, h, :]), (hv(TT6, h), bv[:, h * 64:(h + 1) * 64])],
               lambda g, ps: k.cp("act", vn6.a.rr("p (h d) -> p h d", d=64), ps), w=64, ngrp=1, per=6)
        yield
        grp_mm(S, psp, lambda h: [(qdT[:, h, :], Stb[:, h, :]), (hv(AT6, h), vn6[:, h * 64:(h + 1) * 64])],
               lambda g, ps: k.cp("act", o.a.rr("p (h d) -> p h d", d=64), ps), w=64, ngrp=1, per=6)
        yield

        def st_upd(g, ps):
            k.tt("dve", St.a, St.a, egl[0:64, :].un(2).bc([64, 6, 64]), ALU.mult)
            k.tt("dve", St.a, St.a, ps, ALU.add)
            k.cp("act", Stb.a, St.a)
        grp_mm(S, psp, lambda h: [(kd[:, h * 64:(h + 1) * 64], vn6[:, h * 64:(h + 1) * 64])], st_upd, w=64, p=64, ngrp=1, per=6)
        yield
        epilogue(n, ci, ci * 128, zs_)
        yield

    import os
    STOP = 9
    nch = int(os.environ.get("GDN_NCH", NCH))
    interleave(None, chunkA(0))
    for ci in range(nch):
        interleave(chunkB(ci), chunkA(ci + 1) if ci + 1 < nch else None)
    mk.dma("sp", O["gdn_p"].t[l].rearrange("h k v -> k h v"), St.t, reads=[St], writes=[O["gdn_p"]])
    mk.dma("sp", O["conv_p"].t[l], convo.t[125:128, :], reads=[convo], writes=[O["conv_p"]])
    if STOP <= 5:
        arena.off = m0
        mk.barrier()
        return
    mk.barrier()
    arena.off = mchunk
    gdn_sample(S, l, locals())
    arena.off = m0
    mk.barrier()


def gdn_sample(S, l, loc):
    k = S["k"]; mk = S["mk"]; arena = S["arena"]; I = S["I"]; O = S["O"]
    prologue = loc["prologue"]; epilogue = loc["epilogue"]
    qkn = loc["qkn"]; qkv = loc["qkv"]; beta = loc["beta"]; g = loc["g"]; eg = loc["eg"]; o = loc["o"]; convo = loc["convo"]
    n = NS
    P = NS * 6
    Ss = arena.f32([P, 64, 64])
    mk.dma("act", Ss.t, I["st_gdn"].t[l].rearrange("s h k v -> (s h) k v"), reads=[I["st_gdn"]], writes=[Ss])
    cbuf = arena.f32([NS, 3, 1152])
    S["cbuf"] = cbuf
    import os
    SUB = os.environ.get("GDN_SUB", "")
    mk.dma("sp", cbuf.t, I["st_conv"].t[l], reads=[I["st_conv"]], writes=[cbuf])
    if SUB != "a":
        for _ in prologue(n, HOFF + T, True):
            pass
    if SUB != "b":
        mk.dma("sp", O["conv_s"].t[l, :, 0:2, :], I["st_conv"].t[l, :, 1:3, :], reads=[I["st_conv"]], writes=[O["conv_s"]])
        mk.dma("sp", O["conv_s"].t[l, :, 2, :], convo.t[0:n, :], reads=[convo], writes=[O["conv_s"]])
    import os
    STOP = int(os.environ.get("GDN_STOP", "9"))
    if STOP <= 6:
        return
    k.act(eg[0:n, :], g[0:n, :], AF.Exp)
    tm = arena.f32([NS, 6, 200])
    k.cp("dve", tm[:, :, 0:64], qkn[0:n, 0:384].rr("p (h d) -> p h d", d=64))
    k.cp("pool", tm[:, :, 64:128], qkn[0:n, 384:768].rr("p (h d) -> p h d", d=64))
    k.cp("dve", tm[:, :, 128:192], qkv[0:n, 768:1152].rr("p (h d) -> p h d", d=64))
    k.cp("pool", tm[:, :, 192:193], beta[0:n, :].un(2))
    k.cp("pool", tm[:, :, 193:194], eg[0:n, :].un(2))
    bo = S["bounce"]
    mk.dma("sp", bo.t.rearrange("s (h c) -> s h c", h=6), tm.t, reads=[tm], writes=[bo])
    P = NS * 6
    sh = arena.f32([P, 200])
    mk.dma("sp", sh.t, bo.t.rearrange("s (h c) -> (s h) c", h=6), reads=[bo], writes=[sh])
    tmp = arena.f32([P, 64, 64])
    if STOP <= 7:
        return
    q = sh[:, 0:64]; kk = sh[:, 64:128]; v = sh[:, 128:192]; be = sh[:, 192:193]; egc = sh[:, 193:194]
    kS = arena.f32([P, 64]); vn = arena.f32([P, 64]); os_ = arena.f32([P, 64])
    k.tt("dve", tmp.a, Ss.a, kk.un(2).bc([P, 64, 64]), ALU.mult)
    k.red(kS.a, tmp.a.rr("p k v -> p v k"))
    k.ts("dve", kS.a, kS.a, egc, ALU.mult)
    k.tt("dve", vn.a, v, kS.a, ALU.subtract)
    k.ts("dve", vn.a, vn.a, be, ALU.mult)
    k.tt("pool", tmp.a, kk.un(2).bc([P, 64, 64]), vn.a.un(1).bc([P, 64, 64]), ALU.mult)
    k.stt(Ss.a, Ss.a, egc, tmp.a, ALU.mult, ALU.add)
    mk.dma("sp", O["gdn_s"].t[l].rearrange("s h k v -> (s h) k v"), Ss.t, reads=[Ss], writes=[O["gdn_s"]])
    k.tt("dve", tmp.a, Ss.a, q.un(2).bc([P, 64, 64]), ALU.mult)
    k.red(os_.a, tmp.a.rr("p k v -> p v k"))
    if STOP <= 8:
        return
    b2 = S["bounce2"]
    mk.dma("sp", b2.t, os_.t, reads=[os_], writes=[b2])
    mk.dma("sp", o.t[0:n, :], b2.t.rearrange("(s h) d -> s (h d)", h=6), reads=[b2], writes=[o])
    epilogue(n, NCH, T)


def rwkv_phase(S, l):
    k = S["k"]; mk = S["mk"]; arena = S["arena"]; I = S["I"]; O = S["O"]; banks = S["banks"]; hT = S["hT"]
    identF = S["identF"]
    m0 = arena.off
    Wb = arena.bf16([128, 8, B_COLS]); Wo = arena.bf16([128, 3, 1024])
    load_w(S, Wb, I["w_in"].t[l, :, B0:B0 + B_COLS], I["w_in"])
    load_w(S, Wo, I["w_out"].t[l, 384:768, :], I["w_out"])
    wa2 = arena.f32([128, 384]); g2 = arena.f32([128, 384])
    mk.dma("sp", wa2.t[0:64, :], I["rwkv_w2"].t[l], reads=[I["rwkv_w2"]], writes=[wa2])
    mk.dma("sp", wa2.t[64:128, :], I["rwkv_a2"].t[l], reads=[I["rwkv_a2"]], writes=[wa2])
    mk.dma("sp", g2.t, I["rwkv_g2"].t[l], reads=[I["rwkv_g2"]], writes=[g2])
    if l > 0:
        v1 = arena.f32([128, 3, 32]); v2 = arena.f32([32, 384])
        mk.dma("sp", v1.t, I["rwkv_v1"].t[l - 1].rearrange("(c p) r -> p c r", p=128), reads=[I["rwkv_v1"]], writes=[v1])
        mk.dma("sp", v2.t, I["rwkv_v2"].t[l - 1], reads=[I["rwkv_v2"]], writes=[v2])

    def bc_load(name, idx, w):
        t = arena.f32([128, w])
        mk.dma("sp", t.t, I[name].t[idx:idx + 1, :].to_broadcast([128, w]), reads=[I[name]], writes=[t])
        return t
    mu = bc_load("rwkv_mu", l, B_COLS); w0 = bc_load("rwkv_w0", l, 384); a0 = bc_load("rwkv_a0", l, 384)
    k_k = bc_load("rwkv_k_k", l, 384); k_a = bc_load("rwkv_k_a", l, 384); r_k = bc_load("rwkv_r_k", l, 384)
    ln_w = bc_load("rwkv_ln_w", l, 384); ln_b = bc_load("rwkv_ln_b", l, 384)
    v0 = bc_load("rwkv_v0", l - 1, 384) if l > 0 else None
    Um = arena.f32([128, 128]); ind2 = arena.f32([128, 2])
    k.memset("pool", ind2.a, 1.0)
    k.asel(ind2[:, 0:1], ind2[:, 0:1], [[0, 1]], ALU.is_gt, 0.0, 64, -1)
    k.asel(ind2[:, 1:2], ind2[:, 1:2], [[0, 1]], ALU.is_ge, 0.0, -64, 1)
    k.tt("pool", Um.a, S["Utri"].a, ind2[:, 0:1].bc([128, 128]), ALU.subtract)
    St = arena.f32([64, 6, 64]); k.memset("pool", St.a, 0.0)
    Sth = [Buf(St.t[:, h, :], f"rwS{h}") for h in range(6)]
    S["yT"] = arena.bf16([128, 8, 128])
    S["xt_pool"] = Pool_([arena.f32([128, D]) for _ in range(2)])
    pS = arena.f32([128, B_COLS]); xs = arena.f32([128, B_COLS]); dd = arena.f32([128, B_COLS])
    loraT = arena.f32([128, 2, 128]); ltmp = arena.f32([128, 2, 128])
    t384 = arena.f32([128, 384]); t2 = arena.f32([128, 384]); logw = arena.f32([128, 384]); aa = arena.f32([128, 384]); gg = arena.f32([128, 384])
    vv = arena.f32([128, 384]); kk = arena.f32([128, 384]); k2 = arena.f32([128, 384]); ka = arena.f32([128, 384])
    s6 = arena.f32([128, 6]); s6b = arena.f32([128, 6]); bs = arena.f32([128, 6]); rs = arena.f32([128, 6])
    vT = arena.f32([128, 3, 128]); vl = arena.f32([32, 128])
    Y = arena.f32([128, 384]); yb = arena.bf16([128, 384])
    mchunk = arena.off
    bh = arena.bf16([128, 384]); ah = arena.bf16([128, 384]); kh = arena.bf16([128, 384]); rh = arena.bf16([128, 384])
    E1 = arena.f32([128, 384]); vvb = arena.bf16([128, 384])
    bhT = arena.bf16([64, 6, 128]); ahT = arena.bf16([64, 6, 128]); khT = arena.bf16([64, 6, 128]); rhT = arena.bf16([64, 6, 128])
    elc = arena.f32([64, 12])

    def mk6(dt=F32):
        return [(arena.f32 if dt == F32 else arena.bf16)([128, 3, 128]) for _ in range(2)]
    ABKT6 = mk6(BF16); ARAT6 = mk6(BF16); ARKT6 = mk6(BF16); TTd6 = mk6(BF16); X60 = mk6(BF16)
    X6p = Pool_([mk6(BF16) for _ in range(2)]); XT6p = Pool_([mk6(BF16) for _ in range(3)])
    TFp = Pool_([mk6() for _ in range(3)]); TBp = Pool_([mk6(BF16) for _ in range(3)])
    S0pf = arena.f32([64, 6, 64]); S0p = arena.bf16([64, 6, 64]); S0pp = arena.f32([64, 6, 64]); Stmp = arena.f32([64, 6, 64])
    ru6 = arena.bf16([128, 384]); U6 = arena.bf16([128, 384])
    StT = arena.f32([64, 6, 64])
    bankpool = Pool_([banks[i] for i in (6, 7, 0, 1, 2, 3, 4, 5)])

    class PSP:
        def get(self, w=128, p=128):
            b = bankpool.get()
            return Ref(b, b.t[0:p, 0:w])
    psp = PSP()
    groups = [(0, 512), (512, 512), (1024, 384)]

    def h3(ref, n):
        return ref[0:n, :].rr("p (h d) -> p h d", d=64)

    def b3(ref, n):
        return ref[0:n, :].un(2).bc([n, 6, 64])

    def prologue(n, col0, ti, prev_sb=None):
        for gi, (c0, w) in enumerate(groups):
            pc = banks[gi]
            for c in range(8):
                k.mm(pc[0:n, 0:w], hT[:, c, col0:col0 + n], Wb[:, c, c0:c0 + w], start=(c == 0), stop=(c == 7))
            k.cp("act", pS[0:n, c0:c0 + w], pc[0:n, 0:w])
            if prev_sb is None:
                pp = banks[3 + gi]
                for c in range(8):
                    k.mm(pp[0:n, 0:w], hT[:, c, col0 - 1:col0 - 1 + n], Wb[:, c, c0:c0 + w], start=(c == 0), stop=(c == 7))
                k.tt("dve", dd[0:n, c0:c0 + w], pp[0:n, 0:w], pS[0:n, c0:c0 + w], ALU.subtract)
            else:
                k.tt("dve", dd[0:n, c0:c0 + w], prev_sb[0:n, c0:c0 + w], pS[0:n, c0:c0 + w], ALU.subtract)
            k.tt("pool", dd[0:n, c0:c0 + w], dd[0:n, c0:c0 + w], mu[0:n, c0:c0 + w], ALU.mult)
            k.tt("pool", xs[0:n, c0:c0 + w], dd[0:n, c0:c0 + w], pS[0:n, c0:c0 + w], ALU.add)
        pt = psp.get(256)
        k.tr(pt[:, 0:n], xs[0:n, 1152:1280], identF[0:n, 0:n])
        k.tr(pt[:, 128:128 + n], xs[0:n, 1280:1408], identF[0:n, 0:n])
        k.act(ltmp[0:64, 0, 0:n], pt[0:64, 0:n], AF.Exp, scale=-2.0)
        k.act(ltmp[0:64, 0, 0:n], ltmp[0:64, 0, 0:n], AF.Ln, bias=1.0)
        k.act(ltmp[0:64, 0, 0:n], ltmp[0:64, 0, 0:n], AF.Exp, scale=-1.0)
        k.ts("dve", loraT[0:64, 0, 0:n], ltmp[0:64, 0, 0:n], 2.0, ALU.mult, -1.0, ALU.add)
        k.cp("dve", loraT[64:128, 0, 0:n], pt[64:128, 0:n])
        k.act(ltmp[:, 1, 0:n], pt[:, 128:128 + n], AF.Exp, scale=-1.0)
        k.act(ltmp[:, 1, 0:n], ltmp[:, 1, 0:n], AF.Ln, bias=1.0)
        k.act(loraT[:, 1, 0:n], ltmp[:, 1, 0:n], AF.Exp, scale=-1.0)
        pw_ = psp.get(384); pa_ = psp.get(384); pg_ = psp.get(384)
        k.mm(pw_[0:n, :], loraT[0:64, 0, 0:n], wa2[0:64, :])
        k.mm(pa_[0:n, :], loraT[64:128, 0, 0:n], wa2[64:128, :])
        k.mm(pg_[0:n, :], loraT[:, 1, 0:n], g2.a)
        k.tt("dve", t384[0:n, :], pw_[0:n, :], w0[0:n, :], ALU.add)
        k.act(t384[0:n, :], t384[0:n, :], AF.Exp, scale=-1.0)
        k.act(t384[0:n, :], t384[0:n, :], AF.Ln, bias=1.0)
        k.act(t384[0:n, :], t384[0:n, :], AF.Exp, scale=-1.0, bias=-0.5)
        k.ts("dve", logw[0:n, :], t384[0:n, :], -1.0, ALU.mult)
        k.tt("dve", t2[0:n, :], pa_[0:n, :], a0[0:n, :], ALU.add)
        k.sigmoid(aa[0:n, :], t2[0:n, :], t2[0:n, :])
        k.cp("act", gg[0:n, :], pg_[0:n, :])
        r = xs[:, 0:384]; kx = xs[:, 384:768]; vx = xs[:, 768:1152]
        t0_, _ = S["tiles"][ti]
        if l == 0:
            k.cp("pool", vv[0:n, :], vx[0:n, :])
            mk.dma("sp", S["vfirst"][t0_:t0_ + n, :], vv.t[0:n, :], reads=[vv], writes=[S["vfb"][ti]])
        else:
            ptv = psp.get(384)
            for c in range(3):
                k.tr(ptv[:, c * 128:c * 128 + n], vx[0:n, c * 128:(c + 1) * 128], identF[0:n, 0:n])
            k.cp("act", vT[:, :, 0:n], ptv.a.rr("p (c t) -> p c t", c=3)[:, :, 0:n])
            pv1 = psp.get(128, 32)
            for c in range(3):
                k.mm(pv1[:, 0:n], v1[:, c, :], vT[:, c, 0:n], start=(c == 0), stop=(c == 2))
            k.cp("act", vl[:, 0:n], pv1[:, 0:n])
            pv2 = psp.get(384)
            k.mm(pv2[0:n, :], vl[:, 0:n], v2.a)
            k.tt("dve", t2[0:n, :], pv2[0:n, :], v0[0:n, :], ALU.add)
            k.sigmoid(t2[0:n, :], t2[0:n, :], t384[0:n, :])
            mk.dma("sp", vv.t[0:n, :], S["vfirst"][t0_:t0_ + n, :], reads=[S["vfb"][ti]], writes=[vv])
            k.tt("dve", vv[0:n, :], vv[0:n, :], vx[0:n, :], ALU.subtract)
            k.tt("pool", vv[0:n, :], vv[0:n, :], t2[0:n, :], ALU.mult)
            k.tt("pool", vv[0:n, :], vv[0:n, :], vx[0:n, :], ALU.add)
        k.tt("pool", kk[0:n, :], kx[0:n, :], k_k[0:n, :], ALU.mult)
        k.tt("pool", t384[0:n, :], kk[0:n, :], kk[0:n, :], ALU.mult)
        k.red(s6[0:n, :], h3(t384, n))
        k.rsqrt_ln(s6[0:n, :], s6[0:n, :], 1.0, 1e-6)
        k.tt("dve", h3(kk, n), h3(kk, n), b3(s6, n), ALU.mult)
        k.stt(t2[0:n, :], aa[0:n, :], -1.0, k_a[0:n, :], ALU.add, ALU.mult)
        k.stt(k2[0:n, :], t2[0:n, :], 1.0, kx[0:n, :], ALU.add, ALU.mult)
        k.tt("pool", t384[0:n, :], r[0:n, :], k2[0:n, :], ALU.mult)
        k.tt("pool", t384[0:n, :], t384[0:n, :], r_k[0:n, :], ALU.mult)
        k.red(bs[0:n, :], h3(t384, n))
        k.tt("pool", ka[0:n, :], kk[0:n, :], aa[0:n, :], ALU.mult)

    def epilogue(n, ti, row0):
        k.red(s6[0:n, :], h3(Y, n))
        k.tt("pool", t384[0:n, :], Y[0:n, :], Y[0:n, :], ALU.mult)
        k.red(s6b[0:n, :], h3(t384, n))
        k.ts("dve", s6[0:n, :], s6[0:n, :], 1.0 / 64, ALU.mult)
        k.tt("dve", rs[0:n, :], s6[0:n, :], s6[0:n, :], ALU.mult)
        k.stt(rs[0:n, :], s6b[0:n, :], 1.0 / 64, rs[0:n, :], ALU.mult, ALU.subtract)
        k.rsqrt_ln(rs[0:n, :], rs[0:n, :], 1.0, 64e-5)
        k.tt("dve", h3(t2, n), h3(Y, n), b3(s6, n), ALU.subtract)
        k.tt("dve", h3(t2, n), h3(t2, n), b3(rs, n), ALU.mult)
        k.tt("pool", t2[0:n, :], t2[0:n, :], ln_w[0:n, :], ALU.mult)
        k.tt("pool", t2[0:n, :], t2[0:n, :], ln_b[0:n, :], ALU.add)
        k.tt("dve", h3(t384, n), h3(vv, n), b3(bs, n), ALU.mult)
        k.tt("pool", t2[0:n, :], t2[0:n, :], t384[0:n, :], ALU.add)
        k.tt("dve", yb[0:n, :], t2[0:n, :], gg[0:n, :], ALU.mult)
        if S["dbg"] and l == 0:
            k.tt("pool", t2[0:n, :], t2[0:n, :], gg[0:n, :], ALU.mult)
            mk.dma("sp", O["dbg_yb"].t[row0:row0 + n, :], t2.t[0:n, :], reads=[t2], writes=[O["dbg_yb"]])
        out_proj(S, yb, n, Wo, 3, ti)

    import os
    for ci in range(int(os.environ.get("GDN_NCH", NCH))):
        n = 128
        prologue(n, HOFF + ci * 128, ci)
        if ci == NCH - 1:
            mk.dma("sp", O["shift_p"].t[l:l + 1, :], pS.t[127:128, :], reads=[pS], writes=[O["shift_p"]])
        r = xs[:, 0:384]
        pL = psp.get(384)
        k.mm(pL.a, Um.a, logw.a)
        k.tt("dve", E1.a, pL.a, logw.a, ALU.subtract)
        k.act(E1.a, E1.a, AF.Exp)
        k.tt("dve", bh.a, kk.a, E1.a, ALU.mult)
        k.act(E1.a, pL.a, AF.Exp, scale=-1.0)
        k.stt(ah.a, ka.a, -1.0, E1.a, ALU.mult, ALU.mult)
        k.tt("dve", kh.a, k2.a, E1.a, ALU.mult)
        k.act(t384.a, pL.a, AF.Exp)
        k.tt("dve", rh.a, r, t384.a, ALU.mult)
        k.cp("act", vvb.a, vv.a)
        pel = psp.get(12, 64)
        for h in range(6):
            k.mm(pel[:, 2 * h:2 * h + 2], logw[:, h * 64:(h + 1) * 64], ind2.a)
        k.act(elc.a, pel.a, AF.Exp)
        for (src, dst, eng) in ((bh, bhT, "act"), (ah, ahT, "dve"), (kh, khT, "act"), (rh, rhT, "dve")):
            pb_ = bankpool.get()
            ptr = Ref(pb_, pb_.t[:, :].bitcast(BF16))
            for h in range(6):
                k.tr(ptr[0:64, h * 128:(h + 1) * 128], src[:, h * 64:(h + 1) * 64], S["identB"].a)
            k.cp(eng, dst.a, ptr[0:64, 0:768].rr("p (h t) -> p h t", t=128))
        smb = S["sm"].a.un(1).bc([128, 3, 128]); smTb = S["smT"].a.un(1).bc([128, 3, 128]); cmTb = S["cmT"].a.un(1).bc([128, 3, 128])
        X6 = X60; XT6 = XT6p.get()
        grp_mm(S, psp, lambda h: [(bhT[:, h, :], ahT[:, h, :])], lambda g, ps: k.tt("dve", X6[g].a, ps, smb, ALU.mult))
        grp_mm(S, psp, lambda h: [(ahT[:, h, :], bhT[:, h, :])], lambda g, ps: k.tt("dve", XT6[g].a, ps, smTb, ALU.mult))
        grp_mm(S, psp, lambda h: [(khT[:, h, :], bhT[:, h, :])], lambda g, ps: k.tt("dve", ABKT6[g].a, ps, smTb, ALU.mult))
        grp_mm(S, psp, lambda h: [(ahT[:, h, :], rhT[:, h, :])], lambda g, ps: k.tt("dve", ARAT6[g].a, ps, cmTb, ALU.mult))
        grp_mm(S, psp, lambda h: [(khT[:, h, :], rhT[:, h, :])], lambda g, ps: k.tt("dve", ARKT6[g].a, ps, cmTb, ALU.mult))
        for _ in tri_inverse6(S, X6, XT6, (X6p, XT6p, TFp, TBp), psp, TTd6):
            pass
        TT6 = TTd6
        elc3 = elc.a.rr("p (h two) -> p h two", two=2)
        elm_b = elc3[:, :, 0:1].bc([64, 6, 64]); ecm_b = elc3[:, :, 1:2].bc([64, 6, 64])
        k.tt("dve", S0pf.a, St.a, elm_b, ALU.mult)
        k.cp("act", S0p.a, S0pf.a)
        k.tt("pool", S0pp.a, S0pf.a, ecm_b, ALU.mult)
        hs_ = lambda h: slice(h * 64, (h + 1) * 64)
        grp_mm(S, psp, lambda h: [(bhT[:, h, :], S0p[:, h, :]), (hv(ABKT6, h), vvb[:, hs_(h)])],
               lambda g, ps: k.cp("act", ru6.a.rr("p (h d) -> p h d", d=64), ps), w=64, ngrp=1, per=6)
        grp_mm(S, psp, lambda h: [(hv(TT6, h), ru6[:, hs_(h)])],
               lambda g, ps: k.cp("act", U6.a.rr("p (h d) -> p h d", d=64), ps), w=64, ngrp=1, per=6)
        grp_mm(S, psp, lambda h: [(rhT[:, h, :], S0p[:, h, :]), (hv(ARAT6, h), U6[:, hs_(h)]), (hv(ARKT6, h), vvb[:, hs_(h)])],
               lambda g, ps: k.cp("act", Y.a.rr("p (h d) -> p h d", d=64), ps), w=64, ngrp=1, per=6)

        def st_upd(g, ps):
            k.tt("dve", Stmp.a, ps, ecm_b, ALU.mult)
            k.tt("dve", St.a, Stmp.a, S0pp.a, ALU.add)
        grp_mm(S, psp, lambda h: [(ah[:, hs_(h)], U6[:, hs_(h)]), (kh[:, hs_(h)], vvb[:, hs_(h)])], st_upd, w=64, p=64, ngrp=1, per=6)
        epilogue(n, ci, ci * 128)
    for hh in range(0, 6, 4):
        nh = min(4, 6 - hh)
        ptr = psp.get(512, 64)
        for h in range(hh, hh + nh):
            k.tr(ptr[:, (h - hh) * 64:(h - hh + 1) * 64], St[:, h, :], identF[0:64, 0:64])
        k.cp("act", StT[:, hh:hh + nh, :], ptr[:, 0:nh * 64].rr("p (h t) -> p h t", t=64))
    mk.dma("sp", O["rwkv_p"].t[l].rearrange("h v k -> v h k"), StT.t, reads=[StT], writes=[O["rwkv_p"]])
    mk.barrier()
    arena.off = mchunk
    n = NS
    prev = arena.f32([NS, B_COLS])
    mk.dma("sp", prev.t, I["st_shift"].t[l], reads=[I["st_shift"]], writes=[prev])
    P = NS * 6
    Ss = arena.f32([P, 64, 64])
    mk.dma("act", Ss.t, I["st_rwkv"].t[l].rearrange("s h v k -> (s h) v k"), reads=[I["st_rwkv"]], writes=[Ss])
    prologue(n, HOFF + T, NCH, prev_sb=prev)
    mk.dma("sp", O["shift_s"].t[l], pS.t[0:n, :], reads=[pS], writes=[O["shift_s"]])
    wdec = arena.f32([NS, 384])
    k.act(wdec.a, logw[0:n, :], AF.Exp)
    tm = arena.f32([NS, 6, 392])
    srcs = [xs[0:n, 0:384], wdec.a, k2[0:n, :], vv[0:n, :], kk[0:n, :], ka[0:n, :]]
    for j, sref in enumerate(srcs):
        k.cp("dve" if j % 2 == 0 else "pool", tm[:, :, j * 64:(j + 1) * 64], sref.rr("p (h d) -> p h d", d=64))
    bo = S["bounceB"]
    mk.dma("sp", bo.t.rearrange("s (h c) -> s h c", h=6), tm.t, reads=[tm], writes=[bo])
    P = NS * 6
    sh = arena.f32([P, 392])
    mk.dma("sp", sh.t, bo.t.rearrange("s (h c) -> (s h) c", h=6), reads=[bo], writes=[sh])
    tmp = arena.f32([P, 64, 64])
    r_ = sh[:, 0:64]; w_ = sh[:, 64:128]; k2_ = sh[:, 128:192]; v_ = sh[:, 192:256]; kk_ = sh[:, 256:320]; ka_ = sh[:, 320:384]
    sa = arena.f32([P, 64]); ys = arena.f32([P, 64])
    rowb = lambda ref: ref.un(1).bc([P, 64, 64])
    colb = lambda ref: ref.un(2).bc([P, 64, 64])
    k.tt("dve", tmp.a, Ss.a, rowb(kk_), ALU.mult)
    k.red(sa.a, tmp.a)
    k.ts("dve", sa.a, sa.a, -1.0, ALU.mult)
    k.tt("dve", Ss.a, Ss.a, rowb(w_), ALU.mult)
    k.tt("pool", tmp.a, colb(sa.a), rowb(ka_), ALU.mult)
    k.tt("dve", Ss.a, Ss.a, tmp.a, ALU.add)
    k.tt("pool", tmp.a, colb(v_), rowb(k2_), ALU.mult)
    k.tt("dve", Ss.a, Ss.a, tmp.a, ALU.add)
    mk.dma("sp", O["rwkv_s"].t[l].rearrange("s h v k -> (s h) v k"), Ss.t, reads=[Ss], writes=[O["rwkv_s"]])
    k.tt("dve", tmp.a, Ss.a, rowb(r_), ALU.mult)
    k.red(ys.a, tmp.a)
    b2 = S["bounce2"]
    mk.dma("sp", b2.t, ys.t, reads=[ys], writes=[b2])
    mk.dma("sp", Y.t[0:n, :], b2.t.rearrange("(s h) d -> s (h d)", h=6), reads=[b2], writes=[Y])
    epilogue(n, NCH, T)
    arena.off = m0
    mk.barrier()


def hgrn_phase(S, l):
    k = S["k"]; mk = S["mk"]; arena = S["arena"]; I = S["I"]; O = S["O"]; banks = S["banks"]; hT = S["hT"]
    identF = S["identF"]
    m0 = arena.off
    Wc = arena.bf16([128, 8, C_COLS]); Wo = arena.bf16([128, 2, 1024])
    load_w(S, Wc, I["w_in"].t[l, :, C0:C0 + C_COLS], I["w_in"])
    load_w(S, Wo, I["w_out"].t[l, 768:1024, :], I["w_out"])
    lbr = arena.f32([128, 4, 256]); mx = arena.f32([128, 256]); oml = arena.f32([128, 256]); ssum = arena.f32([128, 256])
    mk.dma("sp", lbr.t, I["hgrn_lower_bounds"].t.rearrange("(o l) c -> o l c", o=1).to_broadcast([128, 4, 256]),
           reads=[I["hgrn_lower_bounds"]], writes=[lbr])
    k.tt("dve", mx.a, lbr[:, 0, :], lbr[:, 1, :], ALU.max)
    k.tt("dve", mx.a, mx.a, lbr[:, 2, :], ALU.max)
    k.tt("dve", mx.a, mx.a, lbr[:, 3, :], ALU.max)
    for j in range(4):
        k.tt("dve", lbr[:, j, :], lbr[:, j, :], mx.a, ALU.subtract)
    k.act(lbr.a, lbr.a, AF.Exp)
    k.tt("dve", ssum.a, lbr[:, 0, :], lbr[:, 1, :], ALU.add)
    k.tt("dve", ssum.a, ssum.a, lbr[:, 2, :], ALU.add)
    k.tt("dve", ssum.a, ssum.a, lbr[:, 3, :], ALU.add)
    k.recip(ssum.a, ssum.a)
    k.memset("pool", oml.a, 0.0)
    for j in range(1, l + 1):
        k.tt("dve", oml.a, oml.a, lbr[:, j, :], ALU.add)
    k.tt("dve", oml.a, oml.a, ssum.a, ALU.mult)
    k.ts("dve", oml.a, oml.a, -1.0, ALU.mult, 1.0, ALU.add)
    gnw = arena.f32([128, 64])
    mk.dma("sp", gnw.t, I["hgrn_norm_w"].t[l:l + 1, :].to_broadcast([128, 64]), reads=[I["hgrn_norm_w"]], writes=[gnw])
    Um1 = arena.f32([128, 128]); Um2 = arena.f32([128, 128]); bdm = arena.f32([128, 128])
    k.memset("pool", Um1.a, 0.0)
    k.memset("pool", Um1[0:32, 0:64], 1.0)
    k.memset("pool", Um1[0:96, 64:128], 1.0)
    k.tt("pool", Um1.a, S["Utri"].a, Um1.a, ALU.subtract)
    k.memset("pool", Um2.a, 0.0)
    k.memset("pool", Um2[0:64, :], 1.0)
    k.tt("pool", Um2.a, S["Utri"].a, Um2.a, ALU.subtract)
    k.cp("pool", bdm.a, S["cmT"].a)
    k.memset("pool", bdm[0:64, 64:128], 0.0)
    St = arena.f32([64, 4, 64]); k.memset("pool", St.a, 0.0)
    Sth = [Buf(St.t[:, h, :], f"hgS{h}") for h in range(4)]
    S["yT"] = arena.bf16([128, 8, 128])
    S["xt_pool"] = Pool_([arena.f32([128, D]) for _ in range(2)])
    qs = arena.f32([128, 256]); kf = arena.f32([128, 256]); logf = arena.f32([128, 256]); iv = arena.f32([128, 256]); gs = arena.f32([128, 256])
    t1 = arena.f32([128, 256]); t2 = arena.f32([128, 256])
    o = arena.f32([128, 256]); y32 = arena.f32([128, 256]); yb = arena.bf16([128, 256]); rs = arena.f32([128, 4])
    mchunk = arena.off
    q1 = arena.bf16([128, 256]); k1 = arena.bf16([128, 256]); q2 = arena.bf16([128, 256]); k2 = arena.bf16([128, 256])
    qS = arena.bf16([128, 256]); kE = arena.bf16([128, 256]); ee = arena.f32([128, 256]); ivb = arena.bf16([128, 256])
    Stb = arena.bf16([64, 4, 64]); k.memset("pool", Stb.a, 0.0)
    AT4 = arena.bf16([128, 4, 128]); AT4f = arena.f32([128, 4, 128])
    k.memset("pool", q2.a, 0.0); k.memset("pool", k2.a, 0.0)
    q1T = arena.bf16([64, 4, 128]); k1T = arena.bf16([64, 4, 128]); q2T = arena.bf16([64, 4, 128]); k2T = arena.bf16([64, 4, 128]); qST = arena.bf16([64, 4, 128])
    eend = arena.f32([64, 4])
    ATp = Pool_([arena.f32([128, 128]) for _ in range(4)])
    bankpool = Pool_([banks[i] for i in (2, 3, 4, 5, 6, 7, 0, 1)])

    class PSP:
        def get(self, w=128, p=128):
            b = bankpool.get()
            return Ref(b, b.t[0:p, 0:w])
    psp = PSP()

    def h3(ref, n):
        return ref[0:n, :].rr("p (h d) -> p h d", d=64)

    def b3(ref, n):
        return ref[0:n, :].un(2).bc([n, 4, 64])

    def prologue(n, col0):
        for gi in range(2):
            for c in range(8):
                k.mm(banks[gi][0:n, :], hT[:, c, col0:col0 + n], Wc[:, c, gi * 512:(gi + 1) * 512], start=(c == 0), stop=(c == 7))
        k.cp("act", t1[0:n, :], banks[0][0:n, 0:256])
        k.sigmoid(t2[0:n, :], t1[0:n, :], t2[0:n, :])
        k.stt(qs[0:n, :], t1[0:n, :], 0.125, t2[0:n, :], ALU.mult, ALU.mult)
        k.sigmoid(t2[0:n, :], banks[0][0:n, 256:512], t2[0:n, :], sign=-1.0)
        k.tt("pool", kf[0:n, :], t2[0:n, :], oml[0:n, :], ALU.mult)
        k.act(logf[0:n, :], kf[0:n, :], AF.Ln, bias=1.0, scale=-1.0)
        k.cp("act", iv[0:n, :], banks[1][0:n, 0:256])
        k.cp("act", t1[0:n, :], banks[1][0:n, 256:512])
        k.sigmoid(t2[0:n, :], t1[0:n, :], t2[0:n, :])
        k.tt("pool", gs[0:n, :], t1[0:n, :], t2[0:n, :], ALU.mult)

    def epilogue(n, ti, row0):
        k.tt("pool", t1[0:n, :], o[0:n, :], o[0:n, :], ALU.mult)
        k.red(rs[0:n, :], h3(t1, n))
        k.rsqrt_ln(rs[0:n, :], rs[0:n, :], 1.0 / 64, 1e-6)
        k.tt("dve", h3(y32, n), h3(o, n), b3(rs, n), ALU.mult)
        k.tt("pool", h3(y32, n), h3(y32, n), gnw[0:n, :].un(1).bc([n, 4, 64]), ALU.mult)
        k.tt("dve", yb[0:n, :], y32[0:n, :], gs[0:n, :], ALU.mult)
        if S["dbg"] and l == 0:
            k.tt("pool", y32[0:n, :], y32[0:n, :], gs[0:n, :], ALU.mult)
            mk.dma("sp", O["dbg_yc"].t[row0:row0 + n, :], y32.t[0:n, :], reads=[y32], writes=[O["dbg_yc"]])
        out_proj(S, yb, n, Wo, 2, ti)

    import os
    for ci in range(int(os.environ.get("GDN_NCH", NCH))):
        n = 128
        prologue(n, HOFF + ci * 128)
        pL1 = psp.get(256); pL2 = psp.get(256); pL = psp.get(256); pLe = psp.get(256)
        k.mm(pL1.a, Um1.a, logf.a)
        k.mm(pL2.a, Um2.a, logf.a)
        k.mm(pL.a, S["Utri"].a, logf.a)
        k.mm(pLe.a, S["sm"].a, logf.a)
        k.ts("dve", ee.a, pL1.a, 80.0, ALU.min)
        k.act(ee.a, ee.a, AF.Exp)
        k.tt("dve", q1.a, qs.a, ee.a, ALU.mult)
        k.ts("dve", ee.a, pL1.a, -1.0, ALU.mult, 80.0, ALU.min)
        k.act(ee.a, ee.a, AF.Exp)
        k.tt("dve", k1.a, kf.a, ee.a, ALU.mult)
        k.act(t1[64:128, :], pL2[64:128, :], AF.Exp)
        k.tt("pool", q2[64:128, :], qs[64:128, :], t1[64:128, :], ALU.mult)
        k.act(t1[0:64, :], pL2[0:64, :], AF.Exp, scale=-1.0)
        k.tt("pool", k2[0:64, :], kf[0:64, :], t1[0:64, :], ALU.mult)
        k.act(t2.a, pL.a, AF.Exp)
        k.tt("dve", qS.a, qs.a, t2.a, ALU.mult)
        k.act(t2.a, pLe.a, AF.Exp)
        k.tt("dve", kE.a, kf.a, t2.a, ALU.mult)
        pe_ = psp.get(4, 64)
        for h in range(4):
            k.mm(pe_[:, h:h + 1], logf[:, h * 64:(h + 1) * 64], S["ones"][:, 0:1])
        k.act(eend.a, pe_.a, AF.Exp)
        k.cp("act", ivb.a, iv.a)
        for (src, dst, eng) in ((q1, q1T, "act"), (k1, k1T, "dve"), (q2, q2T, "act"), (k2, k2T, "dve"), (qS, qST, "act")):
            pb_ = bankpool.get()
            ptr = Ref(pb_, pb_.t[:, :].bitcast(BF16))
            for h in range(4):
                k.tr(ptr[0:64, h * 128:(h + 1) * 128], src[:, h * 64:(h + 1) * 64], S["identB"].a)
            k.cp(eng, dst.a, ptr[0:64, 0:512].rr("p (h t) -> p h t", t=128))
        bdm4 = bdm.a.un(1).bc([128, 4, 128])
        grp_mm(S, psp, lambda h: [(k1T[:, h, :], q1T[:, h, :])], lambda g, ps: k.tt("dve", AT4f.a, ps, bdm4, ALU.mult), ngrp=1, per=4)
        grp_mm(S, psp, lambda h: [(k2T[:, h, :], q2T[:, h, :])], lambda g, ps: k.tt("dve", AT4.a, AT4f.a, ps, ALU.add), ngrp=1, per=4)
        hs_ = lambda h: slice(h * 64, (h + 1) * 64)
        grp_mm(S, psp, lambda h: [(qST[:, h, :], Stb[:, h, :]), (AT4[:, h, :], ivb[:, hs_(h)])],
               lambda g, ps: k.cp("act", o.a.rr("p (h d) -> p h d", d=64), ps), w=64, ngrp=1, per=4)

        def st_upd(g, ps):
            k.tt("dve", St.a, St.a, eend.a.un(2).bc([64, 4, 64]), ALU.mult)
            k.tt("dve", St.a, St.a, ps, ALU.add)
            k.cp("act", Stb.a, St.a)
        grp_mm(S, psp, lambda h: [(kE[:, hs_(h)], ivb[:, hs_(h)])], st_upd, w=64, p=64, ngrp=1, per=4)
        epilogue(n, ci, ci * 128)
    mk.dma("sp", O["hgrn_p"].t[l].rearrange("h d v -> d h v"), St.t, reads=[St], writes=[O["hgrn_p"]])
    mk.barrier()
    arena.off = mchunk
    n = NS
    P = NS * 4
    Ss = arena.f32([P, 64, 64])
    mk.dma("act", Ss.t, I["st_hgrn"].t[l].rearrange("s h d v -> (s h) d v"), reads=[I["st_hgrn"]], writes=[Ss])
    prologue(n, HOFF + T)
    fdec = arena.f32([NS, 256])
    k.act(fdec.a, logf[0:n, :], AF.Exp)
    tm = arena.f32([NS, 4, 256])
    for j, sref in enumerate([qs[0:n, :], kf[0:n, :], fdec.a, iv[0:n, :]]):
        k.cp("dve" if j % 2 == 0 else "pool", tm[:, :, j * 64:(j + 1) * 64], sref.rr("p (h d) -> p h d", d=64))
    bo = S["bounceC"]
    bo_v = bo.t
    mk.dma("sp", bo_v.rearrange("s (h c) -> s h c", h=4), tm.t, reads=[tm], writes=[bo])
    P = NS * 4
    sh = arena.f32([P, 256])
    mk.dma("sp", sh.t, bo_v.rearrange("s (h c) -> (s h) c", h=4), reads=[bo], writes=[sh])
    tmp = arena.f32([P, 64, 64])
    q_ = sh[:, 0:64]; k_ = sh[:, 64:128]; f_ = sh[:, 128:192]; i_ = sh[:, 192:256]
    os_ = arena.f32([P, 64])
    colb = lambda ref: ref.un(2).bc([P, 64, 64])
    rowb = lambda ref: ref.un(1).bc([P, 64, 64])
    k.tt("dve", Ss.a, Ss.a, colb(f_), ALU.mult)
    k.tt("pool", tmp.a, colb(k_), rowb(i_), ALU.mult)
    k.tt("dve", Ss.a, Ss.a, tmp.a, ALU.add)
    mk.dma("sp", O["hgrn_s"].t[l].rearrange("s h d v -> (s h) d v"), Ss.t, reads=[Ss], writes=[O["hgrn_s"]])
    k.tt("dve", tmp.a, Ss.a, colb(q_), ALU.mult)
    k.red(os_.a, tmp.a.rr("p d v -> p v d"))
    b2 = S["bounce2C"]
    mk.dma("sp", b2.t, os_.t, reads=[os_], writes=[b2])
    mk.dma("sp", o.t[0:n, :], b2.t.rearrange("(s h) d -> s (h d)", h=4), reads=[b2], writes=[o])
    epilogue(n, NCH, T)
    arena.off = m0
    mk.barrier()


def ffn_phase(S, l, norm_phase):
    k = S["k"]; mk = S["mk"]; arena = S["arena"]; I = S["I"]; O = S["O"]; banks = S["banks"]; hT = S["hT"]
    norm_phase("norm_ffn", l)
    m0 = arena.off
    quarters = [(0, 6), (6, 6), (12, 5), (17, 5)]
    wb = []
    for j in range(2):
        wb.append((arena.bf16([128, 8, 768]), arena.bf16([128, 8, 768]), arena.bf16([128, 6, 1024])))
    uTs = [arena.bf16([128, 6, 512]) for _ in range(2)]
    xts = Pool_([arena.f32([128, D]) for _ in range(3)])
    sgs = Pool_([arena.f32([128, 512]) for _ in range(2)])
    tgs = Pool_([arena.f32([128, 512]) for _ in range(2)])
    bankpool = Pool_([banks[i] for i in range(8)])
    blocks = [(0, 512), (512, 512), (1024, 512), (1536, 512), (T, NS)]

    def loadq(qi):
        f0, nf = quarters[qi]
        Wg, Wu, Wd = wb[qi % 2]
        mk.dma("pool", Wg.t[:, :, 0:nf * 128], I["w_gate"].t[l, :, f0 * 128:(f0 + nf) * 128].rearrange("(c p) n -> p c n", p=128),
               reads=[I["w_gate"]], writes=[Wg])
        mk.dma("pool", Wu.t[:, :, 0:nf * 128], I["w_up"].t[l, :, f0 * 128:(f0 + nf) * 128].rearrange("(c p) n -> p c n", p=128),
               reads=[I["w_up"]], writes=[Wu])
        mk.dma("pool", Wd.t[:, 0:nf, :], I["w_down"].t[l, f0 * 128:(f0 + nf) * 128, :].rearrange("(c p) n -> p c n", p=128),
               reads=[I["w_down"]], writes=[Wd])
    loadq(0)

    def gateup(qi, bi, uT):
        f0_, nf = quarters[qi]
        Wg, Wu, Wd = wb[qi % 2]
        t0, nb = blocks[bi]
        col0 = HOFF + t0
        for f in range(nf):
            pg = bankpool.get(); pu = bankpool.get()
            for c in range(8):
                k.mm(pg[:, 0:nb], Wg[:, c, f * 128:(f + 1) * 128], hT[:, c, col0:col0 + nb], start=(c == 0), stop=(c == 7))
            for c in range(8):
                k.mm(pu[:, 0:nb], Wu[:, c, f * 128:(f + 1) * 128], hT[:, c, col0:col0 + nb], start=(c == 0), stop=(c == 7))
            sg = sgs.get(); tg = tgs.get()
            k.cp("dve", tg[:, 0:nb], pg[:, 0:nb])
            k.sigmoid(sg[:, 0:nb], tg[:, 0:nb], sg[:, 0:nb])
            k.tt("dve", tg[:, 0:nb], tg[:, 0:nb], sg[:, 0:nb], ALU.mult)
            k.tt("dve", uT[:, f, 0:nb], tg[:, 0:nb], pu[:, 0:nb], ALU.mult)

    def down(qi, bi, uT):
        f0_, nf = quarters[qi]
        Wg, Wu, Wd = wb[qi % 2]
        t0, nb = blocks[bi]
        for ts_ in range(0, nb, 128):
            n = min(128, nb - ts_)
            ti = (t0 + ts_) // 128
            xt = xts.get()
            mk.dma("sp", xt.t[0:n, :], S["xres"][t0 + ts_:t0 + ts_ + n, :], reads=[S["xb"][ti]], writes=[xt])
            for dh in range(2):
                pd = bankpool.get()
                for f in range(nf):
                    k.mm(pd[0:n, :], uT[:, f, ts_:ts_ + n], Wd[:, f, dh * 512:(dh + 1) * 512], start=(f == 0), stop=(f == nf - 1))
                k.tt("dve", xt[0:n, dh * 512:(dh + 1) * 512], xt[0:n, dh * 512:(dh + 1) * 512], pd[0:n, :], ALU.add)
            mk.dma("sp", S["xres"][t0 + ts_:t0 + ts_ + n, :], xt.t[0:n, :], reads=[xt], writes=[S["xb"][ti]])

    items = [(qi, bi) for qi in range(4) for bi in range(len(blocks))]
    loadq(1)
    gateup(items[0][0], items[0][1], uTs[0])
    for idx, (qi, bi) in enumerate(items):
        if idx + 1 < len(items):
            gateup(items[idx + 1][0], items[idx + 1][1], uTs[(idx + 1) % 2])
        down(qi, bi, uTs[idx % 2])
        if bi == len(blocks) - 1 and qi + 2 < 4:
            loadq(qi + 2)
    arena.off = m0
    mk.barrier()


def final_phase(S):
    k = S["k"]; mk = S["mk"]; arena = S["arena"]; I = S["I"]; O = S["O"]
    m0 = arena.off
    wbc = arena.f32([128, D])
    mk.dma("sp", wbc.t, I["norm_final"].t[0:1, :].to_broadcast([128, D]), reads=[I["norm_final"]], writes=[wbc])
    ss = arena.f32([128, 32])
    k.memset("pool", ss.a, 0.0)
    xts = [arena.f32([128, D]) for _ in range(3)]
    junk = arena.bf16([128, D])
    ys = [arena.f32([128, D]) for _ in range(2)]
    tiles = S["tiles"]
    for i, (t0, n) in enumerate(tiles):
        xt = xts[i % 3]
        mk.dma("sp", xt.t[0:n, :], S["xres"][t0:t0 + n, :], reads=[S["xb"][i]], writes=[xt])
        k.act(junk[0:n, :], xt[0:n, :], AF.Square, accum=ss[0:n, i:i + 1])
    k.rsqrt_ln(ss.a, ss.a, 1.0 / D, 1e-6)
    for i, (t0, n) in enumerate(tiles):
        xt = xts[i % 3]
        mk.dma("sp", xt.t[0:n, :], S["xres"][t0:t0 + n, :], reads=[S["xb"][i]], writes=[xt])
        y = ys[i % 2]
        k.stt(y[0:n, :], xt[0:n, :], ss[0:n, i:i + 1], wbc[0:n, :], ALU.mult, ALU.mult)
        mk.dma("sp", O["y"].t[t0:t0 + n, :], y.t[0:n, :], reads=[y], writes=[O["y"]])
    arena.off = m0


_NC_CACHE = {}


def kernel(**inputs):
    if "nc" not in _NC_CACHE:
        _NC_CACHE["nc"] = build()
    nc = _NC_CACHE["nc"]
    f = lambda a: np.ascontiguousarray(np.asarray(a), dtype=np.float32)
    shared = {}
    for name in ("norm_mix", "w_in", "w_out", "gdn_conv_w", "gdn_a_log", "gdn_dt_bias", "gdn_norm_w", "rwkv_mu", "rwkv_w0",
                 "rwkv_w2", "rwkv_a0", "rwkv_a2", "rwkv_g2", "rwkv_v0", "rwkv_v1", "rwkv_v2", "rwkv_k_k", "rwkv_k_a",
                 "rwkv_ln_w", "rwkv_ln_b", "hgrn_lower_bounds", "hgrn_norm_w", "norm_ffn", "w_gate", "w_up", "w_down"):
        shared[name] = f(inputs[name])
    shared["rwkv_r_k"] = f(inputs["rwkv_r_k"]).reshape(DEPTH, 384)
    shared["norm_final"] = f(inputs["norm_final"]).reshape(1, D)
    xp = f(inputs["x_prompt"]); xs = f(inputs["x_sample"])
    sg = f(inputs["state_gdn"]); sc = f(inputs["state_gdn_conv"]); sr = f(inputs["state_rwkv"])
    ssh = f(inputs["state_rwkv_shift"]); shg = f(inputs["state_hgrn"])
    in_maps = []
    for c in range(8):
        sl = slice(NS * c, NS * (c + 1))
        m = dict(shared)
        m["xin"] = np.ascontiguousarray(np.concatenate([xp[c], xs[sl, 0]], 0))
        m["st_gdn"] = np.ascontiguousarray(sg[:, sl]); m["st_conv"] = np.ascontiguousarray(sc[:, sl])
        m["st_rwkv"] = np.ascontiguousarray(sr[:, sl]); m["st_shift"] = np.ascontiguousarray(ssh[:, sl])
        m["st_hgrn"] = np.ascontiguousarray(shg[:, sl])
        in_maps.append(m)
    res = run_bass_kernel_spmd(nc, in_maps, core_ids=list(range(8)))
    R = res.results
    y_prompt = np.stack([R[c]["y"][:T] for c in range(8)], 0).astype(np.float32)
    y_sample = np.concatenate([R[c]["y"][T:] for c in range(8)], 0)[:, None, :].astype(np.float32)

    def pstack(name):
        return np.stack([R[c][name] for c in range(8)], 1).astype(np.float32)

    def scat(name):
        return np.concatenate([R[c][name] for c in range(8)], 1).astype(np.float32)
    return (y_prompt, y_sample, pstack("gdn_p"), scat("gdn_s"), pstack("conv_p"), scat("conv_s"),
            pstack("rwkv_p"), scat("rwkv_s"), pstack("shift_p"), scat("shift_s"), pstack("hgrn_p"), scat("hgrn_s"))
```

```python
import contextlib
import numpy as np
import concourse.bass as bass
import concourse.mybir as mybir

F32 = mybir.dt.float32
BF16 = mybir.dt.bfloat16
AF = mybir.ActivationFunctionType
ALU = mybir.AluOpType
AX = mybir.AxisListType

import os as _os
SAME_ENGINE_SYNC = _os.environ.get("SES", "1") == "1"


class Buf:
    __slots__ = ("t", "lastw", "readers", "name", "excl")

    def __init__(self, t, name=""):
        self.t = t
        self.excl = False
        self.lastw = None
        self.readers = []
        self.name = name

    def __getitem__(self, idx):
        return Ref(self, self.t[idx])

    @property
    def a(self):
        return Ref(self, self.t[:])


class MK:
    ENGS = ("pe", "act", "dve", "pool", "sp")

    def __init__(self, nc, n_dma_sems=6):
        self.nc = nc
        self.stack = contextlib.ExitStack()
        self.prog = {e: [] for e in self.ENGS}
        self.cnt = {e: 0 for e in self.ENGS}
        self.sems = {}
        for e in self.ENGS:
            self.sems[e] = self.stack.enter_context(nc.semaphore("s_" + e))
        self.dsem = {}
        self.dcnt = {}
        self.dnext = {}
        for q in ("sp", "pool", "act"):
            self.dsem[q] = [self.stack.enter_context(nc.semaphore(f"d_{q}{i}")) for i in range(n_dma_sems)]
            self.dcnt[q] = [0] * n_dma_sems
            self.dnext[q] = 0
        self.seen = {e: {} for e in self.ENGS}
        self.semobj = {}
        for e in self.ENGS:
            self.semobj[("e", e)] = self.sems[e]
        for q in self.dsem:
            for i, s in enumerate(self.dsem[q]):
                self.semobj[("d", q, i)] = s
        self.nuniq = 0

    def sb(self, shape, dtype=F32, name=None):
        self.nuniq += 1
        name = f"{name or 'sb'}_{self.nuniq}"
        t = self.stack.enter_context(self.nc.sbuf_tensor(name, list(shape), dtype))
        return Buf(t, name)

    def ps(self, shape, dtype=F32, name=None):
        self.nuniq += 1
        name = f"{name or 'ps'}_{self.nuniq}"
        t = self.stack.enter_context(self.nc.psum_tensor(name, list(shape), dtype))
        b = Buf(t, name)
        b.excl = True
        return b

    def view(self, buf_or_ap, name=""):
        return Buf(buf_or_ap, name)

    def _need(self, eng, reads, writes):
        need = {}

        def add(ev):
            if ev is None:
                return
            key, val, en = ev
            if en == eng and key[0] == "e" and (eng == "pe" or not SAME_ENGINE_SYNC):
                return
            if need.get(key, 0) < val:
                need[key] = val

        for b in reads:
            add(b.lastw)
        for b in writes:
            add(b.lastw)
            for r in b.readers:
                add(r)
        out = []
        seen = self.seen[eng]
        for key, val in need.items():
            if seen.get(key, 0) < val:
                seen[key] = val
                out.append((key, val))
        return out

    def _commit(self, ev, reads, writes):
        for b in reads:
            b.readers.append(ev)
        for b in writes:
            b.lastw = ev
            b.readers = []

    def op(self, eng, fn, reads=(), writes=()):
        rx = [b for b in reads if b.excl]
        if rx:
            writes = list(writes) + rx
            reads = [b for b in reads if not b.excl]
        waits = self._need(eng, reads, writes)
        self.cnt[eng] += 1
        n = self.cnt[eng]
        ev = (("e", eng), n, eng)
        self._commit(ev, reads, writes)
        self.prog[eng].append((waits, fn, ("e", eng), 1))
        return ev

    def dma(self, q, out_ap, in_ap, reads=(), writes=(), **kw):
        i = self.dnext[q]
        self.dnext[q] = (i + 1) % len(self.dsem[q])
        key = ("d", q, i)
        prev = self.dcnt[q][i]
        waits = self._need(q, reads, writes)
        if prev > 0 and self.seen[q].get(key, 0) < prev:
            self.seen[q][key] = prev
            waits.append((key, prev))
        self.dcnt[q][i] = prev + 16
        ev = (key, prev + 16, "dma_" + q)
        self._commit(ev, reads, writes)

        def fn(e, out_ap=out_ap, in_ap=in_ap, kw=kw):
            return e.dma_start(out=out_ap, in_=in_ap, **kw)
        self.prog[q].append((waits, fn, key, 16))
        return ev

    def barrier(self):
        tgt = {}
        for e in self.ENGS:
            if self.cnt[e] > 0:
                tgt[("e", e)] = self.cnt[e]
        for q in self.dsem:
            for i in range(len(self.dsem[q])):
                if self.dcnt[q][i] > 0:
                    tgt[("d", q, i)] = self.dcnt[q][i]
        for e in self.ENGS:
            waits = []
            for k, v in tgt.items():
                if self.seen[e].get(k, 0) < v:
                    self.seen[e][k] = v
                    waits.append((k, v))
            if waits:
                self.prog[e].append((waits, None, None, 0))

    def all_dma_events(self):
        return [(("d", q, i), self.dcnt[q][i], "") for q in self.dsem
                for i in range(len(self.dsem[q])) if self.dcnt[q][i] > 0]

    def emit(self, final_events):
        nc = self.nc
        prog = self.prog
        semobj = self.semobj
        fw = {}
        for key, val, _ in final_events:
            fw[key] = max(fw.get(key, 0), val)

        def run(engname, e):
            for waits, fn, key, inc in prog[engname]:
                for k, v in waits:
                    e.wait_ge(semobj[k], v)
                if fn is not None:
                    fn(e).then_inc(semobj[key], inc)

        with nc.Block() as block:
            @block.tensor
            def _(e):
                run("pe", e)

            @block.scalar
            def _(e):
                run("act", e)

            @block.vector
            def _(e):
                run("dve", e)

            @block.gpsimd
            def _(e):
                run("pool", e)

            @block.sync
            def _(e):
                run("sp", e)
                for k, v in fw.items():
                    e.wait_ge(semobj[k], v)
        self.stack.close()


class Ref:
    __slots__ = ("buf", "ap")

    def __init__(self, buf, ap):
        self.buf = buf
        self.ap = ap

    def __getitem__(self, idx):
        return Ref(self.buf, self.ap[idx])

    @property
    def a(self):
        return self

    def rr(self, s, **kw):
        return Ref(self.buf, self.ap.rearrange(s, **kw))

    def bc(self, shape):
        return Ref(self.buf, self.ap.to_broadcast(list(shape)))

    def un(self, axis):
        return Ref(self.buf, self.ap.unsqueeze(axis))

    def bitcast(self, dt):
        return Ref(self.buf, self.ap.bitcast(dt))


def R_(buf, idx=None):
    return Ref(buf, buf.t[:] if idx is None else buf.t[idx])


from concourse.bass_utils import run_bass_kernel_spmd

D = 1024; T = 2048; NS = 16; NT = T + NS; NCH = 16; DEPTH = 4
HD = 64; HA = 6; HB = 6; HC = 4
A_COLS = 1548; B_COLS = 1408; C_COLS = 1024; IN_COLS = 3980; DFF = 2816
B0 = A_COLS; C0 = A_COLS + B_COLS
HOFF = 3


class Arena:
    def __init__(self, mk, words):
        self.mk = mk
        self.t = mk.stack.enter_context(mk.nc.sbuf_tensor("arena", [128, words], F32))
        self.words = words
        self.off = 0

    def _take(self, n):
        n = (n + 7) // 8 * 8
        o = self.off
        assert o + n <= self.words, f"arena overflow {o}+{n}>{self.words}"
        self.off = o + n
        return o

    def f32(self, shape, name=""):
        p = shape[0]; n = int(np.prod(shape[1:]))
        o = self._take(n)
        ap = self.t[0:p, o:o + n]
        if len(shape) == 3:
            ap = ap.rearrange("p (a b) -> p a b", a=shape[1])
        return Buf(ap, name)

    def bf16(self, shape, name=""):
        p = shape[0]; n = int(np.prod(shape[1:]))
        w = (n + 1) // 2
        o = self._take(w)
        ap = self.t[0:p, o:o + w].bitcast(BF16)[:, 0:n]
        if len(shape) == 3:
            ap = ap.rearrange("p (a b) -> p a b", a=shape[1])
        return Buf(ap, name)


class Pool_:
    def __init__(self, items):
        self.items = items
        self.i = 0

    def get(self):
        b = self.items[self.i]
        self.i = (self.i + 1) % len(self.items)
        return b


class K:
    def __init__(self, mk):
        self.mk = mk

    def act(self, out, in_, func, bias=None, scale=None, accum=None):
        rd = [in_.buf]; wr = [out.buf]
        kw = {}
        if bias is not None:
            if isinstance(bias, Ref):
                rd.append(bias.buf); kw["bias"] = bias.ap
            else:
                kw["bias"] = float(bias)
        if scale is not None:
            if isinstance(scale, Ref):
                rd.append(scale.buf); kw["scale"] = scale.ap
            else:
                kw["scale"] = float(scale)
        if accum is not None:
            wr.append(accum.buf); kw["accum_out"] = accum.ap
        self.mk.op("act", lambda e: e.activation(out=out.ap, in_=in_.ap, func=func, **kw), reads=rd, writes=wr)

    def tt(self, eng, out, a, b, op):
        self.mk.op(eng, lambda e: e.tensor_tensor(out=out.ap, in0=a.ap, in1=b.ap, op=op), reads=[a.buf, b.buf], writes=[out.buf])

    def ts(self, eng, out, a, s1, op0, s2=None, op1=None):
        rd = [a.buf]
        v1 = s1
        if isinstance(s1, Ref):
            rd.append(s1.buf); v1 = s1.ap
        v2 = s2
        if isinstance(s2, Ref):
            rd.append(s2.buf); v2 = s2.ap
        if op1 is None:
            self.mk.op(eng, lambda e: e.tensor_scalar(out=out.ap, in0=a.ap, scalar1=v1, scalar2=None, op0=op0), reads=rd, writes=[out.buf])
        else:
            self.mk.op(eng, lambda e: e.tensor_scalar(out=out.ap, in0=a.ap, scalar1=v1, scalar2=v2, op0=op0, op1=op1), reads=rd, writes=[out.buf])

    def stt(self, out, a, scalar, b, op0, op1):
        rd = [a.buf, b.buf]
        sv = scalar
        if isinstance(scalar, Ref):
            rd.append(scalar.buf); sv = scalar.ap
        self.mk.op("dve", lambda e: e.scalar_tensor_tensor(out=out.ap, in0=a.ap, scalar=sv, in1=b.ap, op0=op0, op1=op1), reads=rd, writes=[out.buf])

    def red(self, out, in_, op=ALU.add):
        self.mk.op("dve", lambda e: e.tensor_reduce(out=out.ap, in_=in_.ap, axis=AX.X, op=op), reads=[in_.buf], writes=[out.buf])

    def cp(self, eng, out, in_):
        if eng == "act":
            self.mk.op("act", lambda e: e.copy(out=out.ap, in_=in_.ap), reads=[in_.buf], writes=[out.buf])
        else:
            self.mk.op(eng, lambda e: e.tensor_copy(out=out.ap, in_=in_.ap), reads=[in_.buf], writes=[out.buf])

    def memset(self, eng, out, val):
        self.mk.op(eng, lambda e: e.memset(out.ap, val), writes=[out.buf])

    def recip(self, out, in_):
        self.mk.op("dve", lambda e: e.reciprocal(out=out.ap, in_=in_.ap), reads=[in_.buf], writes=[out.buf])

    def mm(self, out, lhsT, rhs, start=True, stop=True):
        rd = [lhsT.buf, rhs.buf]
        wr = [out.buf]
        if not start:
            rd.append(out.buf)
        self.mk.op("pe", lambda e: e.matmul(out.ap, lhsT=lhsT.ap, rhs=rhs.ap, start=start, stop=stop), reads=rd, writes=wr)

    def tr(self, out, in_, ident):
        self.mk.op("pe", lambda e: e.transpose(out.ap, in_.ap, ident.ap), reads=[in_.buf, ident.buf], writes=[out.buf])

    def dma(self, q, out, in_, **kw):
        return self.mk.dma(q, out.ap, in_.ap, reads=[in_.buf], writes=[out.buf], **kw)

    def asel(self, out, in_, pattern, cmp, fill, base, cm):
        self.mk.op("pool", lambda e: e.affine_select(out=out.ap, in_=in_.ap, pattern=pattern, compare_op=cmp, fill=fill, base=base, channel_multiplier=cm), reads=[in_.buf], writes=[out.buf])

    def sigmoid(self, out, in_, tmp, sign=1.0):
        self.act(tmp, in_, AF.Exp, scale=-sign)
        self.act(tmp, tmp, AF.Ln, bias=1.0)
        self.act(out, tmp, AF.Exp, scale=-1.0)

    def rsqrt_ln(self, out, in_, scale, eps):
        self.act(out, in_, AF.Ln, bias=eps, scale=scale)
        self.act(out, out, AF.Exp, scale=-0.5)


def build(depth=DEPTH, phases=("gdn", "rwkv", "hgrn", "ffn"), dbg=False):
    nc = bass.Bass("TRN2", target_bir_lowering=False)
    mk = MK(nc)
    k = K(mk)

    def din(name, shape):
        return Buf(nc.dram_tensor(name, list(shape), F32, kind="ExternalInput").ap(), name)

    def dout(name, shape):
        return Buf(nc.dram_tensor(name, list(shape), F32, kind="ExternalOutput").ap(), name)

    def dscr(name, shape):
        return Buf(nc.dram_tensor(name, list(shape), F32, kind="Internal").ap(), name)

    L = DEPTH
    I = {}
    for name, shape in [
        ("xin", (NT, D)), ("st_gdn", (L, NS, HA, HD, HD)), ("st_conv", (L, NS, 3, 1152)), ("st_rwkv", (L, NS, HB, HD, HD)),
        ("st_shift", (L, NS, B_COLS)), ("st_hgrn", (L, NS, HC, HD, HD)),
        ("norm_mix", (L, D)), ("w_in", (L, D, IN_COLS)), ("w_out", (L, D, D)), ("gdn_conv_w", (L, 4, 1152)),
        ("gdn_a_log", (L, HA)), ("gdn_dt_bias", (L, HA)), ("gdn_norm_w", (L, HD)), ("rwkv_mu", (L, B_COLS)),
        ("rwkv_w0", (L, 384)), ("rwkv_w2", (L, 64, 384)), ("rwkv_a0", (L, 384)), ("rwkv_a2", (L, 64, 384)),
        ("rwkv_g2", (L, 128, 384)), ("rwkv_v0", (L - 1, 384)), ("rwkv_v1", (L - 1, 384, 32)), ("rwkv_v2", (L - 1, 32, 384)),
        ("rwkv_k_k", (L, 384)), ("rwkv_k_a", (L, 384)), ("rwkv_r_k", (L, 384)), ("rwkv_ln_w", (L, 384)), ("rwkv_ln_b", (L, 384)),
        ("hgrn_lower_bounds", (L, 256)), ("hgrn_norm_w", (L, HD)), ("norm_ffn", (L, D)),
        ("w_gate", (L, D, DFF)), ("w_up", (L, D, DFF)), ("w_down", (L, DFF, D)), ("norm_final", (1, D)),
    ]:
        I[name] = din(name, shape)
    O = {}
    for name, shape in [
        ("y", (NT, D)), ("gdn_p", (L, HA, HD, HD)), ("gdn_s", (L, NS, HA, HD, HD)), ("conv_p", (L, 3, 1152)), ("conv_s", (L, NS, 3, 1152)),
        ("rwkv_p", (L, HB, HD, HD)), ("rwkv_s", (L, NS, HB, HD, HD)), ("shift_p", (L, B_COLS)), ("shift_s", (L, NS, B_COLS)),
        ("hgrn_p", (L, HC, HD, HD)), ("hgrn_s", (L, NS, HC, HD, HD)),
    ]:
        O[name] = dout(name, shape)
    if dbg:
        O["dbg_ya"] = dout("dbg_ya", (NT, 384))
        O["dbg_yb"] = dout("dbg_yb", (NT, 384))
        O["dbg_yc"] = dout("dbg_yc", (NT, 256))
        O["dbg_x1"] = dout("dbg_x1", (NT, D))
    xres = nc.dram_tensor("xres", [NT, D], F32, kind="Internal").ap()
    tiles = [(i * 128, 128) for i in range(NCH)] + [(T, NS)]
    xb = [Buf(xres[t0:t0 + n, :], f"xres{i}") for i, (t0, n) in enumerate(tiles)]
    vfirst = nc.dram_tensor("vfirst", [NT, 384], F32, kind="Internal").ap()
    vfb = [Buf(vfirst[t0:t0 + n, :], f"vf{i}") for i, (t0, n) in enumerate(tiles)]
    bounce = dscr("bounce", (NS, 6 * 200))
    bounceB = dscr("bounceB", (NS, 6 * 392))
    bounceC = dscr("bounceC", (NS, 4 * 256))
    bounce2 = dscr("bounce2", (NS * 6, 64))
    bounce2C = dscr("bounce2C", (NS * 4, 64))

    arena = Arena(mk, 206 * 256)
    banks = [mk.ps([128, 512], F32, name=f"bank{i}") for i in range(8)]

    def pview(b, c0, c1, p=128, name=""):
        return Buf(banks[b].t[0:p, c0:c1], name)

    identF = arena.f32([128, 128]); identB = arena.bf16([128, 128])
    Utri = arena.f32([128, 128]); ones = arena.f32([128, 128])
    negmaskT = arena.f32([128, 128]); posmaskS = arena.f32([128, 128])
    smT = arena.f32([128, 128]); sm = arena.f32([128, 128]); cmT = arena.f32([128, 128])
    sel = arena.f32([6, 6, 128])
    k.memset("pool", ones.a, 1.0)
    k.asel(identF.a, ones.a, [[-1, 128]], ALU.is_equal, 0.0, 0, 1)
    k.cp("pool", identB.a, identF.a)
    k.asel(Utri.a, ones.a, [[1, 128]], ALU.is_ge, 0.0, 0, -1)
    k.memset("pool", negmaskT.a, 0.0)
    k.asel(negmaskT.a, negmaskT.a, [[1, 128]], ALU.is_ge, -30000.0, 0, -1)
    k.memset("pool", posmaskS.a, 0.0)
    k.asel(posmaskS.a, posmaskS.a, [[-1, 128]], ALU.is_gt, 30000.0, 0, 1)
    k.asel(smT.a, ones.a, [[1, 128]], ALU.is_gt, 0.0, 0, -1)
    k.asel(sm.a, ones.a, [[-1, 128]], ALU.is_gt, 0.0, 0, 1)
    k.asel(cmT.a, ones.a, [[1, 128]], ALU.is_ge, 0.0, 0, -1)
    k.memset("pool", sel.a, 1.0)
    k.asel(sel.a, sel.a, [[-1, 6], [0, 128]], ALU.is_equal, 0.0, 0, 1)

    hT = arena.bf16([128, 8, HOFF + NT], "hT")
    k.memset("pool", hT[:, :, 0:HOFF], 0.0)

    for i, (t0, n) in enumerate(tiles):
        mk.dma("sp", xres[t0:t0 + n, :], I["xin"].t[t0:t0 + n, :], reads=[I["xin"]], writes=[xb[i]])

    persist_mark = arena.off

    def norm_phase(wname, l):
        m0 = arena.off
        wbc = arena.f32([128, D])
        mk.dma("sp", wbc.t, I[wname].t[l:l + 1, :].to_broadcast([128, D]), reads=[I[wname]], writes=[wbc])
        ss = arena.f32([128, 32])
        k.memset("pool", ss.a, 0.0)
        xts = [arena.f32([128, D]) for _ in range(len(tiles))]
        junk = arena.bf16([128, D])
        hb = [arena.bf16([128, D]) for _ in range(2)]
        trp = [Ref(banks[j], banks[j].t[:, :].bitcast(BF16)) for j in range(8)]
        for i, (t0, n) in enumerate(tiles):
            mk.dma("sp" if i % 2 == 0 else "act", xts[i].t[0:n, :], xres[t0:t0 + n, :], reads=[xb[i]], writes=[xts[i]])
        for i, (t0, n) in enumerate(tiles):
            k.act(junk[0:n, :], xts[i][0:n, :], AF.Square, accum=ss[0:n, i:i + 1])
        k.rsqrt_ln(ss.a, ss.a, 1.0 / D, 1e-6)
        for i, (t0, n) in enumerate(tiles):
            xt = xts[i]
            h = hb[i % 2]
            k.stt(h[0:n, :], xt[0:n, :], ss[0:n, i:i + 1], wbc[0:n, :], ALU.mult, ALU.mult)
            tp = trp[i % 8]
            for c in range(8):
                k.tr(tp[:, c * 128:c * 128 + n], h[0:n, c * 128:(c + 1) * 128], identB[0:n, 0:n])
            src = tp.a.rr("p (c t) -> p c t", c=8)[:, :, 0:n]
            eng = "act" if i % 2 == 0 else "dve"
            k.cp(eng, hT[:, :, HOFF + t0:HOFF + t0 + n], src)
        arena.off = m0
        mk.barrier()

    S = dict(arena=arena, banks=banks, pview=pview, I=I, O=O, xres=xres, xb=xb, tiles=tiles, hT=hT, vfirst=vfirst, vfb=vfb,
             identF=identF, identB=identB, Utri=Utri, ones=ones, negmaskT=negmaskT, posmaskS=posmaskS, smT=smT, sm=sm, cmT=cmT,
             sel=sel, bounce=bounce, bounce2=bounce2, bounceB=bounceB, bounceC=bounceC, bounce2C=bounce2C, dbg=dbg, mk=mk, k=k)

    for l in range(depth):
        norm_phase("norm_mix", l)
        if "gdn" in phases:
            gdn_phase(S, l)
        if "rwkv" in phases:
            rwkv_phase(S, l)
        if "hgrn" in phases:
            hgrn_phase(S, l)
        if dbg and l == 0:
            for i, (t0, n) in enumerate(tiles):
                mk.dma("sp", O["dbg_x1"].t[t0:t0 + n, :], xres[t0:t0 + n, :], reads=[xb[i]], writes=[O["dbg_x1"]])
        if "ffn" in phases:
            ffn_phase(S, l, norm_phase)
    final_phase(S)
    mk.emit(mk.all_dma_events())
    return nc


def load_w(S, dst, src_ap, src_buf):
    S["mk"].dma("pool", dst.t, src_ap.rearrange("(c p) n -> p c n", p=128), reads=[src_buf], writes=[dst])


def tri_inverse(S, X, XT, sbp, psp, dest):
    k = S["k"]
    TT = sbp.get()
    k.tt("pool", TT.a, XT.a, S["identF"].a, ALU.add)
    P, PT = X, XT
    for st in range(1, 7):
        psP = psp.get()
        k.mm(psP.a, PT.a, P.a)
        if st < 6:
            psPT = psp.get()
            k.mm(psPT.a, P.a, PT.a)
        Pn = sbp.get()
        k.cp("act", Pn.a, psP.a)
        if st < 6:
            PTn = sbp.get()
            k.cp("dve", PTn.a, psPT.a)
        psT = psp.get()
        k.mm(psT.a, Pn.a, TT.a)
        TTn = sbp.get() if st < 6 else dest
        k.tt("dve", TTn.a, TT.a, psT.a, ALU.add)
        TT = TTn
        P = Pn
        if st < 6:
            PT = PTn
    return TT


def out_proj(S, yb, n, Wo, ne, ti, dbgname=None, dbgw=0):
    k = S["k"]; mk = S["mk"]; pview = S["pview"]
    t0, _ = S["tiles"][ti]
    tp = Ref(S["banks"][6], S["banks"][6].t[:, :].bitcast(BF16))
    for e in range(ne):
        k.tr(tp[:, e * 128:e * 128 + n], yb[0:n, e * 128:(e + 1) * 128], S["identB"][0:n, 0:n])
    yT = S["yT"]
    k.cp("act", yT[:, 0:ne, 0:n], tp.a.rr("p (c t) -> p c t", c=8)[:, 0:ne, 0:n])
    xt = S["xt_pool"].get()
    mk.dma("sp", xt.t[0:n, :], S["xres"][t0:t0 + n, :], reads=[S["xb"][ti]], writes=[xt])
    for dh in range(2):
        ps = S["banks"][1 + dh].a
        for e in range(ne):
            k.mm(ps[0:n, :], yT[:, e, 0:n], Wo[:, e, dh * 512:(dh + 1) * 512], start=(e == 0), stop=(e == ne - 1))
        k.tt("dve", xt[0:n, dh * 512:(dh + 1) * 512], xt[0:n, dh * 512:(dh + 1) * 512], ps[0:n, :], ALU.add)
    mk.dma("sp", S["xres"][t0:t0 + n, :], xt.t[0:n, :], reads=[xt], writes=[S["xb"][ti]])


def interleave(gB, gA):
    while gB is not None or gA is not None:
        if gB is not None:
            try:
                next(gB)
            except StopIteration:
                gB = None
        if gA is not None:
            try:
                next(gA)
            except StopIteration:
                gA = None


def hv(T6, h):
    return T6[h // 3][:, h % 3, :]


def grp_mm(S, psp, pairs_fn, evac_fn, w=128, p=128, ngrp=2, per=3):
    k = S["k"]
    for g in range(ngrp):
        pb = psp.get(per * w, p)
        for j in range(per):
            h = g * per + j
            prs = pairs_fn(h)
            for i, (a, b) in enumerate(prs):
                k.mm(pb[:, j * w:(j + 1) * w], a, b, start=(i == 0), stop=(i == len(prs) - 1))
        evac_fn(g, pb.rr("p (h t) -> p h t", t=w))


def tri_inverse6(S, X6, XT6, pools, psp, dest6):
    k = S["k"]
    P6p, PT6p, TFp, TBp = pools
    identb = S["identF"].a.un(1).bc([128, 3, 128])
    TF = TFp.get(); TB = TBp.get()
    for g in range(2):
        k.tt("dve", TB[g].a, XT6[g].a, identb, ALU.add)
    P6, PT6 = X6, XT6
    Pn6 = P6p.get()
    grp_mm(S, psp, lambda h: [(hv(PT6, h), hv(P6, h))], lambda g, ps: k.cp("act", Pn6[g].a, ps))
    yield
    PTn6 = PT6p.get()
    grp_mm(S, psp, lambda h: [(hv(P6, h), hv(PT6, h))], lambda g, ps: k.cp("act", PTn6[g].a, ps))
    yield
    P6, PT6 = Pn6, PTn6
    for st in range(2, 7):
        TBn = TBp.get()

        def ev(g, ps, TB=TB, TBn=TBn):
            k.tt("dve", TBn[g].a, TB[g].a, ps, ALU.add)
        grp_mm(S, psp, lambda h: [(hv(P6, h), hv(TB, h))], ev)
        yield
        Pn6 = P6p.get()
        grp_mm(S, psp, lambda h: [(hv(PT6, h), hv(P6, h))], lambda g, ps: k.cp("act", Pn6[g].a, ps))
        yield
        if st < 6:
            PTn6 = PT6p.get()
            grp_mm(S, psp, lambda h: [(hv(P6, h), hv(PT6, h))], lambda g, ps: k.cp("act", PTn6[g].a, ps))
            yield
            PT6 = PTn6
        TB = TBn
        P6 = Pn6
    TBd = TBp.get()
    grp_mm(S, psp, lambda h: [(hv(P6, h), hv(TB, h))], lambda g, ps: k.tt("dve", TBd[g].a, TB[g].a, ps, ALU.add))
    yield
    Tn6 = PT6p.get(); Rt6 = P6p.get()
    for g in range(2):
        pb_ = psp.get(512)
        ptr = Ref(pb_.buf, pb_.buf.t[:, :].bitcast(BF16))
        for j in range(3):
            k.tr(ptr[:, j * 128:(j + 1) * 128], TBd[g][:, j, :], S["identB"].a)
        k.cp("act", Tn6[g].a, ptr[:, 0:384].rr("p (h t) -> p h t", t=128))

    def evr(g, ps):
        k.tt("dve", TF[g].a, ps, TBd[g].a, ALU.subtract)
        k.tt("dve", Rt6[g].a, TF[g].a, identb, ALU.add)
    grp_mm(S, psp, lambda h: [(hv(X6, h), hv(TBd, h))], evr)
    yield
    grp_mm(S, psp, lambda h: [(hv(Tn6, h), hv(Rt6, h))], lambda g, ps: k.tt("dve", dest6[g].a, TBd[g].a, ps, ALU.add))
    yield
    return


def gdn_phase(S, l):
    k = S["k"]; mk = S["mk"]; arena = S["arena"]; I = S["I"]; O = S["O"]; banks = S["banks"]; hT = S["hT"]
    m0 = arena.off
    identF = S["identF"]
    Wqkv = arena.bf16([128, 8, 1152]); Wz = arena.bf16([128, 8, 384]); Wba = arena.bf16([128, 8, 12]); Wo = arena.bf16([128, 3, 1024])
    load_w(S, Wqkv, I["w_in"].t[l, :, 0:1152], I["w_in"])
    load_w(S, Wz, I["w_in"].t[l, :, 1152:1536], I["w_in"])
    load_w(S, Wba, I["w_in"].t[l, :, 1536:1548], I["w_in"])
    load_w(S, Wo, I["w_out"].t[l, 0:384, :], I["w_out"])
    wc = arena.f32([128, 4, 1152])
    mk.dma("sp", wc.t, I["gdn_conv_w"].t[l:l + 1, :, :].to_broadcast([128, 4, 1152]), reads=[I["gdn_conv_w"]], writes=[wc])
    negA = arena.f32([128, 6]); dtb = arena.f32([128, 6]); gnw = arena.f32([128, 64])
    mk.dma("sp", negA.t, I["gdn_a_log"].t[l:l + 1, :].to_broadcast([128, 6]), reads=[I["gdn_a_log"]], writes=[negA])
    mk.dma("sp", dtb.t, I["gdn_dt_bias"].t[l:l + 1, :].to_broadcast([128, 6]), reads=[I["gdn_dt_bias"]], writes=[dtb])
    mk.dma("sp", gnw.t, I["gdn_norm_w"].t[l:l + 1, :].to_broadcast([128, 64]), reads=[I["gdn_norm_w"]], writes=[gnw])
    k.act(negA.a, negA.a, AF.Exp)
    k.ts("pool", negA.a, negA.a, -1.0, ALU.mult)
    St = arena.f32([64, 6, 64], "gdnS")
    k.memset("pool", St.a, 0.0)
    Sth = [Buf(St.t[:, h, :], f"gdnS{h}") for h in range(6)]
    S["yT"] = arena.bf16([128, 8, 128])
    S["xt_pool"] = Pool_([arena.f32([128, D]) for _ in range(2)])
    acc = arena.f32([128, 384]); tmp = arena.f32([128, 384])
    qkv = arena.f32([128, 1152]); zsA = [arena.f32([128, 384]) for _ in range(2)]; zt = arena.f32([128, 384])
    zs = zsA[0]
    sqB = arena.f32([128, 384]); rsB = arena.f32([128, 6])
    sm12 = arena.f32([128, 12]); beta = arena.f32([128, 6]); g = arena.f32([128, 6]); t6 = arena.f32([128, 6])
    gc = arena.f32([128, 6]); ngc = arena.f32([128, 6]); eg = arena.f32([128, 6]); egl = arena.f32([128, 6]); ekl = arena.f32([128, 6])
    gT = arena.f32([6, 128])
    sq = arena.f32([128, 768]); rs = arena.f32([128, 12])
    qkn = arena.f32([128, 768])
    o = arena.f32([128, 384]); yb = arena.bf16([128, 384]); y32 = arena.f32([128, 384])
    convo = arena.f32([128, 1152])
    mchunk = arena.off
    bk = arena.bf16([128, 384]); qd = arena.bf16([128, 384])
    bvA = [arena.bf16([128, 384]) for _ in range(2)]; bkeA = [arena.bf16([128, 384]) for _ in range(2)]; kdA = [arena.bf16([128, 384]) for _ in range(2)]
    knT = arena.bf16([64, 6, 128]); bkT = arena.bf16([64, 6, 128]); qnT = arena.bf16([64, 6, 128])
    qdTA = [arena.bf16([64, 6, 128]) for _ in range(2)]
    eglA = [arena.f32([128, 6]) for _ in range(2)]
    def mk6(dt=F32):
        return [(arena.f32 if dt == F32 else arena.bf16)([128, 3, 128]) for _ in range(2)]
    DTc6 = mk6(); Ds6 = mk6(); DTs6 = mk6(); AT6A = [mk6(BF16) for _ in range(2)]; TTd6 = mk6(BF16)
    XT6A = [mk6(BF16) for _ in range(2)]
    X6p = Pool_([mk6(BF16) for _ in range(2)]); XT6p = Pool_([mk6(BF16) for _ in range(3)])
    TFp = Pool_([mk6() for _ in range(3)]); TBp = Pool_([mk6(BF16) for _ in range(3)])
    X60A = [mk6(BF16) for _ in range(2)]
    nwT6 = arena.bf16([64, 6, 128]); vn6 = arena.bf16([128, 384])
    Stb = arena.bf16([64, 6, 64]); k.memset("pool", Stb.a, 0.0)
    qknb = arena.bf16([128, 768])
    SA = [arena.bf16([128, 128]) for _ in range(3)]; SB = [arena.bf16([128, 128]) for _ in range(3)]
    onesb = arena.bf16([128, 128]); k.cp("pool", onesb.a, S["ones"].a)
    for s_ in (1, 2, 3):
        k.asel(SA[s_ - 1].a, onesb.a, [[1, 128]], ALU.is_equal, 0.0, -s_, -1)
        k.asel(SB[s_ - 1].a, onesb.a, [[1, 128]], ALU.is_equal, 0.0, 128 - s_, -1)
    pbA = [arena.bf16([128, 1152]) for _ in range(2)]
    k.memset("pool", pbA[1].a, 0.0)
    pb_par = [0]
    negmaskTb = arena.bf16([128, 128]); posmaskSb = arena.bf16([128, 128])
    k.cp("pool", negmaskTb.a, S["negmaskT"].a); k.cp("pool", posmaskSb.a, S["posmaskS"].a)
    gcp = arena.f32([128, 38]); G38 = arena.f32([38, 128]); Rm = arena.f32([38, 128]); Lm = arena.f32([38, 6, 128]); selhi = arena.f32([38, 6, 128])
    k.memset("pool", gcp.a, 0.0); k.memset("pool", Rm.a, 0.0); k.memset("pool", Lm.a, 0.0)
    k.memset("pool", Rm[32:38, :], 1.0)
    mk.dma("sp", Lm.t[0:6, :, :], S["sel"].t, reads=[S["sel"]], writes=[Lm])
    mk.dma("sp", selhi.t[32:38, :, :], S["sel"].t, reads=[S["sel"]], writes=[selhi])
    pconv = [Ref(banks[j], banks[j].t[:, 0:384]) for j in range(4)]
    bankpool = Pool_([banks[i] for i in (4, 5, 6, 7, 0, 1, 2, 3)])

    class PSP:
        def get(self, w=128, p=128):
            b = bankpool.get()
            return Ref(b, b.t[0:p, 0:w])
    psp = PSP()

    def prologue(n, col0, is_sample, zs=None):
        zs = zs if zs is not None else zsA[0]
        import os
        PSTOP = int(os.environ.get("GDN_PSTOP", "9")) if is_sample else 9
        if not is_sample:
            pcur = pbA[pb_par[0] % 2]; pprev = pbA[(pb_par[0] + 1) % 2]
            pb_par[0] += 1
            for cg in range(3):
                cs = slice(cg * 384, (cg + 1) * 384)
                bb = banks[cg]
                for c in range(8):
                    k.mm(bb[0:n, 0:384], hT[:, c, col0:col0 + n], Wqkv[:, c, cs], start=(c == 0), stop=(c == 7))
                k.cp("act", pcur[:, cs], bb[:, 0:384])
                if col0 == HOFF + T - 128:
                    k.cp("act", convo[96:128, cs], bb[96:128, 0:384])
            for cg in range(3):
                cs = slice(cg * 384, (cg + 1) * 384)
                bb = banks[cg]
                for j in range(3):
                    sh_ = 3 - j
                    k.mm(banks[3 + j][:, 0:384], SA[sh_ - 1].a, pcur[:, cs], start=True, stop=False)
                    k.mm(banks[3 + j][:, 0:384], SB[sh_ - 1].a, pprev[:, cs], start=False, stop=True)
                k.tt("dve", acc[0:n, :], bb[0:n, 0:384], wc[0:n, 3, cs], ALU.mult)
                for j in range(3):
                    k.tt("dve", tmp[0:n, :], banks[3 + j][0:n, 0:384], wc[0:n, j, cs], ALU.mult)
                    k.tt("pool", acc[0:n, :], acc[0:n, :], tmp[0:n, :], ALU.add)
                k.sigmoid(tmp[0:n, :], acc[0:n, :], tmp[0:n, :])
                k.tt("pool", qkv[0:n, cs], acc[0:n, :], tmp[0:n, :], ALU.mult)
            yield
        for cg in range(3):
            if PSTOP <= 0 or not is_sample:
                break
            cs = slice(cg * 384, (cg + 1) * 384)
            for c in range(8):
                k.mm(pconv[3][0:n, :], hT[:, c, col0:col0 + n], Wqkv[:, c, cs], start=(c == 0), stop=(c == 7))
            k.cp("act", convo[0:n, cs], pconv[3][0:n, :])
            k.tt("dve", acc[0:n, :], pconv[3][0:n, :], wc[0:n, 3, cs], ALU.mult)
            for j in range(3):
                k.tt("pool", tmp[0:n, :], S["cbuf"][0:n, j, cs], wc[0:n, j, cs], ALU.mult)
                k.tt("pool", acc[0:n, :], acc[0:n, :], tmp[0:n, :], ALU.add)
            k.sigmoid(tmp[0:n, :], acc[0:n, :], tmp[0:n, :])
            k.tt("pool", qkv[0:n, cs], acc[0:n, :], tmp[0:n, :], ALU.mult)
            yield
        if PSTOP <= 1:
            return
        for c in range(8):
            k.mm(pconv[0][0:n, :], hT[:, c, col0:col0 + n], Wz[:, c, :], start=(c == 0), stop=(c == 7))
        k.cp("act", zt[0:n, :], pconv[0][0:n, :])
        k.sigmoid(zs[0:n, :], zt[0:n, :], zs[0:n, :])
        k.tt("pool", zs[0:n, :], zs[0:n, :], zt[0:n, :], ALU.mult)
        yield
        if PSTOP <= 2:
            return
        pba = psp.get(12)
        for c in range(8):
            k.mm(pba[0:n, :], hT[:, c, col0:col0 + n], Wba[:, c, :], start=(c == 0), stop=(c == 7))
        k.cp("dve", sm12[0:n, :], pba[0:n, :])
        k.sigmoid(beta[0:n, :], sm12[0:n, 0:6], t6[0:n, :])
        k.tt("dve", t6[0:n, :], sm12[0:n, 6:12], dtb[0:n, :], ALU.add)
        k.act(t6[0:n, :], t6[0:n, :], AF.Exp)
        k.act(t6[0:n, :], t6[0:n, :], AF.Ln, bias=1.0)
        k.tt("dve", g[0:n, :], t6[0:n, :], negA[0:n, :], ALU.mult)
        yield
        if PSTOP <= 3:
            return
        k.tt("pool", sq[0:n, :], qkv[0:n, 0:768], qkv[0:n, 0:768], ALU.mult)
        k.red(rs[0:n, :], sq[0:n, :].rr("p (h d) -> p h d", d=64))
        k.rsqrt_ln(rs[0:n, :], rs[0:n, :], 1.0, 1e-6)
        k.ts("dve", rs[0:n, 0:6], rs[0:n, 0:6], 0.125, ALU.mult)
        k.tt("dve", qkn[0:n, :].rr("p (h d) -> p h d", d=64), qkv[0:n, 0:768].rr("p (h d) -> p h d", d=64),
             rs[0:n, :].un(2).bc([n, 12, 64]), ALU.mult)
        yield

    def h3(ref, n):
        return ref[0:n, :].rr("p (h d) -> p h d", d=64)

    def b3(ref, n):
        return ref[0:n, :].un(2).bc([n, 6, 64])

    def epilogue(n, ti, row0, zs=None):
        zs = zs if zs is not None else zsA[0]
        k.tt("dve", sqB[0:n, 0:384], o[0:n, :], o[0:n, :], ALU.mult)
        k.red(rsB[0:n, 0:6], sqB[0:n, 0:384].rr("p (h d) -> p h d", d=64))
        k.rsqrt_ln(rsB[0:n, 0:6], rsB[0:n, 0:6], 1.0 / 64, 1e-6)
        k.tt("dve", h3(y32, n), h3(o, n), b3(rsB[:, 0:6], n), ALU.mult)
        k.tt("dve", h3(y32, n), h3(y32, n), gnw[0:n, :].un(1).bc([n, 6, 64]), ALU.mult)
        k.tt("dve", yb[0:n, :], y32[0:n, :], zs[0:n, :], ALU.mult)
        if S["dbg"] and l == 0:
            k.tt("pool", y32[0:n, :], y32[0:n, :], zs[0:n, :], ALU.mult)
            mk.dma("sp", O["dbg_ya"].t[row0:row0 + n, :], y32.t[0:n, :], reads=[y32], writes=[O["dbg_ya"]])
        out_proj(S, yb, n, Wo, 3, ti)

    def chunkA(ci):
        n = 128
        par = ci % 2
        zs_ = zsA[par]; bv = bvA[par]; bke = bkeA[par]; kd = kdA[par]; qdT = qdTA[par]; egl = eglA[par]
        X6 = X60A[par]; XT6 = XT6A[par]; AT6 = AT6A[par]
        yield from prologue(n, HOFF + ci * 128, False, zs_)
        pg = psp.get(12)
        k.mm(pg[:, 0:6], S["Utri"].a, g.a)
        k.mm(pg[:, 6:12], S["ones"].a, g.a)
        k.cp("dve", gc.a, pg[:, 0:6])
        k.ts("dve", ngc.a, gc.a, -1.0, ALU.mult)
        k.act(eg.a, pg[:, 0:6], AF.Exp)
        k.act(egl.a, pg[:, 6:12], AF.Exp)
        k.tt("dve", ekl.a, pg[:, 6:12], gc.a, ALU.subtract)
        k.act(ekl.a, ekl.a, AF.Exp)
        k.cp("act", gcp[:, 0:6], gc.a)
        k.cp("act", gcp[:, 32:38], ngc.a)
        pgT = psp.get(128, 38)
        k.tr(pgT.a, gcp.a, identF.a)
        k.cp("act", G38.a, pgT.a)
        k.cp("act", Rm[0:6, :], pgT[0:6, :])
        yield
        kn = qkn[:, 384:768]; qn = qkn[:, 0:384]; v = qkv[:, 768:1152]
        k.cp("act", qknb.a, qkn.a)
        k.tt("dve", h3(bk, n), kn.rr("p (h d) -> p h d", d=64), b3(beta, n), ALU.mult)
        k.tt("dve", h3(qd, n), qn.rr("p (h d) -> p h d", d=64), b3(eg, n), ALU.mult)
        k.tt("pool", h3(bv, n), v.rr("p (h d) -> p h d", d=64), b3(beta, n), ALU.mult)
        k.tt("dve", h3(bke, n), h3(bk, n), b3(eg, n), ALU.mult)
        k.tt("dve", h3(kd, n), kn.rr("p (h d) -> p h d", d=64), b3(ekl, n), ALU.mult)
        yield
        for (src, dst, eng) in ((qknb[:, 384:768], knT, "act"), (bk.a, bkT, "dve"), (qknb[:, 0:384], qnT, "act"), (qd.a, qdT, "dve")):
            pb_ = bankpool.get()
            ptr = Ref(pb_, pb_.t[:, :].bitcast(BF16))
            for h in range(6):
                k.tr(ptr[0:64, h * 128:(h + 1) * 128], src[:, h * 64:(h + 1) * 64], S["identB"].a)
            k.cp(eng, dst.a, ptr[0:64, 0:768].rr("p (h t) -> p h t", t=128))
            yield
        k.tt("pool", Lm[32:38, :, :], G38[32:38, :].un(1).bc([6, 6, 128]), selhi[32:38, :, :], ALU.mult)
        grp_mm(S, psp, lambda h: [(Lm[:, h, :], Rm.a), (S["identB"].a, negmaskTb.a)],
               lambda g, ps: k.act(DTc6[g].a, ps, AF.Exp))
        yield
        grp_mm(S, psp, lambda h: [(Lm[:, h, :], Rm.a), (S["identB"].a, posmaskSb.a)],
               lambda g, ps: k.act(Ds6[g].a, ps, AF.Exp, scale=-1.0))
        for gi_ in range(2):
            k.tt("dve", DTs6[gi_].a, DTc6[gi_].a, S["smT"].a.un(1).bc([128, 3, 128]), ALU.mult)
        yield
        grp_mm(S, psp, lambda h: [(bkT[:, h, :], knT[:, h, :])],
               lambda g, ps: k.stt(X6[g].a, ps, -1.0, Ds6[g].a, ALU.mult, ALU.mult))
        yield
        grp_mm(S, psp, lambda h: [(knT[:, h, :], bkT[:, h, :])],
               lambda g, ps: k.stt(XT6[g].a, ps, -1.0, DTs6[g].a, ALU.mult, ALU.mult))
        yield
        grp_mm(S, psp, lambda h: [(knT[:, h, :], qnT[:, h, :])],
               lambda g, ps: k.tt("dve", AT6[g].a, ps, DTc6[g].a, ALU.mult))
        yield

    def chunkB(ci):
        n = 128
        par = ci % 2
        zs_ = zsA[par]; bv = bvA[par]; bke = bkeA[par]; kd = kdA[par]; qdT = qdTA[par]; egl = eglA[par]
        X6 = X60A[par]; XT6 = XT6A[par]; AT6 = AT6A[par]
        yield from tri_inverse6(S, X6, XT6, (X6p, XT6p, TFp, TBp), psp, TTd6)
        TT6 = TTd6
        grp_mm(S, psp, lambda h: [(bke[:, h * 64:(h + 1) * 64], hv(TT6, h))],
               lambda g, ps: k.act(nwT6[:, g * 3:(g + 1) * 3, :], ps, AF.Copy, scale=-1.0), p=64)
        yield
        grp_mm(S, psp, lambda h: [(nwT6[:, h, :], Stb[:, h, :]), (hv(TT6, h), bv[:, h * 64:(h + 1) * 64])],
               lambda g, ps: k.cp("act", vn6.a.rr("p (h d) -> p h d", d=64), ps), w=64, ngrp=1, per=6)
        yield
        grp_mm(S, psp, lambda h: [(qdT[:, h, :], Stb[:, h, :]), (hv(AT6, h), vn6[:, h * 64:(h + 1) * 64])],
               lambda g, ps: k.cp("act", o.a.rr("p (h d) -> p h d", d=64), ps), w=64, ngrp=1, per=6)
        yield

        def st_upd(g, ps):
            k.tt("dve", St.a, St.a, egl[0:64, :].un(2).bc([64, 6, 64]), ALU.mult)
            k.tt("dve", St.a, St.a, ps, ALU.add)
            k.cp("act", Stb.a, St.a)
        grp_mm(S, psp, lambda h: [(kd[:, h * 64:(h + 1) * 64], vn6[:, h * 64:(h + 1) * 64])], st_upd, w=64, p=64, ngrp=1, per=6)
        yield
        epilogue(n, ci, ci * 128, zs_)
        yield

    import os
    STOP = 9
    nch = int(os.environ.get("GDN_NCH", NCH))
    interleave(None, chunkA(0))
    for ci in range(nch):
        interleave(chunkB(ci), chunkA(ci + 1) if ci + 1 < nch else None)
    mk.dma("sp", O["gdn_p"].t[l].rearrange("h k v -> k h v"), St.t, reads=[St], writes=[O["gdn_p"]])
    mk.dma("sp", O["conv_p"].t[l], convo.t[125:128, :], reads=[convo], writes=[O["conv_p"]])
    if STOP <= 5:
        arena.off = m0
        mk.barrier()
        return
    mk.barrier()
    arena.off = mchunk
    gdn_sample(S, l, locals())
    arena.off = m0
    mk.barrier()


def gdn_sample(S, l, loc):
    k = S["k"]; mk = S["mk"]; arena = S["arena"]; I = S["I"]; O = S["O"]
    prologue = loc["prologue"]; epilogue = loc["epilogue"]
    qkn = loc["qkn"]; qkv = loc["qkv"]; beta = loc["beta"]; g = loc["g"]; eg = loc["eg"]; o = loc["o"]; convo = loc["convo"]
    n = NS
    P = NS * 6
    Ss = arena.f32([P, 64, 64])
    mk.dma("act", Ss.t, I["st_gdn"].t[l].rearrange("s h k v -> (s h) k v"), reads=[I["st_gdn"]], writes=[Ss])
    cbuf = arena.f32([NS, 3, 1152])
    S["cbuf"] = cbuf
    import os
    SUB = os.environ.get("GDN_SUB", "")
    mk.dma("sp", cbuf.t, I["st_conv"].t[l], reads=[I["st_conv"]], writes=[cbuf])
    if SUB != "a":
        for _ in prologue(n, HOFF + T, True):
            pass
    if SUB != "b":
        mk.dma("sp", O["conv_s"].t[l, :, 0:2, :], I["st_conv"].t[l, :, 1:3, :], reads=[I["st_conv"]], writes=[O["conv_s"]])
        mk.dma("sp", O["conv_s"].t[l, :, 2, :], convo.t[0:n, :], reads=[convo], writes=[O["conv_s"]])
    import os
    STOP = int(os.environ.get("GDN_STOP", "9"))
    if STOP <= 6:
        return
    k.act(eg[0:n, :], g[0:n, :], AF.Exp)
    tm = arena.f32([NS, 6, 200])
    k.cp("dve", tm[:, :, 0:64], qkn[0:n, 0:384].rr("p (h d) -> p h d", d=64))
    k.cp("pool", tm[:, :, 64:128], qkn[0:n, 384:768].rr("p (h d) -> p h d", d=64))
    k.cp("dve", tm[:, :, 128:192], qkv[0:n, 768:1152].rr("p (h d) -> p h d", d=64))
    k.cp("pool", tm[:, :, 192:193], beta[0:n, :].un(2))
    k.cp("pool", tm[:, :, 193:194], eg[0:n, :].un(2))
    bo = S["bounce"]
    mk.dma("sp", bo.t.rearrange("s (h c) -> s h c", h=6), tm.t, reads=[tm], writes=[bo])
    P = NS * 6
    sh = arena.f32([P, 200])
    mk.dma("sp", sh.t, bo.t.rearrange("s (h c) -> (s h) c", h=6), reads=[bo], writes=[sh])
    tmp = arena.f32([P, 64, 64])
    if STOP <= 7:
        return
    q = sh[:, 0:64]; kk = sh[:, 64:128]; v = sh[:, 128:192]; be = sh[:, 192:193]; egc = sh[:, 193:194]
    kS = arena.f32([P, 64]); vn = arena.f32([P, 64]); os_ = arena.f32([P, 64])
    k.tt("dve", tmp.a, Ss.a, kk.un(2).bc([P, 64, 64]), ALU.mult)
    k.red(kS.a, tmp.a.rr("p k v -> p v k"))
    k.ts("dve", kS.a, kS.a, egc, ALU.mult)
    k.tt("dve", vn.a, v, kS.a, ALU.subtract)
    k.ts("dve", vn.a, vn.a, be, ALU.mult)
    k.tt("pool", tmp.a, kk.un(2).bc([P, 64, 64]), vn.a.un(1).bc([P, 64, 64]), ALU.mult)
    k.stt(Ss.a, Ss.a, egc, tmp.a, ALU.mult, ALU.add)
    mk.dma("sp", O["gdn_s"].t[l].rearrange("s h k v -> (s h) k v"), Ss.t, reads=[Ss], writes=[O["gdn_s"]])
    k.tt("dve", tmp.a, Ss.a, q.un(2).bc([P, 64, 64]), ALU.mult)
    k.red(os_.a, tmp.a.rr("p k v -> p v k"))
    if STOP <= 8:
        return
    b2 = S["bounce2"]
    mk.dma("sp", b2.t, os_.t, reads=[os_], writes=[b2])
    mk.dma("sp", o.t[0:n, :], b2.t.rearrange("(s h) d -> s (h d)", h=6), reads=[b2], writes=[o])
    epilogue(n, NCH, T)


def rwkv_phase(S, l):
    k = S["k"]; mk = S["mk"]; arena = S["arena"]; I = S["I"]; O = S["O"]; banks = S["banks"]; hT = S["hT"]
    identF = S["identF"]
    m0 = arena.off
    Wb = arena.bf16([128, 8, B_COLS]); Wo = arena.bf16([128, 3, 1024])
    load_w(S, Wb, I["w_in"].t[l, :, B0:B0 + B_COLS], I["w_in"])
    load_w(S, Wo, I["w_out"].t[l, 384:768, :], I["w_out"])
    wa2 = arena.f32([128, 384]); g2 = arena.f32([128, 384])
    mk.dma("sp", wa2.t[0:64, :], I["rwkv_w2"].t[l], reads=[I["rwkv_w2"]], writes=[wa2])
    mk.dma("sp", wa2.t[64:128, :], I["rwkv_a2"].t[l], reads=[I["rwkv_a2"]], writes=[wa2])
    mk.dma("sp", g2.t, I["rwkv_g2"].t[l], reads=[I["rwkv_g2"]], writes=[g2])
    if l > 0:
        v1 = arena.f32([128, 3, 32]); v2 = arena.f32([32, 384])
        mk.dma("sp", v1.t, I["rwkv_v1"].t[l - 1].rearrange("(c p) r -> p c r", p=128), reads=[I["rwkv_v1"]], writes=[v1])
        mk.dma("sp", v2.t, I["rwkv_v2"].t[l - 1], reads=[I["rwkv_v2"]], writes=[v2])

    def bc_load(name, idx, w):
        t = arena.f32([128, w])
        mk.dma("sp", t.t, I[name].t[idx:idx + 1, :].to_broadcast([128, w]), reads=[I[name]], writes=[t])
        return t
    mu = bc_load("rwkv_mu", l, B_COLS); w0 = bc_load("rwkv_w0", l, 384); a0 = bc_load("rwkv_a0", l, 384)
    k_k = bc_load("rwkv_k_k", l, 384); k_a = bc_load("rwkv_k_a", l, 384); r_k = bc_load("rwkv_r_k", l, 384)
    ln_w = bc_load("rwkv_ln_w", l, 384); ln_b = bc_load("rwkv_ln_b", l, 384)
    v0 = bc_load("rwkv_v0", l - 1, 384) if l > 0 else None
    Um = arena.f32([128, 128]); ind2 = arena.f32([128, 2])
    k.memset("pool", ind2.a, 1.0)
    k.asel(ind2[:, 0:1], ind2[:, 0:1], [[0, 1]], ALU.is_gt, 0.0, 64, -1)
    k.asel(ind2[:, 1:2], ind2[:, 1:2], [[0, 1]], ALU.is_ge, 0.0, -64, 1)
    k.tt("pool", Um.a, S["Utri"].a, ind2[:, 0:1].bc([128, 128]), ALU.subtract)
    St = arena.f32([64, 6, 64]); k.memset("pool", St.a, 0.0)
    Sth = [Buf(St.t[:, h, :], f"rwS{h}") for h in range(6)]
    S["yT"] = arena.bf16([128, 8, 128])
    S["xt_pool"] = Pool_([arena.f32([128, D]) for _ in range(2)])
    pS = arena.f32([128, B_COLS]); xs = arena.f32([128, B_COLS]); dd = arena.f32([128, B_COLS])
    loraT = arena.f32([128, 2, 128]); ltmp = arena.f32([128, 2, 128])
    t384 = arena.f32([128, 384]); t2 = arena.f32([128, 384]); logw = arena.f32([128, 384]); aa = arena.f32([128, 384]); gg = arena.f32([128, 384])
    vv = arena.f32([128, 384]); kk = arena.f32([128, 384]); k2 = arena.f32([128, 384]); ka = arena.f32([128, 384])
    s6 = arena.f32([128, 6]); s6b = arena.f32([128, 6]); bs = arena.f32([128, 6]); rs = arena.f32([128, 6])
    vT = arena.f32([128, 3, 128]); vl = arena.f32([32, 128])
    Y = arena.f32([128, 384]); yb = arena.bf16([128, 384])
    mchunk = arena.off
    bh = arena.bf16([128, 384]); ah = arena.bf16([128, 384]); kh = arena.bf16([128, 384]); rh = arena.bf16([128, 384])
    E1 = arena.f32([128, 384]); vvb = arena.bf16([128, 384])
    bhT = arena.bf16([64, 6, 128]); ahT = arena.bf16([64, 6, 128]); khT = arena.bf16([64, 6, 128]); rhT = arena.bf16([64, 6, 128])
    elc = arena.f32([64, 12])

    def mk6(dt=F32):
        return [(arena.f32 if dt == F32 else arena.bf16)([128, 3, 128]) for _ in range(2)]
    ABKT6 = mk6(BF16); ARAT6 = mk6(BF16); ARKT6 = mk6(BF16); TTd6 = mk6(BF16); X60 = mk6(BF16)
    X6p = Pool_([mk6(BF16) for _ in range(2)]); XT6p = Pool_([mk6(BF16) for _ in range(3)])
    TFp = Pool_([mk6() for _ in range(3)]); TBp = Pool_([mk6(BF16) for _ in range(3)])
    S0pf = arena.f32([64, 6, 64]); S0p = arena.bf16([64, 6, 64]); S0pp = arena.f32([64, 6, 64]); Stmp = arena.f32([64, 6, 64])
    ru6 = arena.bf16([128, 384]); U6 = arena.bf16([128, 384])
    StT = arena.f32([64, 6, 64])
    bankpool = Pool_([banks[i] for i in (6, 7, 0, 1, 2, 3, 4, 5)])

    class PSP:
        def get(self, w=128, p=128):
            b = bankpool.get()
            return Ref(b, b.t[0:p, 0:w])
    psp = PSP()
    groups = [(0, 512), (512, 512), (1024, 384)]

    def h3(ref, n):
        return ref[0:n, :].rr("p (h d) -> p h d", d=64)

    def b3(ref, n):
        return ref[0:n, :].un(2).bc([n, 6, 64])

    def prologue(n, col0, ti, prev_sb=None):
        for gi, (c0, w) in enumerate(groups):
            pc = banks[gi]
            for c in range(8):
                k.mm(pc[0:n, 0:w], hT[:, c, col0:col0 + n], Wb[:, c, c0:c0 + w], start=(c == 0), stop=(c == 7))
            k.cp("act", pS[0:n, c0:c0 + w], pc[0:n, 0:w])
            if prev_sb is None:
                pp = banks[3 + gi]
                for c in range(8):
                    k.mm(pp[0:n, 0:w], hT[:, c, col0 - 1:col0 - 1 + n], Wb[:, c, c0:c0 + w], start=(c == 0), stop=(c == 7))
                k.tt("dve", dd[0:n, c0:c0 + w], pp[0:n, 0:w], pS[0:n, c0:c0 + w], ALU.subtract)
            else:
                k.tt("dve", dd[0:n, c0:c0 + w], prev_sb[0:n, c0:c0 + w], pS[0:n, c0:c0 + w], ALU.subtract)
            k.tt("pool", dd[0:n, c0:c0 + w], dd[0:n, c0:c0 + w], mu[0:n, c0:c0 + w], ALU.mult)
            k.tt("pool", xs[0:n, c0:c0 + w], dd[0:n, c0:c0 + w], pS[0:n, c0:c0 + w], ALU.add)
        pt = psp.get(256)
        k.tr(pt[:, 0:n], xs[0:n, 1152:1280], identF[0:n, 0:n])
        k.tr(pt[:, 128:128 + n], xs[0:n, 1280:1408], identF[0:n, 0:n])
        k.act(ltmp[0:64, 0, 0:n], pt[0:64, 0:n], AF.Exp, scale=-2.0)
        k.act(ltmp[0:64, 0, 0:n], ltmp[0:64, 0, 0:n], AF.Ln, bias=1.0)
        k.act(ltmp[0:64, 0, 0:n], ltmp[0:64, 0, 0:n], AF.Exp, scale=-1.0)
        k.ts("dve", loraT[0:64, 0, 0:n], ltmp[0:64, 0, 0:n], 2.0, ALU.mult, -1.0, ALU.add)
        k.cp("dve", loraT[64:128, 0, 0:n], pt[64:128, 0:n])
        k.act(ltmp[:, 1, 0:n], pt[:, 128:128 + n], AF.Exp, scale=-1.0)
        k.act(ltmp[:, 1, 0:n], ltmp[:, 1, 0:n], AF.Ln, bias=1.0)
        k.act(loraT[:, 1, 0:n], ltmp[:, 1, 0:n], AF.Exp, scale=-1.0)
        pw_ = psp.get(384); pa_ = psp.get(384); pg_ = psp.get(384)
        k.mm(pw_[0:n, :], loraT[0:64, 0, 0:n], wa2[0:64, :])
        k.mm(pa_[0:n, :], loraT[64:128, 0, 0:n], wa2[64:128, :])
        k.mm(pg_[0:n, :], loraT[:, 1, 0:n], g2.a)
        k.tt("dve", t384[0:n, :], pw_[0:n, :], w0[0:n, :], ALU.add)
        k.act(t384[0:n, :], t384[0:n, :], AF.Exp, scale=-1.0)
        k.act(t384[0:n, :], t384[0:n, :], AF.Ln, bias=1.0)
        k.act(t384[0:n, :], t384[0:n, :], AF.Exp, scale=-1.0, bias=-0.5)
        k.ts("dve", logw[0:n, :], t384[0:n, :], -1.0, ALU.mult)
        k.tt("dve", t2[0:n, :], pa_[0:n, :], a0[0:n, :], ALU.add)
        k.sigmoid(aa[0:n, :], t2[0:n, :], t2[0:n, :])
        k.cp("act", gg[0:n, :], pg_[0:n, :])
        r = xs[:, 0:384]; kx = xs[:, 384:768]; vx = xs[:, 768:1152]
        t0_, _ = S["tiles"][ti]
        if l == 0:
            k.cp("pool", vv[0:n, :], vx[0:n, :])
            mk.dma("sp", S["vfirst"][t0_:t0_ + n, :], vv.t[0:n, :], reads=[vv], writes=[S["vfb"][ti]])
        else:
            ptv = psp.get(384)
            for c in range(3):
                k.tr(ptv[:, c * 128:c * 128 + n], vx[0:n, c * 128:(c + 1) * 128], identF[0:n, 0:n])
            k.cp("act", vT[:, :, 0:n], ptv.a.rr("p (c t) -> p c t", c=3)[:, :, 0:n])
            pv1 = psp.get(128, 32)
            for c in range(3):
                k.mm(pv1[:, 0:n], v1[:, c, :], vT[:, c, 0:n], start=(c == 0), stop=(c == 2))
            k.cp("act", vl[:, 0:n], pv1[:, 0:n])
            pv2 = psp.get(384)
            k.mm(pv2[0:n, :], vl[:, 0:n], v2.a)
            k.tt("dve", t2[0:n, :], pv2[0:n, :], v0[0:n, :], ALU.add)
            k.sigmoid(t2[0:n, :], t2[0:n, :], t384[0:n, :])
            mk.dma("sp", vv.t[0:n, :], S["vfirst"][t0_:t0_ + n, :], reads=[S["vfb"][ti]], writes=[vv])
            k.tt("dve", vv[0:n, :], vv[0:n, :], vx[0:n, :], ALU.subtract)
            k.tt("pool", vv[0:n, :], vv[0:n, :], t2[0:n, :], ALU.mult)
            k.tt("pool", vv[0:n, :], vv[0:n, :], vx[0:n, :], ALU.add)
        k.tt("pool", kk[0:n, :], kx[0:n, :], k_k[0:n, :], ALU.mult)
        k.tt("pool", t384[0:n, :], kk[0:n, :], kk[0:n, :], ALU.mult)
        k.red(s6[0:n, :], h3(t384, n))
        k.rsqrt_ln(s6[0:n, :], s6[0:n, :], 1.0, 1e-6)
        k.tt("dve", h3(kk, n), h3(kk, n), b3(s6, n), ALU.mult)
        k.stt(t2[0:n, :], aa[0:n, :], -1.0, k_a[0:n, :], ALU.add, ALU.mult)
        k.stt(k2[0:n, :], t2[0:n, :], 1.0, kx[0:n, :], ALU.add, ALU.mult)
        k.tt("pool", t384[0:n, :], r[0:n, :], k2[0:n, :], ALU.mult)
        k.tt("pool", t384[0:n, :], t384[0:n, :], r_k[0:n, :], ALU.mult)
        k.red(bs[0:n, :], h3(t384, n))
        k.tt("pool", ka[0:n, :], kk[0:n, :], aa[0:n, :], ALU.mult)

    def epilogue(n, ti, row0):
        k.red(s6[0:n, :], h3(Y, n))
        k.tt("dve", t384[0:n, :], Y[0:n, :], Y[0:n, :], ALU.mult)
        k.red(s6b[0:n, :], h3(t384, n))
        k.ts("dve", s6[0:n, :], s6[0:n, :], 1.0 / 64, ALU.mult)
        k.tt("dve", rs[0:n, :], s6[0:n, :], s6[0:n, :], ALU.mult)
        k.stt(rs[0:n, :], s6b[0:n, :], 1.0 / 64, rs[0:n, :], ALU.mult, ALU.subtract)
        k.rsqrt_ln(rs[0:n, :], rs[0:n, :], 1.0, 64e-5)
        k.tt("dve", h3(t2, n), h3(Y, n), b3(s6, n), ALU.subtract)
        k.tt("dve", h3(t2, n), h3(t2, n), b3(rs, n), ALU.mult)
        k.tt("dve", t2[0:n, :], t2[0:n, :], ln_w[0:n, :], ALU.mult)
        k.tt("dve", t2[0:n, :], t2[0:n, :], ln_b[0:n, :], ALU.add)
        k.tt("dve", h3(t384, n), h3(vv, n), b3(bs, n), ALU.mult)
        k.tt("dve", t2[0:n, :], t2[0:n, :], t384[0:n, :], ALU.add)
        k.tt("dve", yb[0:n, :], t2[0:n, :], gg[0:n, :], ALU.mult)
        if S["dbg"] and l == 0:
            k.tt("pool", t2[0:n, :], t2[0:n, :], gg[0:n, :], ALU.mult)
            mk.dma("sp", O["dbg_yb"].t[row0:row0 + n, :], t2.t[0:n, :], reads=[t2], writes=[O["dbg_yb"]])
        out_proj(S, yb, n, Wo, 3, ti)

    import os
    for ci in range(int(os.environ.get("GDN_NCH", NCH))):
        n = 128
        prologue(n, HOFF + ci * 128, ci)
        if ci == NCH - 1:
            mk.dma("sp", O["shift_p"].t[l:l + 1, :], pS.t[127:128, :], reads=[pS], writes=[O["shift_p"]])
        r = xs[:, 0:384]
        pL = psp.get(384)
        k.mm(pL.a, Um.a, logw.a)
        k.tt("dve", E1.a, pL.a, logw.a, ALU.subtract)
        k.act(E1.a, E1.a, AF.Exp)
        k.tt("dve", bh.a, kk.a, E1.a, ALU.mult)
        k.act(E1.a, pL.a, AF.Exp, scale=-1.0)
        k.stt(ah.a, ka.a, -1.0, E1.a, ALU.mult, ALU.mult)
        k.tt("dve", kh.a, k2.a, E1.a, ALU.mult)
        k.act(t384.a, pL.a, AF.Exp)
        k.tt("dve", rh.a, r, t384.a, ALU.mult)
        k.cp("act", vvb.a, vv.a)
        pel = psp.get(12, 64)
        for h in range(6):
            k.mm(pel[:, 2 * h:2 * h + 2], logw[:, h * 64:(h + 1) * 64], ind2.a)
        k.act(elc.a, pel.a, AF.Exp)
        for (src, dst, eng) in ((bh, bhT, "act"), (ah, ahT, "dve"), (kh, khT, "act"), (rh, rhT, "dve")):
            pb_ = bankpool.get()
            ptr = Ref(pb_, pb_.t[:, :].bitcast(BF16))
            for h in range(6):
                k.tr(ptr[0:64, h * 128:(h + 1) * 128], src[:, h * 64:(h + 1) * 64], S["identB"].a)
            k.cp(eng, dst.a, ptr[0:64, 0:768].rr("p (h t) -> p h t", t=128))
        smb = S["sm"].a.un(1).bc([128, 3, 128]); smTb = S["smT"].a.un(1).bc([128, 3, 128]); cmTb = S["cmT"].a.un(1).bc([128, 3, 128])
        X6 = X60; XT6 = XT6p.get()
        grp_mm(S, psp, lambda h: [(bhT[:, h, :], ahT[:, h, :])], lambda g, ps: k.tt("dve", X6[g].a, ps, smb, ALU.mult))
        grp_mm(S, psp, lambda h: [(ahT[:, h, :], bhT[:, h, :])], lambda g, ps: k.tt("dve", XT6[g].a, ps, smTb, ALU.mult))
        grp_mm(S, psp, lambda h: [(khT[:, h, :], bhT[:, h, :])], lambda g, ps: k.tt("dve", ABKT6[g].a, ps, smTb, ALU.mult))
        grp_mm(S, psp, lambda h: [(ahT[:, h, :], rhT[:, h, :])], lambda g, ps: k.tt("dve", ARAT6[g].a, ps, cmTb, ALU.mult))
        grp_mm(S, psp, lambda h: [(khT[:, h, :], rhT[:, h, :])], lambda g, ps: k.tt("dve", ARKT6[g].a, ps, cmTb, ALU.mult))
        for _ in tri_inverse6(S, X6, XT6, (X6p, XT6p, TFp, TBp), psp, TTd6):
            pass
        TT6 = TTd6
        elc3 = elc.a.rr("p (h two) -> p h two", two=2)
        elm_b = elc3[:, :, 0:1].bc([64, 6, 64]); ecm_b = elc3[:, :, 1:2].bc([64, 6, 64])
        k.tt("dve", S0pf.a, St.a, elm_b, ALU.mult)
        k.cp("act", S0p.a, S0pf.a)
        k.tt("pool", S0pp.a, S0pf.a, ecm_b, ALU.mult)
        hs_ = lambda h: slice(h * 64, (h + 1) * 64)
        grp_mm(S, psp, lambda h: [(bhT[:, h, :], S0p[:, h, :]), (hv(ABKT6, h), vvb[:, hs_(h)])],
               lambda g, ps: k.cp("act", ru6.a.rr("p (h d) -> p h d", d=64), ps), w=64, ngrp=1, per=6)
        grp_mm(S, psp, lambda h: [(hv(TT6, h), ru6[:, hs_(h)])],
               lambda g, ps: k.cp("act", U6.a.rr("p (h d) -> p h d", d=64), ps), w=64, ngrp=1, per=6)
        grp_mm(S, psp, lambda h: [(rhT[:, h, :], S0p[:, h, :]), (hv(ARAT6, h), U6[:, hs_(h)]), (hv(ARKT6, h), vvb[:, hs_(h)])],
               lambda g, ps: k.cp("act", Y.a.rr("p (h d) -> p h d", d=64), ps), w=64, ngrp=1, per=6)

        def st_upd(g, ps):
            k.tt("dve", Stmp.a, ps, ecm_b, ALU.mult)
            k.tt("dve", St.a, Stmp.a, S0pp.a, ALU.add)
        grp_mm(S, psp, lambda h: [(ah[:, hs_(h)], U6[:, hs_(h)]), (kh[:, hs_(h)], vvb[:, hs_(h)])], st_upd, w=64, p=64, ngrp=1, per=6)
        epilogue(n, ci, ci * 128)
    for hh in range(0, 6, 4):
        nh = min(4, 6 - hh)
        ptr = psp.get(512, 64)
        for h in range(hh, hh + nh):
            k.tr(ptr[:, (h - hh) * 64:(h - hh + 1) * 64], St[:, h, :], identF[0:64, 0:64])
        k.cp("act", StT[:, hh:hh + nh, :], ptr[:, 0:nh * 64].rr("p (h t) -> p h t", t=64))
    mk.dma("sp", O["rwkv_p"].t[l].rearrange("h v k -> v h k"), StT.t, reads=[StT], writes=[O["rwkv_p"]])
    mk.barrier()
    arena.off = mchunk
    n = NS
    prev = arena.f32([NS, B_COLS])
    mk.dma("sp", prev.t, I["st_shift"].t[l], reads=[I["st_shift"]], writes=[prev])
    P = NS * 6
    Ss = arena.f32([P, 64, 64])
    mk.dma("act", Ss.t, I["st_rwkv"].t[l].rearrange("s h v k -> (s h) v k"), reads=[I["st_rwkv"]], writes=[Ss])
    prologue(n, HOFF + T, NCH, prev_sb=prev)
    mk.dma("sp", O["shift_s"].t[l], pS.t[0:n, :], reads=[pS], writes=[O["shift_s"]])
    wdec = arena.f32([NS, 384])
    k.act(wdec.a, logw[0:n, :], AF.Exp)
    tm = arena.f32([NS, 6, 392])
    srcs = [xs[0:n, 0:384], wdec.a, k2[0:n, :], vv[0:n, :], kk[0:n, :], ka[0:n, :]]
    for j, sref in enumerate(srcs):
        k.cp("dve" if j % 2 == 0 else "pool", tm[:, :, j * 64:(j + 1) * 64], sref.rr("p (h d) -> p h d", d=64))
    bo = S["bounceB"]
    mk.dma("sp", bo.t.rearrange("s (h c) -> s h c", h=6), tm.t, reads=[tm], writes=[bo])
    P = NS * 6
    sh = arena.f32([P, 392])
    mk.dma("sp", sh.t, bo.t.rearrange("s (h c) -> (s h) c", h=6), reads=[bo], writes=[sh])
    tmp = arena.f32([P, 64, 64])
    r_ = sh[:, 0:64]; w_ = sh[:, 64:128]; k2_ = sh[:, 128:192]; v_ = sh[:, 192:256]; kk_ = sh[:, 256:320]; ka_ = sh[:, 320:384]
    sa = arena.f32([P, 64]); ys = arena.f32([P, 64])
    rowb = lambda ref: ref.un(1).bc([P, 64, 64])
    colb = lambda ref: ref.un(2).bc([P, 64, 64])
    k.tt("dve", tmp.a, Ss.a, rowb(kk_), ALU.mult)
    k.red(sa.a, tmp.a)
    k.ts("dve", sa.a, sa.a, -1.0, ALU.mult)
    k.tt("dve", Ss.a, Ss.a, rowb(w_), ALU.mult)
    k.tt("pool", tmp.a, colb(sa.a), rowb(ka_), ALU.mult)
    k.tt("dve", Ss.a, Ss.a, tmp.a, ALU.add)
    k.tt("pool", tmp.a, colb(v_), rowb(k2_), ALU.mult)
    k.tt("dve", Ss.a, Ss.a, tmp.a, ALU.add)
    mk.dma("sp", O["rwkv_s"].t[l].rearrange("s h v k -> (s h) v k"), Ss.t, reads=[Ss], writes=[O["rwkv_s"]])
    k.tt("dve", tmp.a, Ss.a, rowb(r_), ALU.mult)
    k.red(ys.a, tmp.a)
    b2 = S["bounce2"]
    mk.dma("sp", b2.t, ys.t, reads=[ys], writes=[b2])
    mk.dma("sp", Y.t[0:n, :], b2.t.rearrange("(s h) d -> s (h d)", h=6), reads=[b2], writes=[Y])
    epilogue(n, NCH, T)
    arena.off = m0
    mk.barrier()


def hgrn_phase(S, l):
    k = S["k"]; mk = S["mk"]; arena = S["arena"]; I = S["I"]; O = S["O"]; banks = S["banks"]; hT = S["hT"]
    identF = S["identF"]
    m0 = arena.off
    Wc = arena.bf16([128, 8, C_COLS]); Wo = arena.bf16([128, 2, 1024])
    load_w(S, Wc, I["w_in"].t[l, :, C0:C0 + C_COLS], I["w_in"])
    load_w(S, Wo, I["w_out"].t[l, 768:1024, :], I["w_out"])
    lbr = arena.f32([128, 4, 256]); mx = arena.f32([128, 256]); oml = arena.f32([128, 256]); ssum = arena.f32([128, 256])
    mk.dma("sp", lbr.t, I["hgrn_lower_bounds"].t.rearrange("(o l) c -> o l c", o=1).to_broadcast([128, 4, 256]),
           reads=[I["hgrn_lower_bounds"]], writes=[lbr])
    k.tt("dve", mx.a, lbr[:, 0, :], lbr[:, 1, :], ALU.max)
    k.tt("dve", mx.a, mx.a, lbr[:, 2, :], ALU.max)
    k.tt("dve", mx.a, mx.a, lbr[:, 3, :], ALU.max)
    for j in range(4):
        k.tt("dve", lbr[:, j, :], lbr[:, j, :], mx.a, ALU.subtract)
    k.act(lbr.a, lbr.a, AF.Exp)
    k.tt("dve", ssum.a, lbr[:, 0, :], lbr[:, 1, :], ALU.add)
    k.tt("dve", ssum.a, ssum.a, lbr[:, 2, :], ALU.add)
    k.tt("dve", ssum.a, ssum.a, lbr[:, 3, :], ALU.add)
    k.recip(ssum.a, ssum.a)
    k.memset("pool", oml.a, 0.0)
    for j in range(1, l + 1):
        k.tt("dve", oml.a, oml.a, lbr[:, j, :], ALU.add)
    k.tt("dve", oml.a, oml.a, ssum.a, ALU.mult)
    k.ts("dve", oml.a, oml.a, -1.0, ALU.mult, 1.0, ALU.add)
    gnw = arena.f32([128, 64])
    mk.dma("sp", gnw.t, I["hgrn_norm_w"].t[l:l + 1, :].to_broadcast([128, 64]), reads=[I["hgrn_norm_w"]], writes=[gnw])
    Um1 = arena.f32([128, 128]); Um2 = arena.f32([128, 128]); bdm = arena.f32([128, 128])
    k.memset("pool", Um1.a, 0.0)
    k.memset("pool", Um1[0:32, 0:64], 1.0)
    k.memset("pool", Um1[0:96, 64:128], 1.0)
    k.tt("pool", Um1.a, S["Utri"].a, Um1.a, ALU.subtract)
    k.memset("pool", Um2.a, 0.0)
    k.memset("pool", Um2[0:64, :], 1.0)
    k.tt("pool", Um2.a, S["Utri"].a, Um2.a, ALU.subtract)
    k.cp("pool", bdm.a, S["cmT"].a)
    k.memset("pool", bdm[0:64, 64:128], 0.0)
    St = arena.f32([64, 4, 64]); k.memset("pool", St.a, 0.0)
    Sth = [Buf(St.t[:, h, :], f"hgS{h}") for h in range(4)]
    S["yT"] = arena.bf16([128, 8, 128])
    S["xt_pool"] = Pool_([arena.f32([128, D]) for _ in range(2)])
    qs = arena.f32([128, 256]); kf = arena.f32([128, 256]); logf = arena.f32([128, 256]); iv = arena.f32([128, 256]); gs = arena.f32([128, 256])
    t1 = arena.f32([128, 256]); t2 = arena.f32([128, 256])
    o = arena.f32([128, 256]); y32 = arena.f32([128, 256]); yb = arena.bf16([128, 256]); rs = arena.f32([128, 4])
    mchunk = arena.off
    q1 = arena.bf16([128, 256]); k1 = arena.bf16([128, 256]); q2 = arena.bf16([128, 256]); k2 = arena.bf16([128, 256])
    qS = arena.bf16([128, 256]); kE = arena.bf16([128, 256]); ee = arena.f32([128, 256]); ivb = arena.bf16([128, 256])
    Stb = arena.bf16([64, 4, 64]); k.memset("pool", Stb.a, 0.0)
    AT4 = arena.bf16([128, 4, 128]); AT4f = arena.f32([128, 4, 128])
    k.memset("pool", q2.a, 0.0); k.memset("pool", k2.a, 0.0)
    q1T = arena.bf16([64, 4, 128]); k1T = arena.bf16([64, 4, 128]); q2T = arena.bf16([64, 4, 128]); k2T = arena.bf16([64, 4, 128]); qST = arena.bf16([64, 4, 128])
    eend = arena.f32([64, 4])
    ATp = Pool_([arena.f32([128, 128]) for _ in range(4)])
    bankpool = Pool_([banks[i] for i in (2, 3, 4, 5, 6, 7, 0, 1)])

    class PSP:
        def get(self, w=128, p=128):
            b = bankpool.get()
            return Ref(b, b.t[0:p, 0:w])
    psp = PSP()

    def h3(ref, n):
        return ref[0:n, :].rr("p (h d) -> p h d", d=64)

    def b3(ref, n):
        return ref[0:n, :].un(2).bc([n, 4, 64])

    def prologue(n, col0):
        for gi in range(2):
            for c in range(8):
                k.mm(banks[gi][0:n, :], hT[:, c, col0:col0 + n], Wc[:, c, gi * 512:(gi + 1) * 512], start=(c == 0), stop=(c == 7))
        k.cp("act", t1[0:n, :], banks[0][0:n, 0:256])
        k.sigmoid(t2[0:n, :], t1[0:n, :], t2[0:n, :])
        k.stt(qs[0:n, :], t1[0:n, :], 0.125, t2[0:n, :], ALU.mult, ALU.mult)
        k.sigmoid(t2[0:n, :], banks[0][0:n, 256:512], t2[0:n, :], sign=-1.0)
        k.tt("pool", kf[0:n, :], t2[0:n, :], oml[0:n, :], ALU.mult)
        k.act(logf[0:n, :], kf[0:n, :], AF.Ln, bias=1.0, scale=-1.0)
        k.cp("act", iv[0:n, :], banks[1][0:n, 0:256])
        k.cp("act", t1[0:n, :], banks[1][0:n, 256:512])
        k.sigmoid(t2[0:n, :], t1[0:n, :], t2[0:n, :])
        k.tt("pool", gs[0:n, :], t1[0:n, :], t2[0:n, :], ALU.mult)

    def epilogue(n, ti, row0):
        k.tt("dve", t1[0:n, :], o[0:n, :], o[0:n, :], ALU.mult)
        k.red(rs[0:n, :], h3(t1, n))
        k.rsqrt_ln(rs[0:n, :], rs[0:n, :], 1.0 / 64, 1e-6)
        k.tt("dve", h3(y32, n), h3(o, n), b3(rs, n), ALU.mult)
        k.tt("dve", h3(y32, n), h3(y32, n), gnw[0:n, :].un(1).bc([n, 4, 64]), ALU.mult)
        k.tt("dve", yb[0:n, :], y32[0:n, :], gs[0:n, :], ALU.mult)
        if S["dbg"] and l == 0:
            k.tt("pool", y32[0:n, :], y32[0:n, :], gs[0:n, :], ALU.mult)
            mk.dma("sp", O["dbg_yc"].t[row0:row0 + n, :], y32.t[0:n, :], reads=[y32], writes=[O["dbg_yc"]])
        out_proj(S, yb, n, Wo, 2, ti)

    import os
    for ci in range(int(os.environ.get("GDN_NCH", NCH))):
        n = 128
        prologue(n, HOFF + ci * 128)
        pL1 = psp.get(256); pL2 = psp.get(256); pL = psp.get(256); pLe = psp.get(256)
        k.mm(pL1.a, Um1.a, logf.a)
        k.mm(pL2.a, Um2.a, logf.a)
        k.mm(pL.a, S["Utri"].a, logf.a)
        k.mm(pLe.a, S["sm"].a, logf.a)
        k.ts("dve", ee.a, pL1.a, 80.0, ALU.min)
        k.act(ee.a, ee.a, AF.Exp)
        k.tt("dve", q1.a, qs.a, ee.a, ALU.mult)
        k.ts("dve", ee.a, pL1.a, -1.0, ALU.mult, 80.0, ALU.min)
        k.act(ee.a, ee.a, AF.Exp)
        k.tt("dve", k1.a, kf.a, ee.a, ALU.mult)
        k.act(t1[64:128, :], pL2[64:128, :], AF.Exp)
        k.tt("pool", q2[64:128, :], qs[64:128, :], t1[64:128, :], ALU.mult)
        k.act(t1[0:64, :], pL2[0:64, :], AF.Exp, scale=-1.0)
        k.tt("pool", k2[0:64, :], kf[0:64, :], t1[0:64, :], ALU.mult)
        k.act(t2.a, pL.a, AF.Exp)
        k.tt("dve", qS.a, qs.a, t2.a, ALU.mult)
        k.act(t2.a, pLe.a, AF.Exp)
        k.tt("dve", kE.a, kf.a, t2.a, ALU.mult)
        pe_ = psp.get(4, 64)
        for h in range(4):
            k.mm(pe_[:, h:h + 1], logf[:, h * 64:(h + 1) * 64], S["ones"][:, 0:1])
        k.act(eend.a, pe_.a, AF.Exp)
        k.cp("act", ivb.a, iv.a)
        for (src, dst, eng) in ((q1, q1T, "act"), (k1, k1T, "dve"), (q2, q2T, "act"), (k2, k2T, "dve"), (qS, qST, "act")):
            pb_ = bankpool.get()
            ptr = Ref(pb_, pb_.t[:, :].bitcast(BF16))
            for h in range(4):
                k.tr(ptr[0:64, h * 128:(h + 1) * 128], src[:, h * 64:(h + 1) * 64], S["identB"].a)
            k.cp(eng, dst.a, ptr[0:64, 0:512].rr("p (h t) -> p h t", t=128))
        bdm4 = bdm.a.un(1).bc([128, 4, 128])
        grp_mm(S, psp, lambda h: [(k1T[:, h, :], q1T[:, h, :])], lambda g, ps: k.tt("dve", AT4f.a, ps, bdm4, ALU.mult), ngrp=1, per=4)
        grp_mm(S, psp, lambda h: [(k2T[:, h, :], q2T[:, h, :])], lambda g, ps: k.tt("dve", AT4.a, AT4f.a, ps, ALU.add), ngrp=1, per=4)
        hs_ = lambda h: slice(h * 64, (h + 1) * 64)
        grp_mm(S, psp, lambda h: [(qST[:, h, :], Stb[:, h, :]), (AT4[:, h, :], ivb[:, hs_(h)])],
               lambda g, ps: k.cp("act", o.a.rr("p (h d) -> p h d", d=64), ps), w=64, ngrp=1, per=4)

        def st_upd(g, ps):
            k.tt("dve", St.a, St.a, eend.a.un(2).bc([64, 4, 64]), ALU.mult)
            k.tt("dve", St.a, St.a, ps, ALU.add)
            k.cp("act", Stb.a, St.a)
        grp_mm(S, psp, lambda h: [(kE[:, hs_(h)], ivb[:, hs_(h)])], st_upd, w=64, p=64, ngrp=1, per=4)
        epilogue(n, ci, ci * 128)
    mk.dma("sp", O["hgrn_p"].t[l].rearrange("h d v -> d h v"), St.t, reads=[St], writes=[O["hgrn_p"]])
    mk.barrier()
    arena.off = mchunk
    n = NS
    P = NS * 4
    Ss = arena.f32([P, 64, 64])
    mk.dma("act", Ss.t, I["st_hgrn"].t[l].rearrange("s h d v -> (s h) d v"), reads=[I["st_hgrn"]], writes=[Ss])
    prologue(n, HOFF + T)
    fdec = arena.f32([NS, 256])
    k.act(fdec.a, logf[0:n, :], AF.Exp)
    tm = arena.f32([NS, 4, 256])
    for j, sref in enumerate([qs[0:n, :], kf[0:n, :], fdec.a, iv[0:n, :]]):
        k.cp("dve" if j % 2 == 0 else "pool", tm[:, :, j * 64:(j + 1) * 64], sref.rr("p (h d) -> p h d", d=64))
    bo = S["bounceC"]
    bo_v = bo.t
    mk.dma("sp", bo_v.rearrange("s (h c) -> s h c", h=4), tm.t, reads=[tm], writes=[bo])
    P = NS * 4
    sh = arena.f32([P, 256])
    mk.dma("sp", sh.t, bo_v.rearrange("s (h c) -> (s h) c", h=4), reads=[bo], writes=[sh])
    tmp = arena.f32([P, 64, 64])
    q_ = sh[:, 0:64]; k_ = sh[:, 64:128]; f_ = sh[:, 128:192]; i_ = sh[:, 192:256]
    os_ = arena.f32([P, 64])
    colb = lambda ref: ref.un(2).bc([P, 64, 64])
    rowb = lambda ref: ref.un(1).bc([P, 64, 64])
    k.tt("dve", Ss.a, Ss.a, colb(f_), ALU.mult)
    k.tt("pool", tmp.a, colb(k_), rowb(i_), ALU.mult)
    k.tt("dve", Ss.a, Ss.a, tmp.a, ALU.add)
    mk.dma("sp", O["hgrn_s"].t[l].rearrange("s h d v -> (s h) d v"), Ss.t, reads=[Ss], writes=[O["hgrn_s"]])
    k.tt("dve", tmp.a, Ss.a, colb(q_), ALU.mult)
    k.red(os_.a, tmp.a.rr("p d v -> p v d"))
    b2 = S["bounce2C"]
    mk.dma("sp", b2.t, os_.t, reads=[os_], writes=[b2])
    mk.dma("sp", o.t[0:n, :], b2.t.rearrange("(s h) d -> s (h d)", h=4), reads=[b2], writes=[o])
    epilogue(n, NCH, T)
    arena.off = m0
    mk.barrier()


def ffn_phase(S, l, norm_phase):
    k = S["k"]; mk = S["mk"]; arena = S["arena"]; I = S["I"]; O = S["O"]; banks = S["banks"]; hT = S["hT"]
    norm_phase("norm_ffn", l)
    m0 = arena.off
    quarters = [(0, 6), (6, 6), (12, 5), (17, 5)]
    wb = []
    for j in range(2):
        wb.append((arena.bf16([128, 8, 768]), arena.bf16([128, 8, 768]), arena.bf16([128, 6, 1024])))
    uTs = [arena.bf16([128, 6, 512]) for _ in range(2)]
    xts = Pool_([arena.f32([128, D]) for _ in range(3)])
    sgs = Pool_([arena.f32([128, 512]) for _ in range(2)])
    tgs = Pool_([arena.f32([128, 512]) for _ in range(2)])
    bankpool = Pool_([banks[i] for i in range(8)])
    blocks = [(0, 512), (512, 512), (1024, 512), (1536, 512), (T, NS)]

    def loadq(qi):
        f0, nf = quarters[qi]
        Wg, Wu, Wd = wb[qi % 2]
        mk.dma("pool", Wg.t[:, :, 0:nf * 128], I["w_gate"].t[l, :, f0 * 128:(f0 + nf) * 128].rearrange("(c p) n -> p c n", p=128),
               reads=[I["w_gate"]], writes=[Wg])
        mk.dma("pool", Wu.t[:, :, 0:nf * 128], I["w_up"].t[l, :, f0 * 128:(f0 + nf) * 128].rearrange("(c p) n -> p c n", p=128),
               reads=[I["w_up"]], writes=[Wu])
        mk.dma("pool", Wd.t[:, 0:nf, :], I["w_down"].t[l, f0 * 128:(f0 + nf) * 128, :].rearrange("(c p) n -> p c n", p=128),
               reads=[I["w_down"]], writes=[Wd])
    loadq(0)

    def gateup(qi, bi, uT):
        f0_, nf = quarters[qi]
        Wg, Wu, Wd = wb[qi % 2]
        t0, nb = blocks[bi]
        col0 = HOFF + t0
        for f in range(nf):
            pg = bankpool.get(); pu = bankpool.get()
            for c in range(8):
                k.mm(pg[:, 0:nb], Wg[:, c, f * 128:(f + 1) * 128], hT[:, c, col0:col0 + nb], start=(c == 0), stop=(c == 7))
            for c in range(8):
                k.mm(pu[:, 0:nb], Wu[:, c, f * 128:(f + 1) * 128], hT[:, c, col0:col0 + nb], start=(c == 0), stop=(c == 7))
            sg = sgs.get(); tg = tgs.get()
            k.cp("dve", tg[:, 0:nb], pg[:, 0:nb])
            k.sigmoid(sg[:, 0:nb], tg[:, 0:nb], sg[:, 0:nb])
            k.tt("dve", tg[:, 0:nb], tg[:, 0:nb], sg[:, 0:nb], ALU.mult)
            k.tt("dve", uT[:, f, 0:nb], tg[:, 0:nb], pu[:, 0:nb], ALU.mult)

    def down(qi, bi, uT):
        f0_, nf = quarters[qi]
        Wg, Wu, Wd = wb[qi % 2]
        t0, nb = blocks[bi]
        for ts_ in range(0, nb, 128):
            n = min(128, nb - ts_)
            ti = (t0 + ts_) // 128
            xt = xts.get()
            mk.dma("sp", xt.t[0:n, :], S["xres"][t0 + ts_:t0 + ts_ + n, :], reads=[S["xb"][ti]], writes=[xt])
            for dh in range(2):
                pd = bankpool.get()
                for f in range(nf):
                    k.mm(pd[0:n, :], uT[:, f, ts_:ts_ + n], Wd[:, f, dh * 512:(dh + 1) * 512], start=(f == 0), stop=(f == nf - 1))
                k.tt("dve", xt[0:n, dh * 512:(dh + 1) * 512], xt[0:n, dh * 512:(dh + 1) * 512], pd[0:n, :], ALU.add)
            mk.dma("sp", S["xres"][t0 + ts_:t0 + ts_ + n, :], xt.t[0:n, :], reads=[xt], writes=[S["xb"][ti]])

    items = [(qi, bi) for qi in range(4) for bi in range(len(blocks))]
    loadq(1)
    gateup(items[0][0], items[0][1], uTs[0])
    for idx, (qi, bi) in enumerate(items):
        if idx + 1 < len(items):
            gateup(items[idx + 1][0], items[idx + 1][1], uTs[(idx + 1) % 2])
        down(qi, bi, uTs[idx % 2])
        if bi == len(blocks) - 1 and qi + 2 < 4:
            loadq(qi + 2)
    arena.off = m0
    mk.barrier()


def final_phase(S):
    k = S["k"]; mk = S["mk"]; arena = S["arena"]; I = S["I"]; O = S["O"]
    m0 = arena.off
    wbc = arena.f32([128, D])
    mk.dma("sp", wbc.t, I["norm_final"].t[0:1, :].to_broadcast([128, D]), reads=[I["norm_final"]], writes=[wbc])
    ss = arena.f32([128, 32])
    k.memset("pool", ss.a, 0.0)
    xts = [arena.f32([128, D]) for _ in range(3)]
    junk = arena.bf16([128, D])
    ys = [arena.f32([128, D]) for _ in range(2)]
    tiles = S["tiles"]
    for i, (t0, n) in enumerate(tiles):
        xt = xts[i % 3]
        mk.dma("sp", xt.t[0:n, :], S["xres"][t0:t0 + n, :], reads=[S["xb"][i]], writes=[xt])
        k.act(junk[0:n, :], xt[0:n, :], AF.Square, accum=ss[0:n, i:i + 1])
    k.rsqrt_ln(ss.a, ss.a, 1.0 / D, 1e-6)
    for i, (t0, n) in enumerate(tiles):
        xt = xts[i % 3]
        mk.dma("sp", xt.t[0:n, :], S["xres"][t0:t0 + n, :], reads=[S["xb"][i]], writes=[xt])
        y = ys[i % 2]
        k.stt(y[0:n, :], xt[0:n, :], ss[0:n, i:i + 1], wbc[0:n, :], ALU.mult, ALU.mult)
        mk.dma("sp", O["y"].t[t0:t0 + n, :], y.t[0:n, :], reads=[y], writes=[O["y"]])
    arena.off = m0


_NC_CACHE = {}


def kernel(**inputs):
    if "nc" not in _NC_CACHE:
        _NC_CACHE["nc"] = build()
    nc = _NC_CACHE["nc"]
    f = lambda a: np.ascontiguousarray(np.asarray(a), dtype=np.float32)
    shared = {}
    for name in ("norm_mix", "w_in", "w_out", "gdn_conv_w", "gdn_a_log", "gdn_dt_bias", "gdn_norm_w", "rwkv_mu", "rwkv_w0",
                 "rwkv_w2", "rwkv_a0", "rwkv_a2", "rwkv_g2", "rwkv_v0", "rwkv_v1", "rwkv_v2", "rwkv_k_k", "rwkv_k_a",
                 "rwkv_ln_w", "rwkv_ln_b", "hgrn_lower_bounds", "hgrn_norm_w", "norm_ffn", "w_gate", "w_up", "w_down"):
        shared[name] = f(inputs[name])
    shared["rwkv_r_k"] = f(inputs["rwkv_r_k"]).reshape(DEPTH, 384)
    shared["norm_final"] = f(inputs["norm_final"]).reshape(1, D)
    xp = f(inputs["x_prompt"]); xs = f(inputs["x_sample"])
    sg = f(inputs["state_gdn"]); sc = f(inputs["state_gdn_conv"]); sr = f(inputs["state_rwkv"])
    ssh = f(inputs["state_rwkv_shift"]); shg = f(inputs["state_hgrn"])
    in_maps = []
    for c in range(8):
        sl = slice(NS * c, NS * (c + 1))
        m = dict(shared)
        m["xin"] = np.ascontiguousarray(np.concatenate([xp[c], xs[sl, 0]], 0))
        m["st_gdn"] = np.ascontiguousarray(sg[:, sl]); m["st_conv"] = np.ascontiguousarray(sc[:, sl])
        m["st_rwkv"] = np.ascontiguousarray(sr[:, sl]); m["st_shift"] = np.ascontiguousarray(ssh[:, sl])
        m["st_hgrn"] = np.ascontiguousarray(shg[:, sl])
        in_maps.append(m)
    res = run_bass_kernel_spmd(nc, in_maps, core_ids=list(range(8)))
    R = res.results
    y_prompt = np.stack([R[c]["y"][:T] for c in range(8)], 0).astype(np.float32)
    y_sample = np.concatenate([R[c]["y"][T:] for c in range(8)], 0)[:, None, :].astype(np.float32)

    def pstack(name):
        return np.stack([R[c][name] for c in range(8)], 1).astype(np.float32)

    def scat(name):
        return np.concatenate([R[c][name] for c in range(8)], 1).astype(np.float32)
    return (y_prompt, y_sample, pstack("gdn_p"), scat("gdn_s"), pstack("conv_p"), scat("conv_s"),
            pstack("rwkv_p"), scat("rwkv_s"), pstack("shift_p"), scat("shift_s"), pstack("hgrn_p"), scat("hgrn_s"))
```

```python
import contextlib
import numpy as np
import concourse.bass as bass
import concourse.mybir as mybir

F32 = mybir.dt.float32
BF16 = mybir.dt.bfloat16
AF = mybir.ActivationFunctionType
ALU = mybir.AluOpType
AX = mybir.AxisListType

import os as _os
SAME_ENGINE_SYNC = _os.environ.get("SES", "1") == "1"


class Buf:
    __slots__ = ("t", "lastw", "readers", "name", "excl")

    def __init__(self, t, name=""):
        self.t = t
        self.excl = False
        self.lastw = None
        self.readers = []
        self.name = name

    def __getitem__(self, idx):
        return Ref(self, self.t[idx])

    @property
    def a(self):
        return Ref(self, self.t[:])


class MK:
    ENGS = ("pe", "act", "dve", "pool", "sp")

    def __init__(self, nc, n_dma_sems=6):
        self.nc = nc
        self.stack = contextlib.ExitStack()
        self.prog = {e: [] for e in self.ENGS}
        self.cnt = {e: 0 for e in self.ENGS}
        self.sems = {}
        for e in self.ENGS:
            self.sems[e] = self.stack.enter_context(nc.semaphore("s_" + e))
        self.dsem = {}
        self.dcnt = {}
        self.dnext = {}
        for q in ("sp", "pool", "act"):
            self.dsem[q] = [self.stack.enter_context(nc.semaphore(f"d_{q}{i}")) for i in range(n_dma_sems)]
            self.dcnt[q] = [0] * n_dma_sems
            self.dnext[q] = 0
        self.seen = {e: {} for e in self.ENGS}
        self.semobj = {}
        for e in self.ENGS:
            self.semobj[("e", e)] = self.sems[e]
        for q in self.dsem:
            for i, s in enumerate(self.dsem[q]):
                self.semobj[("d", q, i)] = s
        self.nuniq = 0

    def sb(self, shape, dtype=F32, name=None):
        self.nuniq += 1
        name = f"{name or 'sb'}_{self.nuniq}"
        t = self.stack.enter_context(self.nc.sbuf_tensor(name, list(shape), dtype))
        return Buf(t, name)

    def ps(self, shape, dtype=F32, name=None):
        self.nuniq += 1
        name = f"{name or 'ps'}_{self.nuniq}"
        t = self.stack.enter_context(self.nc.psum_tensor(name, list(shape), dtype))
        b = Buf(t, name)
        b.excl = True
        return b

    def view(self, buf_or_ap, name=""):
        return Buf(buf_or_ap, name)

    def _need(self, eng, reads, writes):
        need = {}

        def add(ev):
            if ev is None:
                return
            key, val, en = ev
            if en == eng and key[0] == "e" and (eng == "pe" or not SAME_ENGINE_SYNC):
                return
            if need.get(key, 0) < val:
                need[key] = val

        for b in reads:
            add(b.lastw)
        for b in writes:
            add(b.lastw)
            for r in b.readers:
                add(r)
        out = []
        seen = self.seen[eng]
        for key, val in need.items():
            if seen.get(key, 0) < val:
                seen[key] = val
                out.append((key, val))
        return out

    def _commit(self, ev, reads, writes):
        for b in reads:
            b.readers.append(ev)
        for b in writes:
            b.lastw = ev
            b.readers = []

    def op(self, eng, fn, reads=(), writes=()):
        rx = [b for b in reads if b.excl]
        if rx:
            writes = list(writes) + rx
            reads = [b for b in reads if not b.excl]
        waits = self._need(eng, reads, writes)
        self.cnt[eng] += 1
        n = self.cnt[eng]
        ev = (("e", eng), n, eng)
        self._commit(ev, reads, writes)
        self.prog[eng].append((waits, fn, ("e", eng), 1))
        return ev

    def dma(self, q, out_ap, in_ap, reads=(), writes=(), **kw):
        i = self.dnext[q]
        self.dnext[q] = (i + 1) % len(self.dsem[q])
        key = ("d", q, i)
        prev = self.dcnt[q][i]
        waits = self._need(q, reads, writes)
        if prev > 0 and self.seen[q].get(key, 0) < prev:
            self.seen[q][key] = prev
            waits.append((key, prev))
        self.dcnt[q][i] = prev + 16
        ev = (key, prev + 16, "dma_" + q)
        self._commit(ev, reads, writes)

        def fn(e, out_ap=out_ap, in_ap=in_ap, kw=kw):
            return e.dma_start(out=out_ap, in_=in_ap, **kw)
        self.prog[q].append((waits, fn, key, 16))
        return ev

    def barrier(self):
        tgt = {}
        for e in self.ENGS:
            if self.cnt[e] > 0:
                tgt[("e", e)] = self.cnt[e]
        for q in self.dsem:
            for i in range(len(self.dsem[q])):
                if self.dcnt[q][i] > 0:
                    tgt[("d", q, i)] = self.dcnt[q][i]
        for e in self.ENGS:
            waits = []
            for k, v in tgt.items():
                if self.seen[e].get(k, 0) < v:
                    self.seen[e][k] = v
                    waits.append((k, v))
            if waits:
                self.prog[e].append((waits, None, None, 0))

    def all_dma_events(self):
        return [(("d", q, i), self.dcnt[q][i], "") for q in self.dsem
                for i in range(len(self.dsem[q])) if self.dcnt[q][i] > 0]

    def emit(self, final_events):
        nc = self.nc
        prog = self.prog
        semobj = self.semobj
        fw = {}
        for key, val, _ in final_events:
            fw[key] = max(fw.get(key, 0), val)

        def run(engname, e):
            for waits, fn, key, inc in prog[engname]:
                for k, v in waits:
                    e.wait_ge(semobj[k], v)
                if fn is not None:
                    fn(e).then_inc(semobj[key], inc)

        with nc.Block() as block:
            @block.tensor
            def _(e):
                run("pe", e)

            @block.scalar
            def _(e):
                run("act", e)

            @block.vector
            def _(e):
                run("dve", e)

            @block.gpsimd
            def _(e):
                run("pool", e)

            @block.sync
            def _(e):
                run("sp", e)
                for k, v in fw.items():
                    e.wait_ge(semobj[k], v)
        self.stack.close()


class Ref:
    __slots__ = ("buf", "ap")

    def __init__(self, buf, ap):
        self.buf = buf
        self.ap = ap

    def __getitem__(self, idx):
        return Ref(self.buf, self.ap[idx])

    @property
    def a(self):
        return self

    def rr(self, s, **kw):
        return Ref(self.buf, self.ap.rearrange(s, **kw))

    def bc(self, shape):
        return Ref(self.buf, self.ap.to_broadcast(list(shape)))

    def un(self, axis):
        return Ref(self.buf, self.ap.unsqueeze(axis))

    def bitcast(self, dt):
        return Ref(self.buf, self.ap.bitcast(dt))


def R_(buf, idx=None):
    return Ref(buf, buf.t[:] if idx is None else buf.t[idx])


from concourse.bass_utils import run_bass_kernel_spmd

D = 1024; T = 2048; NS = 16; NT = T + NS; NCH = 16; DEPTH = 4
HD = 64; HA = 6; HB = 6; HC = 4
A_COLS = 1548; B_COLS = 1408; C_COLS = 1024; IN_COLS = 3980; DFF = 2816
B0 = A_COLS; C0 = A_COLS + B_COLS
HOFF = 3


class Arena:
    def __init__(self, mk, words):
        self.mk = mk
        self.t = mk.stack.enter_context(mk.nc.sbuf_tensor("arena", [128, words], F32))
        self.words = words
        self.off = 0

    def _take(self, n):
        n = (n + 7) // 8 * 8
        o = self.off
        assert o + n <= self.words, f"arena overflow {o}+{n}>{self.words}"
        self.off = o + n
        return o

    def f32(self, shape, name=""):
        p = shape[0]; n = int(np.prod(shape[1:]))
        o = self._take(n)
        ap = self.t[0:p, o:o + n]
        if len(shape) == 3:
            ap = ap.rearrange("p (a b) -> p a b", a=shape[1])
        return Buf(ap, name)

    def bf16(self, shape, name=""):
        p = shape[0]; n = int(np.prod(shape[1:]))
        w = (n + 1) // 2
        o = self._take(w)
        ap = self.t[0:p, o:o + w].bitcast(BF16)[:, 0:n]
        if len(shape) == 3:
            ap = ap.rearrange("p (a b) -> p a b", a=shape[1])
        return Buf(ap, name)


class Pool_:
    def __init__(self, items):
        self.items = items
        self.i = 0

    def get(self):
        b = self.items[self.i]
        self.i = (self.i + 1) % len(self.items)
        return b


class K:
    def __init__(self, mk):
        self.mk = mk

    def act(self, out, in_, func, bias=None, scale=None, accum=None):
        rd = [in_.buf]; wr = [out.buf]
        kw = {}
        if bias is not None:
            if isinstance(bias, Ref):
                rd.append(bias.buf); kw["bias"] = bias.ap
            else:
                kw["bias"] = float(bias)
        if scale is not None:
            if isinstance(scale, Ref):
                rd.append(scale.buf); kw["scale"] = scale.ap
            else:
                kw["scale"] = float(scale)
        if accum is not None:
            wr.append(accum.buf); kw["accum_out"] = accum.ap
        self.mk.op("act", lambda e: e.activation(out=out.ap, in_=in_.ap, func=func, **kw), reads=rd, writes=wr)

    def tt(self, eng, out, a, b, op):
        self.mk.op(eng, lambda e: e.tensor_tensor(out=out.ap, in0=a.ap, in1=b.ap, op=op), reads=[a.buf, b.buf], writes=[out.buf])

    def ts(self, eng, out, a, s1, op0, s2=None, op1=None):
        rd = [a.buf]
        v1 = s1
        if isinstance(s1, Ref):
            rd.append(s1.buf); v1 = s1.ap
        v2 = s2
        if isinstance(s2, Ref):
            rd.append(s2.buf); v2 = s2.ap
        if op1 is None:
            self.mk.op(eng, lambda e: e.tensor_scalar(out=out.ap, in0=a.ap, scalar1=v1, scalar2=None, op0=op0), reads=rd, writes=[out.buf])
        else:
            self.mk.op(eng, lambda e: e.tensor_scalar(out=out.ap, in0=a.ap, scalar1=v1, scalar2=v2, op0=op0, op1=op1), reads=rd, writes=[out.buf])

    def stt(self, out, a, scalar, b, op0, op1):
        rd = [a.buf, b.buf]
        sv = scalar
        if isinstance(scalar, Ref):
            rd.append(scalar.buf); sv = scalar.ap
        self.mk.op("dve", lambda e: e.scalar_tensor_tensor(out=out.ap, in0=a.ap, scalar=sv, in1=b.ap, op0=op0, op1=op1), reads=rd, writes=[out.buf])

    def red(self, out, in_, op=ALU.add):
        self.mk.op("dve", lambda e: e.tensor_reduce(out=out.ap, in_=in_.ap, axis=AX.X, op=op), reads=[in_.buf], writes=[out.buf])

    def cp(self, eng, out, in_):
        if eng == "act":
            self.mk.op("act", lambda e: e.copy(out=out.ap, in_=in_.ap), reads=[in_.buf], writes=[out.buf])
        else:
            self.mk.op(eng, lambda e: e.tensor_copy(out=out.ap, in_=in_.ap), reads=[in_.buf], writes=[out.buf])

    def memset(self, eng, out, val):
        self.mk.op(eng, lambda e: e.memset(out.ap, val), writes=[out.buf])

    def recip(self, out, in_):
        self.mk.op("dve", lambda e: e.reciprocal(out=out.ap, in_=in_.ap), reads=[in_.buf], writes=[out.buf])

    def mm(self, out, lhsT, rhs, start=True, stop=True):
        rd = [lhsT.buf, rhs.buf]
        wr = [out.buf]
        if not start:
            rd.append(out.buf)
        self.mk.op("pe", lambda e: e.matmul(out.ap, lhsT=lhsT.ap, rhs=rhs.ap, start=start, stop=stop), reads=rd, writes=wr)

    def tr(self, out, in_, ident):
        self.mk.op("pe", lambda e: e.transpose(out.ap, in_.ap, ident.ap), reads=[in_.buf, ident.buf], writes=[out.buf])

    def dma(self, q, out, in_, **kw):
        return self.mk.dma(q, out.ap, in_.ap, reads=[in_.buf], writes=[out.buf], **kw)

    def asel(self, out, in_, pattern, cmp, fill, base, cm):
        self.mk.op("pool", lambda e: e.affine_select(out=out.ap, in_=in_.ap, pattern=pattern, compare_op=cmp, fill=fill, base=base, channel_multiplier=cm), reads=[in_.buf], writes=[out.buf])

    def sigmoid(self, out, in_, tmp, sign=1.0):
        self.act(tmp, in_, AF.Exp, scale=-sign)
        self.act(tmp, tmp, AF.Ln, bias=1.0)
        self.act(out, tmp, AF.Exp, scale=-1.0)

    def rsqrt_ln(self, out, in_, scale, eps):
        self.act(out, in_, AF.Ln, bias=eps, scale=scale)
        self.act(out, out, AF.Exp, scale=-0.5)


def build(depth=DEPTH, phases=("gdn", "rwkv", "hgrn", "ffn"), dbg=False):
    nc = bass.Bass("TRN2", target_bir_lowering=False)
    mk = MK(nc)
    k = K(mk)

    def din(name, shape):
        return Buf(nc.dram_tensor(name, list(shape), F32, kind="ExternalInput").ap(), name)

    def dout(name, shape):
        return Buf(nc.dram_tensor(name, list(shape), F32, kind="ExternalOutput").ap(), name)

    def dscr(name, shape):
        return Buf(nc.dram_tensor(name, list(shape), F32, kind="Internal").ap(), name)

    L = DEPTH
    I = {}
    for name, shape in [
        ("xin", (NT, D)), ("st_gdn", (L, NS, HA, HD, HD)), ("st_conv", (L, NS, 3, 1152)), ("st_rwkv", (L, NS, HB, HD, HD)),
        ("st_shift", (L, NS, B_COLS)), ("st_hgrn", (L, NS, HC, HD, HD)),
        ("norm_mix", (L, D)), ("w_in", (L, D, IN_COLS)), ("w_out", (L, D, D)), ("gdn_conv_w", (L, 4, 1152)),
        ("gdn_a_log", (L, HA)), ("gdn_dt_bias", (L, HA)), ("gdn_norm_w", (L, HD)), ("rwkv_mu", (L, B_COLS)),
        ("rwkv_w0", (L, 384)), ("rwkv_w2", (L, 64, 384)), ("rwkv_a0", (L, 384)), ("rwkv_a2", (L, 64, 384)),
        ("rwkv_g2", (L, 128, 384)), ("rwkv_v0", (L - 1, 384)), ("rwkv_v1", (L - 1, 384, 32)), ("rwkv_v2", (L - 1, 32, 384)),
        ("rwkv_k_k", (L, 384)), ("rwkv_k_a", (L, 384)), ("rwkv_r_k", (L, 384)), ("rwkv_ln_w", (L, 384)), ("rwkv_ln_b", (L, 384)),
        ("hgrn_lower_bounds", (L, 256)), ("hgrn_norm_w", (L, HD)), ("norm_ffn", (L, D)),
        ("w_gate", (L, D, DFF)), ("w_up", (L, D, DFF)), ("w_down", (L, DFF, D)), ("norm_final", (1, D)),
    ]:
        I[name] = din(name, shape)
    O = {}
    for name, shape in [
        ("y", (NT, D)), ("gdn_p", (L, HA, HD, HD)), ("gdn_s", (L, NS, HA, HD, HD)), ("conv_p", (L, 3, 1152)), ("conv_s", (L, NS, 3, 1152)),
        ("rwkv_p", (L, HB, HD, HD)), ("rwkv_s", (L, NS, HB, HD, HD)), ("shift_p", (L, B_COLS)), ("shift_s", (L, NS, B_COLS)),
        ("hgrn_p", (L, HC, HD, HD)), ("hgrn_s", (L, NS, HC, HD, HD)),
    ]:
        O[name] = dout(name, shape)
    if dbg:
        O["dbg_ya"] = dout("dbg_ya", (NT, 384))
        O["dbg_yb"] = dout("dbg_yb", (NT, 384))
        O["dbg_yc"] = dout("dbg_yc", (NT, 256))
        O["dbg_x1"] = dout("dbg_x1", (NT, D))
    xres = nc.dram_tensor("xres", [NT, D], F32, kind="Internal").ap()
    tiles = [(i * 128, 128) for i in range(NCH)] + [(T, NS)]
    xb = [Buf(xres[t0:t0 + n, :], f"xres{i}") for i, (t0, n) in enumerate(tiles)]
    vfirst = nc.dram_tensor("vfirst", [NT, 384], F32, kind="Internal").ap()
    vfb = [Buf(vfirst[t0:t0 + n, :], f"vf{i}") for i, (t0, n) in enumerate(tiles)]
    bounce = dscr("bounce", (NS, 6 * 200))
    bounceB = dscr("bounceB", (NS, 6 * 392))
    bounceC = dscr("bounceC", (NS, 4 * 256))
    bounce2 = dscr("bounce2", (NS * 6, 64))
    bounce2C = dscr("bounce2C", (NS * 4, 64))

    arena = Arena(mk, 206 * 256)
    banks = [mk.ps([128, 512], F32, name=f"bank{i}") for i in range(8)]

    def pview(b, c0, c1, p=128, name=""):
        return Buf(banks[b].t[0:p, c0:c1], name)

    identF = arena.f32([128, 128]); identB = arena.bf16([128, 128])
    Utri = arena.f32([128, 128]); ones = arena.f32([128, 128])
    negmaskT = arena.f32([128, 128]); posmaskS = arena.f32([128, 128])
    smT = arena.f32([128, 128]); sm = arena.f32([128, 128]); cmT = arena.f32([128, 128])
    sel = arena.f32([6, 6, 128])
    k.memset("pool", ones.a, 1.0)
    k.asel(identF.a, ones.a, [[-1, 128]], ALU.is_equal, 0.0, 0, 1)
    k.cp("pool", identB.a, identF.a)
    k.asel(Utri.a, ones.a, [[1, 128]], ALU.is_ge, 0.0, 0, -1)
    k.memset("pool", negmaskT.a, 0.0)
    k.asel(negmaskT.a, negmaskT.a, [[1, 128]], ALU.is_ge, -30000.0, 0, -1)
    k.memset("pool", posmaskS.a, 0.0)
    k.asel(posmaskS.a, posmaskS.a, [[-1, 128]], ALU.is_gt, 30000.0, 0, 1)
    k.asel(smT.a, ones.a, [[1, 128]], ALU.is_gt, 0.0, 0, -1)
    k.asel(sm.a, ones.a, [[-1, 128]], ALU.is_gt, 0.0, 0, 1)
    k.asel(cmT.a, ones.a, [[1, 128]], ALU.is_ge, 0.0, 0, -1)
    k.memset("pool", sel.a, 1.0)
    k.asel(sel.a, sel.a, [[-1, 6], [0, 128]], ALU.is_equal, 0.0, 0, 1)

    hT = arena.bf16([128, 8, HOFF + NT], "hT")
    k.memset("pool", hT[:, :, 0:HOFF], 0.0)

    for i, (t0, n) in enumerate(tiles):
        mk.dma("sp", xres[t0:t0 + n, :], I["xin"].t[t0:t0 + n, :], reads=[I["xin"]], writes=[xb[i]])

    persist_mark = arena.off

    def norm_phase(wname, l):
        m0 = arena.off
        wbc = arena.f32([128, D])
        mk.dma("sp", wbc.t, I[wname].t[l:l + 1, :].to_broadcast([128, D]), reads=[I[wname]], writes=[wbc])
        ss = arena.f32([128, 32])
        k.memset("pool", ss.a, 0.0)
        xts = [arena.f32([128, D]) for _ in range(len(tiles))]
        junk = arena.bf16([128, D])
        hb = [arena.bf16([128, D]) for _ in range(2)]
        trp = [Ref(banks[j], banks[j].t[:, :].bitcast(BF16)) for j in range(8)]
        for i, (t0, n) in enumerate(tiles):
            mk.dma("sp" if i % 2 == 0 else "act", xts[i].t[0:n, :], xres[t0:t0 + n, :], reads=[xb[i]], writes=[xts[i]])
        for i, (t0, n) in enumerate(tiles):
            k.act(junk[0:n, :], xts[i][0:n, :], AF.Square, accum=ss[0:n, i:i + 1])
        k.rsqrt_ln(ss.a, ss.a, 1.0 / D, 1e-6)
        for i, (t0, n) in enumerate(tiles):
            xt = xts[i]
            h = hb[i % 2]
            k.stt(h[0:n, :], xt[0:n, :], ss[0:n, i:i + 1], wbc[0:n, :], ALU.mult, ALU.mult)
            tp = trp[i % 8]
            for c in range(8):
                k.tr(tp[:, c * 128:c * 128 + n], h[0:n, c * 128:(c + 1) * 128], identB[0:n, 0:n])
            src = tp.a.rr("p (c t) -> p c t", c=8)[:, :, 0:n]
            eng = "act" if i % 2 == 0 else "dve"
            k.cp(eng, hT[:, :, HOFF + t0:HOFF + t0 + n], src)
        arena.off = m0
        mk.barrier()

    S = dict(arena=arena, banks=banks, pview=pview, I=I, O=O, xres=xres, xb=xb, tiles=tiles, hT=hT, vfirst=vfirst, vfb=vfb,
             identF=identF, identB=identB, Utri=Utri, ones=ones, negmaskT=negmaskT, posmaskS=posmaskS, smT=smT, sm=sm, cmT=cmT,
             sel=sel, bounce=bounce, bounce2=bounce2, bounceB=bounceB, bounceC=bounceC, bounce2C=bounce2C, dbg=dbg, mk=mk, k=k)

    for l in range(depth):
        norm_phase("norm_mix", l)
        if "gdn" in phases:
            gdn_phase(S, l)
        if "rwkv" in phases:
            rwkv_phase(S, l)
        if "hgrn" in phases:
            hgrn_phase(S, l)
        if dbg and l == 0:
            for i, (t0, n) in enumerate(tiles):
                mk.dma("sp", O["dbg_x1"].t[t0:t0 + n, :], xres[t0:t0 + n, :], reads=[xb[i]], writes=[O["dbg_x1"]])
        if "ffn" in phases:
            ffn_phase(S, l, norm_phase)
    final_phase(S)
    mk.emit(mk.all_dma_events())
    return nc


def load_w(S, dst, src_ap, src_buf):
    S["mk"].dma("pool", dst.t, src_ap.rearrange("(c p) n -> p c n", p=128), reads=[src_buf], writes=[dst])


def tri_inverse(S, X, XT, sbp, psp, dest):
    k = S["k"]
    TT = sbp.get()
    k.tt("pool", TT.a, XT.a, S["identF"].a, ALU.add)
    P, PT = X, XT
    for st in range(1, 7):
        psP = psp.get()
        k.mm(psP.a, PT.a, P.a)
        if st < 6:
            psPT = psp.get()
            k.mm(psPT.a, P.a, PT.a)
        Pn = sbp.get()
        k.cp("act", Pn.a, psP.a)
        if st < 6:
            PTn = sbp.get()
            k.cp("dve", PTn.a, psPT.a)
        psT = psp.get()
        k.mm(psT.a, Pn.a, TT.a)
        TTn = sbp.get() if st < 6 else dest
        k.tt("dve", TTn.a, TT.a, psT.a, ALU.add)
        TT = TTn
        P = Pn
        if st < 6:
            PT = PTn
    return TT


def out_proj(S, yb, n, Wo, ne, ti, dbgname=None, dbgw=0):
    k = S["k"]; mk = S["mk"]; pview = S["pview"]
    t0, _ = S["tiles"][ti]
    tp = Ref(S["banks"][6], S["banks"][6].t[:, :].bitcast(BF16))
    for e in range(ne):
        k.tr(tp[:, e * 128:e * 128 + n], yb[0:n, e * 128:(e + 1) * 128], S["identB"][0:n, 0:n])
    yT = S["yT"]
    k.cp("act", yT[:, 0:ne, 0:n], tp.a.rr("p (c t) -> p c t", c=8)[:, 0:ne, 0:n])
    xt = S["xt_pool"].get()
    mk.dma("sp", xt.t[0:n, :], S["xres"][t0:t0 + n, :], reads=[S["xb"][ti]], writes=[xt])
    for dh in range(2):
        ps = S["banks"][1 + dh].a
        for e in range(ne):
            k.mm(ps[0:n, :], yT[:, e, 0:n], Wo[:, e, dh * 512:(dh + 1) * 512], start=(e == 0), stop=(e == ne - 1))
        k.tt("dve", xt[0:n, dh * 512:(dh + 1) * 512], xt[0:n, dh * 512:(dh + 1) * 512], ps[0:n, :], ALU.add)
    mk.dma("sp", S["xres"][t0:t0 + n, :], xt.t[0:n, :], reads=[xt], writes=[S["xb"][ti]])


def interleave(gB, gA):
    while gB is not None or gA is not None:
        if gB is not None:
            try:
                next(gB)
            except StopIteration:
                gB = None
        if gA is not None:
            try:
                next(gA)
            except StopIteration:
                gA = None


def hv(T6, h):
    return T6[h // 3][:, h % 3, :]


def grp_mm(S, psp, pairs_fn, evac_fn, w=128, p=128, ngrp=2, per=3):
    k = S["k"]
    for g in range(ngrp):
        pb = psp.get(per * w, p)
        for j in range(per):
            h = g * per + j
            prs = pairs_fn(h)
            for i, (a, b) in enumerate(prs):
                k.mm(pb[:, j * w:(j + 1) * w], a, b, start=(i == 0), stop=(i == len(prs) - 1))
        evac_fn(g, pb.rr("p (h t) -> p h t", t=w))


def tri_inverse6(S, X6, XT6, pools, psp, dest6):
    k = S["k"]
    P6p, PT6p, TFp, TBp = pools
    identb = S["identF"].a.un(1).bc([128, 3, 128])
    TF = TFp.get(); TB = TBp.get()
    for g in range(2):
        k.tt("dve", TB[g].a, XT6[g].a, identb, ALU.add)
    P6, PT6 = X6, XT6
    Pn6 = P6p.get()
    grp_mm(S, psp, lambda h: [(hv(PT6, h), hv(P6, h))], lambda g, ps: k.cp("act", Pn6[g].a, ps))
    yield
    PTn6 = PT6p.get()
    grp_mm(S, psp, lambda h: [(hv(P6, h), hv(PT6, h))], lambda g, ps: k.cp("act", PTn6[g].a, ps))
    yield
    P6, PT6 = Pn6, PTn6
    for st in range(2, 7):
        TBn = TBp.get()

        def ev(g, ps, TB=TB, TBn=TBn):
            k.tt("dve", TBn[g].a, TB[g].a, ps, ALU.add)
        grp_mm(S, psp, lambda h: [(hv(P6, h), hv(TB, h))], ev)
        yield
        Pn6 = P6p.get()
        grp_mm(S, psp, lambda h: [(hv(PT6, h), hv(P6, h))], lambda g, ps: k.cp("act", Pn6[g].a, ps))
        yield
        if st < 6:
            PTn6 = PT6p.get()
            grp_mm(S, psp, lambda h: [(hv(P6, h), hv(PT6, h))], lambda g, ps: k.cp("act", PTn6[g].a, ps))
            yield
            PT6 = PTn6
        TB = TBn
        P6 = Pn6
    TBd = TBp.get()
    grp_mm(S, psp, lambda h: [(hv(P6, h), hv(TB, h))], lambda g, ps: k.tt("dve", TBd[g].a, TB[g].a, ps, ALU.add))
    yield
    Tn6 = PT6p.get(); Rt6 = P6p.get()
    for g in range(2):
        pb_ = psp.get(512)
        ptr = Ref(pb_.buf, pb_.buf.t[:, :].bitcast(BF16))
        for j in range(3):
            k.tr(ptr[:, j * 128:(j + 1) * 128], TBd[g][:, j, :], S["identB"].a)
        k.cp("act", Tn6[g].a, ptr[:, 0:384].rr("p (h t) -> p h t", t=128))

    def evr(g, ps):
        k.tt("dve", TF[g].a, ps, TBd[g].a, ALU.subtract)
        k.tt("dve", Rt6[g].a, TF[g].a, identb, ALU.add)
    grp_mm(S, psp, lambda h: [(hv(X6, h), hv(TBd, h))], evr)
    yield
    grp_mm(S, psp, lambda h: [(hv(Tn6, h), hv(Rt6, h))], lambda g, ps: k.tt("dve", dest6[g].a, TBd[g].a, ps, ALU.add))
    yield
    return


def gdn_phase(S, l):
    k = S["k"]; mk = S["mk"]; arena = S["arena"]; I = S["I"]; O = S["O"]; banks = S["banks"]; hT = S["hT"]
    m0 = arena.off
    identF = S["identF"]
    Wqkv = arena.bf16([128, 8, 1152]); Wz = arena.bf16([128, 8, 384]); Wba = arena.bf16([128, 8, 12]); Wo = arena.bf16([128, 3, 1024])
    load_w(S, Wqkv, I["w_in"].t[l, :, 0:1152], I["w_in"])
    load_w(S, Wz, I["w_in"].t[l, :, 1152:1536], I["w_in"])
    load_w(S, Wba, I["w_in"].t[l, :, 1536:1548], I["w_in"])
    load_w(S, Wo, I["w_out"].t[l, 0:384, :], I["w_out"])
    wc = arena.f32([128, 4, 1152])
    mk.dma("sp", wc.t, I["gdn_conv_w"].t[l:l + 1, :, :].to_broadcast([128, 4, 1152]), reads=[I["gdn_conv_w"]], writes=[wc])
    negA = arena.f32([128, 6]); dtb = arena.f32([128, 6]); gnw = arena.f32([128, 64])
    mk.dma("sp", negA.t, I["gdn_a_log"].t[l:l + 1, :].to_broadcast([128, 6]), reads=[I["gdn_a_log"]], writes=[negA])
    mk.dma("sp", dtb.t, I["gdn_dt_bias"].t[l:l + 1, :].to_broadcast([128, 6]), reads=[I["gdn_dt_bias"]], writes=[dtb])
    mk.dma("sp", gnw.t, I["gdn_norm_w"].t[l:l + 1, :].to_broadcast([128, 64]), reads=[I["gdn_norm_w"]], writes=[gnw])
    k.act(negA.a, negA.a, AF.Exp)
    k.ts("pool", negA.a, negA.a, -1.0, ALU.mult)
    St = arena.f32([64, 6, 64], "gdnS")
    k.memset("pool", St.a, 0.0)
    Sth = [Buf(St.t[:, h, :], f"gdnS{h}") for h in range(6)]
    S["yT"] = arena.bf16([128, 8, 128])
    S["xt_pool"] = Pool_([arena.f32([128, D]) for _ in range(2)])
    acc = arena.f32([128, 384]); tmp = arena.f32([128, 384])
    qkv = arena.f32([128, 1152]); zsA = [arena.f32([128, 384]) for _ in range(2)]; zt = arena.f32([128, 384])
    zs = zsA[0]
    sqB = arena.f32([128, 384]); rsB = arena.f32([128, 6])
    sm12 = arena.f32([128, 12]); beta = arena.f32([128, 6]); g = arena.f32([128, 6]); t6 = arena.f32([128, 6])
    gc = arena.f32([128, 6]); ngc = arena.f32([128, 6]); eg = arena.f32([128, 6]); egl = arena.f32([128, 6]); ekl = arena.f32([128, 6])
    gT = arena.f32([6, 128])
    sq = arena.f32([128, 768]); rs = arena.f32([128, 12])
    qkn = arena.f32([128, 768])
    o = arena.f32([128, 384]); yb = arena.bf16([128, 384]); y32 = arena.f32([128, 384])
    convo = arena.f32([128, 1152])
    mchunk = arena.off
    bk = arena.bf16([128, 384]); qd = arena.bf16([128, 384])
    bvA = [arena.bf16([128, 384]) for _ in range(2)]; bkeA = [arena.bf16([128, 384]) for _ in range(2)]; kdA = [arena.bf16([128, 384]) for _ in range(2)]
    knT = arena.bf16([64, 6, 128]); bkT = arena.bf16([64, 6, 128]); qnT = arena.bf16([64, 6, 128])
    qdTA = [arena.bf16([64, 6, 128]) for _ in range(2)]
    eglA = [arena.f32([128, 6]) for _ in range(2)]
    def mk6(dt=F32):
        return [(arena.f32 if dt == F32 else arena.bf16)([128, 3, 128]) for _ in range(2)]
    DTc6 = mk6(); Ds6 = mk6(); DTs6 = mk6(); AT6A = [mk6(BF16) for _ in range(2)]; TTd6 = mk6(BF16)
    XT6A = [mk6(BF16) for _ in range(2)]
    X6p = Pool_([mk6(BF16) for _ in range(2)]); XT6p = Pool_([mk6(BF16) for _ in range(3)])
    TFp = Pool_([mk6() for _ in range(3)]); TBp = Pool_([mk6(BF16) for _ in range(3)])
    X60A = [mk6(BF16) for _ in range(2)]
    nwT6 = arena.bf16([64, 6, 128]); vn6 = arena.bf16([128, 384])
    Stb = arena.bf16([64, 6, 64]); k.memset("pool", Stb.a, 0.0)
    qknb = arena.bf16([128, 768])
    SA = [arena.bf16([128, 128]) for _ in range(3)]; SB = [arena.bf16([128, 128]) for _ in range(3)]
    onesb = arena.bf16([128, 128]); k.cp("pool", onesb.a, S["ones"].a)
    for s_ in (1, 2, 3):
        k.asel(SA[s_ - 1].a, onesb.a, [[1, 128]], ALU.is_equal, 0.0, -s_, -1)
        k.asel(SB[s_ - 1].a, onesb.a, [[1, 128]], ALU.is_equal, 0.0, 128 - s_, -1)
    pbA = [arena.bf16([128, 1152]) for _ in range(2)]
    k.memset("pool", pbA[1].a, 0.0)
    pb_par = [0]
    negmaskTb = arena.bf16([128, 128]); posmaskSb = arena.bf16([128, 128])
    k.cp("pool", negmaskTb.a, S["negmaskT"].a); k.cp("pool", posmaskSb.a, S["posmaskS"].a)
    gcp = arena.f32([128, 38]); G38 = arena.f32([38, 128]); Rm = arena.f32([38, 128]); Lm = arena.f32([38, 6, 128]); selhi = arena.f32([38, 6, 128])
    k.memset("pool", gcp.a, 0.0); k.memset("pool", Rm.a, 0.0); k.memset("pool", Lm.a, 0.0)
    k.memset("pool", Rm[32:38, :], 1.0)
    mk.dma("sp", Lm.t[0:6, :, :], S["sel"].t, reads=[S["sel"]], writes=[Lm])
    mk.dma("sp", selhi.t[32:38, :, :], S["sel"].t, reads=[S["sel"]], writes=[selhi])
    pconv = [Ref(banks[j], banks[j].t[:, 0:384]) for j in range(4)]
    bankpool = Pool_([banks[i] for i in (4, 5, 6, 7, 0, 1, 2, 3)])

    class PSP:
        def get(self, w=128, p=128):
            b = bankpool.get()
            return Ref(b, b.t[0:p, 0:w])
    psp = PSP()

    def prologue(n, col0, is_sample, zs=None):
        zs = zs if zs is not None else zsA[0]
        import os
        PSTOP = int(os.environ.get("GDN_PSTOP", "9")) if is_sample else 9
        if not is_sample:
            pcur = pbA[pb_par[0] % 2]; pprev = pbA[(pb_par[0] + 1) % 2]
            pb_par[0] += 1
            for cg in range(3):
                cs = slice(cg * 384, (cg + 1) * 384)
                bb = banks[cg]
                for c in range(8):
                    k.mm(bb[0:n, 0:384], hT[:, c, col0:col0 + n], Wqkv[:, c, cs], start=(c == 0), stop=(c == 7))
                k.cp("act", pcur[:, cs], bb[:, 0:384])
                if col0 == HOFF + T - 128:
                    k.cp("act", convo[96:128, cs], bb[96:128, 0:384])
            for cg in range(3):
                cs = slice(cg * 384, (cg + 1) * 384)
                bb = banks[cg]
                for j in range(3):
                    sh_ = 3 - j
                    k.mm(banks[3 + j][:, 0:384], SA[sh_ - 1].a, pcur[:, cs], start=True, stop=False)
                    k.mm(banks[3 + j][:, 0:384], SB[sh_ - 1].a, pprev[:, cs], start=False, stop=True)
                k.tt("dve", acc[0:n, :], bb[0:n, 0:384], wc[0:n, 3, cs], ALU.mult)
                for j in range(3):
                    k.tt("dve", tmp[0:n, :], banks[3 + j][0:n, 0:384], wc[0:n, j, cs], ALU.mult)
                    k.tt("dve", acc[0:n, :], acc[0:n, :], tmp[0:n, :], ALU.add)
                k.sigmoid(tmp[0:n, :], acc[0:n, :], tmp[0:n, :])
                k.tt("dve", qkv[0:n, cs], acc[0:n, :], tmp[0:n, :], ALU.mult)
            yield
        for cg in range(3):
            if PSTOP <= 0 or not is_sample:
                break
            cs = slice(cg * 384, (cg + 1) * 384)
            for c in range(8):
                k.mm(pconv[3][0:n, :], hT[:, c, col0:col0 + n], Wqkv[:, c, cs], start=(c == 0), stop=(c == 7))
            k.cp("act", convo[0:n, cs], pconv[3][0:n, :])
            k.tt("dve", acc[0:n, :], pconv[3][0:n, :], wc[0:n, 3, cs], ALU.mult)
            for j in range(3):
                k.tt("pool", tmp[0:n, :], S["cbuf"][0:n, j, cs], wc[0:n, j, cs], ALU.mult)
                k.tt("dve", acc[0:n, :], acc[0:n, :], tmp[0:n, :], ALU.add)
            k.sigmoid(tmp[0:n, :], acc[0:n, :], tmp[0:n, :])
            k.tt("dve", qkv[0:n, cs], acc[0:n, :], tmp[0:n, :], ALU.mult)
            yield
        if PSTOP <= 1:
            return
        for c in range(8):
            k.mm(pconv[0][0:n, :], hT[:, c, col0:col0 + n], Wz[:, c, :], start=(c == 0), stop=(c == 7))
        k.cp("act", zt[0:n, :], pconv[0][0:n, :])
        k.sigmoid(zs[0:n, :], zt[0:n, :], zs[0:n, :])
        k.tt("pool", zs[0:n, :], zs[0:n, :], zt[0:n, :], ALU.mult)
        yield
        if PSTOP <= 2:
            return
        pba = psp.get(12)
        for c in range(8):
            k.mm(pba[0:n, :], hT[:, c, col0:col0 + n], Wba[:, c, :], start=(c == 0), stop=(c == 7))
        k.cp("dve", sm12[0:n, :], pba[0:n, :])
        k.sigmoid(beta[0:n, :], sm12[0:n, 0:6], t6[0:n, :])
        k.tt("dve", t6[0:n, :], sm12[0:n, 6:12], dtb[0:n, :], ALU.add)
        k.act(t6[0:n, :], t6[0:n, :], AF.Exp)
        k.act(t6[0:n, :], t6[0:n, :], AF.Ln, bias=1.0)
        k.tt("dve", g[0:n, :], t6[0:n, :], negA[0:n, :], ALU.mult)
        yield
        if PSTOP <= 3:
            return
        k.tt("dve", sq[0:n, :], qkv[0:n, 0:768], qkv[0:n, 0:768], ALU.mult)
        k.red(rs[0:n, :], sq[0:n, :].rr("p (h d) -> p h d", d=64))
        k.rsqrt_ln(rs[0:n, :], rs[0:n, :], 1.0, 1e-6)
        k.ts("dve", rs[0:n, 0:6], rs[0:n, 0:6], 0.125, ALU.mult)
        k.tt("dve", qkn[0:n, :].rr("p (h d) -> p h d", d=64), qkv[0:n, 0:768].rr("p (h d) -> p h d", d=64),
             rs[0:n, :].un(2).bc([n, 12, 64]), ALU.mult)
        yield

    def h3(ref, n):
        return ref[0:n, :].rr("p (h d) -> p h d", d=64)

    def b3(ref, n):
        return ref[0:n, :].un(2).bc([n, 6, 64])

    def epilogue(n, ti, row0, zs=None):
        zs = zs if zs is not None else zsA[0]
        k.tt("dve", sqB[0:n, 0:384], o[0:n, :], o[0:n, :], ALU.mult)
        k.red(rsB[0:n, 0:6], sqB[0:n, 0:384].rr("p (h d) -> p h d", d=64))
        k.rsqrt_ln(rsB[0:n, 0:6], rsB[0:n, 0:6], 1.0 / 64, 1e-6)
        k.tt("dve", h3(y32, n), h3(o, n), b3(rsB[:, 0:6], n), ALU.mult)
        k.tt("dve", h3(y32, n), h3(y32, n), gnw[0:n, :].un(1).bc([n, 6, 64]), ALU.mult)
        k.tt("dve", yb[0:n, :], y32[0:n, :], zs[0:n, :], ALU.mult)
        if S["dbg"] and l == 0:
            k.tt("pool", y32[0:n, :], y32[0:n, :], zs[0:n, :], ALU.mult)
            mk.dma("sp", O["dbg_ya"].t[row0:row0 + n, :], y32.t[0:n, :], reads=[y32], writes=[O["dbg_ya"]])
        out_proj(S, yb, n, Wo, 3, ti)

    def chunkA(ci):
        n = 128
        par = ci % 2
        zs_ = zsA[par]; bv = bvA[par]; bke = bkeA[par]; kd = kdA[par]; qdT = qdTA[par]; egl = eglA[par]
        X6 = X60A[par]; XT6 = XT6A[par]; AT6 = AT6A[par]
        yield from prologue(n, HOFF + ci * 128, False, zs_)
        pg = psp.get(12)
        k.mm(pg[:, 0:6], S["Utri"].a, g.a)
        k.mm(pg[:, 6:12], S["ones"].a, g.a)
        k.cp("dve", gc.a, pg[:, 0:6])
        k.ts("dve", ngc.a, gc.a, -1.0, ALU.mult)
        k.act(eg.a, pg[:, 0:6], AF.Exp)
        k.act(egl.a, pg[:, 6:12], AF.Exp)
        k.tt("dve", ekl.a, pg[:, 6:12], gc.a, ALU.subtract)
        k.act(ekl.a, ekl.a, AF.Exp)
        k.cp("act", gcp[:, 0:6], gc.a)
        k.cp("act", gcp[:, 32:38], ngc.a)
        pgT = psp.get(128, 38)
        k.tr(pgT.a, gcp.a, identF.a)
        k.cp("act", G38.a, pgT.a)
        k.cp("act", Rm[0:6, :], pgT[0:6, :])
        yield
        kn = qkn[:, 384:768]; qn = qkn[:, 0:384]; v = qkv[:, 768:1152]
        k.cp("act", qknb.a, qkn.a)
        k.tt("dve", h3(bk, n), kn.rr("p (h d) -> p h d", d=64), b3(beta, n), ALU.mult)
        k.tt("dve", h3(qd, n), qn.rr("p (h d) -> p h d", d=64), b3(eg, n), ALU.mult)
        k.tt("pool", h3(bv, n), v.rr("p (h d) -> p h d", d=64), b3(beta, n), ALU.mult)
        k.tt("dve", h3(bke, n), h3(bk, n), b3(eg, n), ALU.mult)
        k.tt("dve", h3(kd, n), kn.rr("p (h d) -> p h d", d=64), b3(ekl, n), ALU.mult)
        yield
        for (src, dst, eng) in ((qknb[:, 384:768], knT, "act"), (bk.a, bkT, "dve"), (qknb[:, 0:384], qnT, "act"), (qd.a, qdT, "dve")):
            pb_ = bankpool.get()
            ptr = Ref(pb_, pb_.t[:, :].bitcast(BF16))
            for h in range(6):
                k.tr(ptr[0:64, h * 128:(h + 1) * 128], src[:, h * 64:(h + 1) * 64], S["identB"].a)
            k.cp(eng, dst.a, ptr[0:64, 0:768].rr("p (h t) -> p h t", t=128))
            yield
        k.tt("pool", Lm[32:38, :, :], G38[32:38, :].un(1).bc([6, 6, 128]), selhi[32:38, :, :], ALU.mult)
        grp_mm(S, psp, lambda h: [(Lm[:, h, :], Rm.a), (S["identB"].a, negmaskTb.a)],
               lambda g, ps: k.act(DTc6[g].a, ps, AF.Exp))
        yield
        grp_mm(S, psp, lambda h: [(Lm[:, h, :], Rm.a), (S["identB"].a, posmaskSb.a)],
               lambda g, ps: k.act(Ds6[g].a, ps, AF.Exp, scale=-1.0))
        for gi_ in range(2):
            k.tt("dve", DTs6[gi_].a, DTc6[gi_].a, S["smT"].a.un(1).bc([128, 3, 128]), ALU.mult)
        yield
        grp_mm(S, psp, lambda h: [(bkT[:, h, :], knT[:, h, :])],
               lambda g, ps: k.stt(X6[g].a, ps, -1.0, Ds6[g].a, ALU.mult, ALU.mult))
        yield
        grp_mm(S, psp, lambda h: [(knT[:, h, :], bkT[:, h, :])],
               lambda g, ps: k.stt(XT6[g].a, ps, -1.0, DTs6[g].a, ALU.mult, ALU.mult))
        yield
        grp_mm(S, psp, lambda h: [(knT[:, h, :], qnT[:, h, :])],
               lambda g, ps: k.tt("dve", AT6[g].a, ps, DTc6[g].a, ALU.mult))
        yield

    def chunkB(ci):
        n = 128
        par = ci % 2
        zs_ = zsA[par]; bv = bvA[par]; bke = bkeA[par]; kd = kdA[par]; qdT = qdTA[par]; egl = eglA[par]
        X6 = X60A[par]; XT6 = XT6A[par]; AT6 = AT6A[par]
        yield from tri_inverse6(S, X6, XT6, (X6p, XT6p, TFp, TBp), psp, TTd6)
        TT6 = TTd6
        grp_mm(S, psp, lambda h: [(bke[:, h * 64:(h + 1) * 64], hv(TT6, h))],
               lambda g, ps: k.act(nwT6[:, g * 3:(g + 1) * 3, :], ps, AF.Copy, scale=-1.0), p=64)
        yield
        grp_mm(S, psp, lambda h: [(nwT6[:, h, :], Stb[:, h, :]), (hv(TT6, h), bv[:, h * 64:(h + 1) * 64])],
               lambda g, ps: k.cp("act", vn6.a.rr("p (h d) -> p h d", d=64), ps), w=64, ngrp=1, per=6)
        yield
        grp_mm(S, psp, lambda h: [(qdT[:, h, :], Stb[:, h, :]), (hv(AT6, h), vn6[:, h * 64:(h + 1) * 64])],
               lambda g, ps: k.cp("act", o.a.rr("p (h d) -> p h d", d=64), ps), w=64, ngrp=1, per=6)
        yield

        def st_upd(g, ps):
            k.tt("dve", St.a, St.a, egl[0:64, :].un(2).bc([64, 6, 64]), ALU.mult)
            k.tt("dve", St.a, St.a, ps, ALU.add)
            k.cp("act", Stb.a, St.a)
        grp_mm(S, psp, lambda h: [(kd[:, h * 64:(h + 1) * 64], vn6[:, h * 64:(h + 1) * 64])], st_upd, w=64, p=64, ngrp=1, per=6)
        yield
        epilogue(n, ci, ci * 128, zs_)
        yield

    import os
    STOP = 9
    nch = int(os.environ.get("GDN_NCH", NCH))
    interleave(None, chunkA(0))
    for ci in range(nch):
        interleave(chunkB(ci), chunkA(ci + 1) if ci + 1 < nch else None)
    mk.dma("sp", O["gdn_p"].t[l].rearrange("h k v -> k h v"), St.t, reads=[St], writes=[O["gdn_p"]])
    mk.dma("sp", O["conv_p"].t[l], convo.t[125:128, :], reads=[convo], writes=[O["conv_p"]])
    if STOP <= 5:
        arena.off = m0
        mk.barrier()
        return
    mk.barrier()
    arena.off = mchunk
    gdn_sample(S, l, locals())
    arena.off = m0
    mk.barrier()


def gdn_sample(S, l, loc):
    k = S["k"]; mk = S["mk"]; arena = S["arena"]; I = S["I"]; O = S["O"]
    prologue = loc["prologue"]; epilogue = loc["epilogue"]
    qkn = loc["qkn"]; qkv = loc["qkv"]; beta = loc["beta"]; g = loc["g"]; eg = loc["eg"]; o = loc["o"]; convo = loc["convo"]
    n = NS
    P = NS * 6
    Ss = arena.f32([P, 64, 64])
    mk.dma("act", Ss.t, I["st_gdn"].t[l].rearrange("s h k v -> (s h) k v"), reads=[I["st_gdn"]], writes=[Ss])
    cbuf = arena.f32([NS, 3, 1152])
    S["cbuf"] = cbuf
    import os
    SUB = os.environ.get("GDN_SUB", "")
    mk.dma("sp", cbuf.t, I["st_conv"].t[l], reads=[I["st_conv"]], writes=[cbuf])
    if SUB != "a":
        for _ in prologue(n, HOFF + T, True):
            pass
    if SUB != "b":
        mk.dma("sp", O["conv_s"].t[l, :, 0:2, :], I["st_conv"].t[l, :, 1:3, :], reads=[I["st_conv"]], writes=[O["conv_s"]])
        mk.dma("sp", O["conv_s"].t[l, :, 2, :], convo.t[0:n, :], reads=[convo], writes=[O["conv_s"]])
    import os
    STOP = int(os.environ.get("GDN_STOP", "9"))
    if STOP <= 6:
        return
    k.act(eg[0:n, :], g[0:n, :], AF.Exp)
    tm = arena.f32([NS, 6, 200])
    k.cp("dve", tm[:, :, 0:64], qkn[0:n, 0:384].rr("p (h d) -> p h d", d=64))
    k.cp("pool", tm[:, :, 64:128], qkn[0:n, 384:768].rr("p (h d) -> p h d", d=64))
    k.cp("dve", tm[:, :, 128:192], qkv[0:n, 768:1152].rr("p (h d) -> p h d", d=64))
    k.cp("pool", tm[:, :, 192:193], beta[0:n, :].un(2))
    k.cp("pool", tm[:, :, 193:194], eg[0:n, :].un(2))
    bo = S["bounce"]
    mk.dma("sp", bo.t.rearrange("s (h c) -> s h c", h=6), tm.t, reads=[tm], writes=[bo])
    P = NS * 6
    sh = arena.f32([P, 200])
    mk.dma("sp", sh.t, bo.t.rearrange("s (h c) -> (s h) c", h=6), reads=[bo], writes=[sh])
    tmp = arena.f32([P, 64, 64])
    if STOP <= 7:
        return
    q = sh[:, 0:64]; kk = sh[:, 64:128]; v = sh[:, 128:192]; be = sh[:, 192:193]; egc = sh[:, 193:194]
    kS = arena.f32([P, 64]); vn = arena.f32([P, 64]); os_ = arena.f32([P, 64])
    k.tt("dve", tmp.a, Ss.a, kk.un(2).bc([P, 64, 64]), ALU.mult)
    k.red(kS.a, tmp.a.rr("p k v -> p v k"))
    k.ts("dve", kS.a, kS.a, egc, ALU.mult)
    k.tt("dve", vn.a, v, kS.a, ALU.subtract)
    k.ts("dve", vn.a, vn.a, be, ALU.mult)
    k.tt("pool", tmp.a, kk.un(2).bc([P, 64, 64]), vn.a.un(1).bc([P, 64, 64]), ALU.mult)
    k.stt(Ss.a, Ss.a, egc, tmp.a, ALU.mult, ALU.add)
    mk.dma("sp", O["gdn_s"].t[l].rearrange("s h k v -> (s h) k v"), Ss.t, reads=[Ss], writes=[O["gdn_s"]])
    k.tt("dve", tmp.a, Ss.a, q.un(2).bc([P, 64, 64]), ALU.mult)
    k.red(os_.a, tmp.a.rr("p k v -> p v k"))
    if STOP <= 8:
        return
    b2 = S["bounce2"]
    mk.dma("sp", b2.t, os_.t, reads=[os_], writes=[b2])
    mk.dma("sp", o.t[0:n, :], b2.t.rearrange("(s h) d -> s (h d)", h=6), reads=[b2], writes=[o])
    epilogue(n, NCH, T)


def rwkv_phase(S, l):
    k = S["k"]; mk = S["mk"]; arena = S["arena"]; I = S["I"]; O = S["O"]; banks = S["banks"]; hT = S["hT"]
    identF = S["identF"]
    m0 = arena.off
    Wb = arena.bf16([128, 8, B_COLS]); Wo = arena.bf16([128, 3, 1024])
    load_w(S, Wb, I["w_in"].t[l, :, B0:B0 + B_COLS], I["w_in"])
    load_w(S, Wo, I["w_out"].t[l, 384:768, :], I["w_out"])
    wa2 = arena.f32([128, 384]); g2 = arena.f32([128, 384])
    mk.dma("sp", wa2.t[0:64, :], I["rwkv_w2"].t[l], reads=[I["rwkv_w2"]], writes=[wa2])
    mk.dma("sp", wa2.t[64:128, :], I["rwkv_a2"].t[l], reads=[I["rwkv_a2"]], writes=[wa2])
    mk.dma("sp", g2.t, I["rwkv_g2"].t[l], reads=[I["rwkv_g2"]], writes=[g2])
    if l > 0:
        v1 = arena.f32([128, 3, 32]); v2 = arena.f32([32, 384])
        mk.dma("sp", v1.t, I["rwkv_v1"].t[l - 1].rearrange("(c p) r -> p c r", p=128), reads=[I["rwkv_v1"]], writes=[v1])
        mk.dma("sp", v2.t, I["rwkv_v2"].t[l - 1], reads=[I["rwkv_v2"]], writes=[v2])

    def bc_load(name, idx, w):
        t = arena.f32([128, w])
        mk.dma("sp", t.t, I[name].t[idx:idx + 1, :].to_broadcast([128, w]), reads=[I[name]], writes=[t])
        return t
    mu = bc_load("rwkv_mu", l, B_COLS); w0 = bc_load("rwkv_w0", l, 384); a0 = bc_load("rwkv_a0", l, 384)
    k_k = bc_load("rwkv_k_k", l, 384); k_a = bc_load("rwkv_k_a", l, 384); r_k = bc_load("rwkv_r_k", l, 384)
    ln_w = bc_load("rwkv_ln_w", l, 384); ln_b = bc_load("rwkv_ln_b", l, 384)
    v0 = bc_load("rwkv_v0", l - 1, 384) if l > 0 else None
    Um = arena.f32([128, 128]); ind2 = arena.f32([128, 2])
    k.memset("pool", ind2.a, 1.0)
    k.asel(ind2[:, 0:1], ind2[:, 0:1], [[0, 1]], ALU.is_gt, 0.0, 64, -1)
    k.asel(ind2[:, 1:2], ind2[:, 1:2], [[0, 1]], ALU.is_ge, 0.0, -64, 1)
    k.tt("pool", Um.a, S["Utri"].a, ind2[:, 0:1].bc([128, 128]), ALU.subtract)
    St = arena.f32([64, 6, 64]); k.memset("pool", St.a, 0.0)
    Sth = [Buf(St.t[:, h, :], f"rwS{h}") for h in range(6)]
    S["yT"] = arena.bf16([128, 8, 128])
    S["xt_pool"] = Pool_([arena.f32([128, D]) for _ in range(2)])
    pS = arena.f32([128, B_COLS]); xs = arena.f32([128, B_COLS]); dd = arena.f32([128, B_COLS])
    loraT = arena.f32([128, 2, 128]); ltmp = arena.f32([128, 2, 128])
    t384 = arena.f32([128, 384]); t2 = arena.f32([128, 384]); logw = arena.f32([128, 384]); aa = arena.f32([128, 384]); gg = arena.f32([128, 384])
    vv = arena.f32([128, 384]); kk = arena.f32([128, 384]); k2 = arena.f32([128, 384]); ka = arena.f32([128, 384])
    s6 = arena.f32([128, 6]); s6b = arena.f32([128, 6]); bs = arena.f32([128, 6]); rs = arena.f32([128, 6])
    vT = arena.f32([128, 3, 128]); vl = arena.f32([32, 128])
    Y = arena.f32([128, 384]); yb = arena.bf16([128, 384])
    mchunk = arena.off
    bh = arena.bf16([128, 384]); ah = arena.bf16([128, 384]); kh = arena.bf16([128, 384]); rh = arena.bf16([128, 384])
    E1 = arena.f32([128, 384]); vvb = arena.bf16([128, 384])
    bhT = arena.bf16([64, 6, 128]); ahT = arena.bf16([64, 6, 128]); khT = arena.bf16([64, 6, 128]); rhT = arena.bf16([64, 6, 128])
    elc = arena.f32([64, 12])

    def mk6(dt=F32):
        return [(arena.f32 if dt == F32 else arena.bf16)([128, 3, 128]) for _ in range(2)]
    ABKT6 = mk6(BF16); ARAT6 = mk6(BF16); ARKT6 = mk6(BF16); TTd6 = mk6(BF16); X60 = mk6(BF16)
    X6p = Pool_([mk6(BF16) for _ in range(2)]); XT6p = Pool_([mk6(BF16) for _ in range(3)])
    TFp = Pool_([mk6() for _ in range(3)]); TBp = Pool_([mk6(BF16) for _ in range(3)])
    S0pf = arena.f32([64, 6, 64]); S0p = arena.bf16([64, 6, 64]); S0pp = arena.f32([64, 6, 64]); Stmp = arena.f32([64, 6, 64])
    ru6 = arena.bf16([128, 384]); U6 = arena.bf16([128, 384])
    StT = arena.f32([64, 6, 64])
    bankpool = Pool_([banks[i] for i in (6, 7, 0, 1, 2, 3, 4, 5)])

    class PSP:
        def get(self, w=128, p=128):
            b = bankpool.get()
            return Ref(b, b.t[0:p, 0:w])
    psp = PSP()
    groups = [(0, 512), (512, 512), (1024, 384)]

    def h3(ref, n):
        return ref[0:n, :].rr("p (h d) -> p h d", d=64)

    def b3(ref, n):
        return ref[0:n, :].un(2).bc([n, 6, 64])

    def prologue(n, col0, ti, prev_sb=None):
        for gi, (c0, w) in enumerate(groups):
            pc = banks[gi]
            for c in range(8):
                k.mm(pc[0:n, 0:w], hT[:, c, col0:col0 + n], Wb[:, c, c0:c0 + w], start=(c == 0), stop=(c == 7))
            k.cp("act", pS[0:n, c0:c0 + w], pc[0:n, 0:w])
            if prev_sb is None:
                pp = banks[3 + gi]
                for c in range(8):
                    k.mm(pp[0:n, 0:w], hT[:, c, col0 - 1:col0 - 1 + n], Wb[:, c, c0:c0 + w], start=(c == 0), stop=(c == 7))
                k.tt("dve", dd[0:n, c0:c0 + w], pp[0:n, 0:w], pS[0:n, c0:c0 + w], ALU.subtract)
            else:
                k.tt("dve", dd[0:n, c0:c0 + w], prev_sb[0:n, c0:c0 + w], pS[0:n, c0:c0 + w], ALU.subtract)
            k.tt("dve", dd[0:n, c0:c0 + w], dd[0:n, c0:c0 + w], mu[0:n, c0:c0 + w], ALU.mult)
            k.tt("dve", xs[0:n, c0:c0 + w], dd[0:n, c0:c0 + w], pS[0:n, c0:c0 + w], ALU.add)
        pt = psp.get(256)
        k.tr(pt[:, 0:n], xs[0:n, 1152:1280], identF[0:n, 0:n])
        k.tr(pt[:, 128:128 + n], xs[0:n, 1280:1408], identF[0:n, 0:n])
        k.act(ltmp[0:64, 0, 0:n], pt[0:64, 0:n], AF.Exp, scale=-2.0)
        k.act(ltmp[0:64, 0, 0:n], ltmp[0:64, 0, 0:n], AF.Ln, bias=1.0)
        k.act(ltmp[0:64, 0, 0:n], ltmp[0:64, 0, 0:n], AF.Exp, scale=-1.0)
        k.ts("dve", loraT[0:64, 0, 0:n], ltmp[0:64, 0, 0:n], 2.0, ALU.mult, -1.0, ALU.add)
        k.cp("dve", loraT[64:128, 0, 0:n], pt[64:128, 0:n])
        k.act(ltmp[:, 1, 0:n], pt[:, 128:128 + n], AF.Exp, scale=-1.0)
        k.act(ltmp[:, 1, 0:n], ltmp[:, 1, 0:n], AF.Ln, bias=1.0)
        k.act(loraT[:, 1, 0:n], ltmp[:, 1, 0:n], AF.Exp, scale=-1.0)
        pw_ = psp.get(384); pa_ = psp.get(384); pg_ = psp.get(384)
        k.mm(pw_[0:n, :], loraT[0:64, 0, 0:n], wa2[0:64, :])
        k.mm(pa_[0:n, :], loraT[64:128, 0, 0:n], wa2[64:128, :])
        k.mm(pg_[0:n, :], loraT[:, 1, 0:n], g2.a)
        k.tt("dve", t384[0:n, :], pw_[0:n, :], w0[0:n, :], ALU.add)
        k.act(t384[0:n, :], t384[0:n, :], AF.Exp, scale=-1.0)
        k.act(t384[0:n, :], t384[0:n, :], AF.Ln, bias=1.0)
        k.act(t384[0:n, :], t384[0:n, :], AF.Exp, scale=-1.0, bias=-0.5)
        k.ts("dve", logw[0:n, :], t384[0:n, :], -1.0, ALU.mult)
        k.tt("dve", t2[0:n, :], pa_[0:n, :], a0[0:n, :], ALU.add)
        k.sigmoid(aa[0:n, :], t2[0:n, :], t2[0:n, :])
        k.cp("act", gg[0:n, :], pg_[0:n, :])
        r = xs[:, 0:384]; kx = xs[:, 384:768]; vx = xs[:, 768:1152]
        t0_, _ = S["tiles"][ti]
        if l == 0:
            k.cp("pool", vv[0:n, :], vx[0:n, :])
            mk.dma("sp", S["vfirst"][t0_:t0_ + n, :], vv.t[0:n, :], reads=[vv], writes=[S["vfb"][ti]])
        else:
            ptv = psp.get(384)
            for c in range(3):
                k.tr(ptv[:, c * 128:c * 128 + n], vx[0:n, c * 128:(c + 1) * 128], identF[0:n, 0:n])
            k.cp("act", vT[:, :, 0:n], ptv.a.rr("p (c t) -> p c t", c=3)[:, :, 0:n])
            pv1 = psp.get(128, 32)
            for c in range(3):
                k.mm(pv1[:, 0:n], v1[:, c, :], vT[:, c, 0:n], start=(c == 0), stop=(c == 2))
            k.cp("act", vl[:, 0:n], pv1[:, 0:n])
            pv2 = psp.get(384)
            k.mm(pv2[0:n, :], vl[:, 0:n], v2.a)
            k.tt("dve", t2[0:n, :], pv2[0:n, :], v0[0:n, :], ALU.add)
            k.sigmoid(t2[0:n, :], t2[0:n, :], t384[0:n, :])
            mk.dma("sp", vv.t[0:n, :], S["vfirst"][t0_:t0_ + n, :], reads=[S["vfb"][ti]], writes=[vv])
            k.tt("dve", vv[0:n, :], vv[0:n, :], vx[0:n, :], ALU.subtract)
            k.tt("pool", vv[0:n, :], vv[0:n, :], t2[0:n, :], ALU.mult)
            k.tt("pool", vv[0:n, :], vv[0:n, :], vx[0:n, :], ALU.add)
        k.tt("dve", kk[0:n, :], kx[0:n, :], k_k[0:n, :], ALU.mult)
        k.tt("dve", t384[0:n, :], kk[0:n, :], kk[0:n, :], ALU.mult)
        k.red(s6[0:n, :], h3(t384, n))
        k.rsqrt_ln(s6[0:n, :], s6[0:n, :], 1.0, 1e-6)
        k.tt("dve", h3(kk, n), h3(kk, n), b3(s6, n), ALU.mult)
        k.stt(t2[0:n, :], aa[0:n, :], -1.0, k_a[0:n, :], ALU.add, ALU.mult)
        k.stt(k2[0:n, :], t2[0:n, :], 1.0, kx[0:n, :], ALU.add, ALU.mult)
        k.tt("pool", t384[0:n, :], r[0:n, :], k2[0:n, :], ALU.mult)
        k.tt("pool", t384[0:n, :], t384[0:n, :], r_k[0:n, :], ALU.mult)
        k.red(bs[0:n, :], h3(t384, n))
        k.tt("dve", ka[0:n, :], kk[0:n, :], aa[0:n, :], ALU.mult)

    def epilogue(n, ti, row0):
        k.red(s6[0:n, :], h3(Y, n))
        k.tt("dve", t384[0:n, :], Y[0:n, :], Y[0:n, :], ALU.mult)
        k.red(s6b[0:n, :], h3(t384, n))
        k.ts("dve", s6[0:n, :], s6[0:n, :], 1.0 / 64, ALU.mult)
        k.tt("dve", rs[0:n, :], s6[0:n, :], s6[0:n, :], ALU.mult)
        k.stt(rs[0:n, :], s6b[0:n, :], 1.0 / 64, rs[0:n, :], ALU.mult, ALU.subtract)
        k.rsqrt_ln(rs[0:n, :], rs[0:n, :], 1.0, 64e-5)
        k.tt("dve", h3(t2, n), h3(Y, n), b3(s6, n), ALU.subtract)
        k.tt("dve", h3(t2, n), h3(t2, n), b3(rs, n), ALU.mult)
        k.tt("dve", t2[0:n, :], t2[0:n, :], ln_w[0:n, :], ALU.mult)
        k.tt("dve", t2[0:n, :], t2[0:n, :], ln_b[0:n, :], ALU.add)
        k.tt("dve", h3(t384, n), h3(vv, n), b3(bs, n), ALU.mult)
        k.tt("dve", t2[0:n, :], t2[0:n, :], t384[0:n, :], ALU.add)
        k.tt("dve", yb[0:n, :], t2[0:n, :], gg[0:n, :], ALU.mult)
        if S["dbg"] and l == 0:
            k.tt("pool", t2[0:n, :], t2[0:n, :], gg[0:n, :], ALU.mult)
            mk.dma("sp", O["dbg_yb"].t[row0:row0 + n, :], t2.t[0:n, :], reads=[t2], writes=[O["dbg_yb"]])
        out_proj(S, yb, n, Wo, 3, ti)

    import os
    for ci in range(int(os.environ.get("GDN_NCH", NCH))):
        n = 128
        prologue(n, HOFF + ci * 128, ci)
        if ci == NCH - 1:
            mk.dma("sp", O["shift_p"].t[l:l + 1, :], pS.t[127:128, :], reads=[pS], writes=[O["shift_p"]])
        r = xs[:, 0:384]
        pL = psp.get(384)
        k.mm(pL.a, Um.a, logw.a)
        k.tt("dve", E1.a, pL.a, logw.a, ALU.subtract)
        k.act(E1.a, E1.a, AF.Exp)
        k.tt("dve", bh.a, kk.a, E1.a, ALU.mult)
        k.act(E1.a, pL.a, AF.Exp, scale=-1.0)
        k.stt(ah.a, ka.a, -1.0, E1.a, ALU.mult, ALU.mult)
        k.tt("dve", kh.a, k2.a, E1.a, ALU.mult)
        k.act(t384.a, pL.a, AF.Exp)
        k.tt("dve", rh.a, r, t384.a, ALU.mult)
        k.cp("act", vvb.a, vv.a)
        pel = psp.get(12, 64)
        for h in range(6):
            k.mm(pel[:, 2 * h:2 * h + 2], logw[:, h * 64:(h + 1) * 64], ind2.a)
        k.act(elc.a, pel.a, AF.Exp)
        for (src, dst, eng) in ((bh, bhT, "act"), (ah, ahT, "dve"), (kh, khT, "act"), (rh, rhT, "dve")):
            pb_ = bankpool.get()
            ptr = Ref(pb_, pb_.t[:, :].bitcast(BF16))
            for h in range(6):
                k.tr(ptr[0:64, h * 128:(h + 1) * 128], src[:, h * 64:(h + 1) * 64], S["identB"].a)
            k.cp(eng, dst.a, ptr[0:64, 0:768].rr("p (h t) -> p h t", t=128))
        smb = S["sm"].a.un(1).bc([128, 3, 128]); smTb = S["smT"].a.un(1).bc([128, 3, 128]); cmTb = S["cmT"].a.un(1).bc([128, 3, 128])
        X6 = X60; XT6 = XT6p.get()
        grp_mm(S, psp, lambda h: [(bhT[:, h, :], ahT[:, h, :])], lambda g, ps: k.tt("dve", X6[g].a, ps, smb, ALU.mult))
        grp_mm(S, psp, lambda h: [(ahT[:, h, :], bhT[:, h, :])], lambda g, ps: k.tt("dve", XT6[g].a, ps, smTb, ALU.mult))
        grp_mm(S, psp, lambda h: [(khT[:, h, :], bhT[:, h, :])], lambda g, ps: k.tt("dve", ABKT6[g].a, ps, smTb, ALU.mult))
        grp_mm(S, psp, lambda h: [(ahT[:, h, :], rhT[:, h, :])], lambda g, ps: k.tt("dve", ARAT6[g].a, ps, cmTb, ALU.mult))
        grp_mm(S, psp, lambda h: [(khT[:, h, :], rhT[:, h, :])], lambda g, ps: k.tt("dve", ARKT6[g].a, ps, cmTb, ALU.mult))
        for _ in tri_inverse6(S, X6, XT6, (X6p, XT6p, TFp, TBp), psp, TTd6):
            pass
        TT6 = TTd6
        elc3 = elc.a.rr("p (h two) -> p h two", two=2)
        elm_b = elc3[:, :, 0:1].bc([64, 6, 64]); ecm_b = elc3[:, :, 1:2].bc([64, 6, 64])
        k.tt("dve", S0pf.a, St.a, elm_b, ALU.mult)
        k.cp("act", S0p.a, S0pf.a)
        k.tt("pool", S0pp.a, S0pf.a, ecm_b, ALU.mult)
        hs_ = lambda h: slice(h * 64, (h + 1) * 64)
        grp_mm(S, psp, lambda h: [(bhT[:, h, :], S0p[:, h, :]), (hv(ABKT6, h), vvb[:, hs_(h)])],
               lambda g, ps: k.cp("act", ru6.a.rr("p (h d) -> p h d", d=64), ps), w=64, ngrp=1, per=6)
        grp_mm(S, psp, lambda h: [(hv(TT6, h), ru6[:, hs_(h)])],
               lambda g, ps: k.cp("act", U6.a.rr("p (h d) -> p h d", d=64), ps), w=64, ngrp=1, per=6)
        grp_mm(S, psp, lambda h: [(rhT[:, h, :], S0p[:, h, :]), (hv(ARAT6, h), U6[:, hs_(h)]), (hv(ARKT6, h), vvb[:, hs_(h)])],
               lambda g, ps: k.cp("act", Y.a.rr("p (h d) -> p h d", d=64), ps), w=64, ngrp=1, per=6)

        def st_upd(g, ps):
            k.tt("dve", Stmp.a, ps, ecm_b, ALU.mult)
            k.tt("dve", St.a, Stmp.a, S0pp.a, ALU.add)
        grp_mm(S, psp, lambda h: [(ah[:, hs_(h)], U6[:, hs_(h)]), (kh[:, hs_(h)], vvb[:, hs_(h)])], st_upd, w=64, p=64, ngrp=1, per=6)
        epilogue(n, ci, ci * 128)
    for hh in range(0, 6, 4):
        nh = min(4, 6 - hh)
        ptr = psp.get(512, 64)
        for h in range(hh, hh + nh):
            k.tr(ptr[:, (h - hh) * 64:(h - hh + 1) * 64], St[:, h, :], identF[0:64, 0:64])
        k.cp("act", StT[:, hh:hh + nh, :], ptr[:, 0:nh * 64].rr("p (h t) -> p h t", t=64))
    mk.dma("sp", O["rwkv_p"].t[l].rearrange("h v k -> v h k"), StT.t, reads=[StT], writes=[O["rwkv_p"]])
    mk.barrier()
    arena.off = mchunk
    n = NS
    prev = arena.f32([NS, B_COLS])
    mk.dma("sp", prev.t, I["st_shift"].t[l], reads=[I["st_shift"]], writes=[prev])
    P = NS * 6
    Ss = arena.f32([P, 64, 64])
    mk.dma("act", Ss.t, I["st_rwkv"].t[l].rearrange("s h v k -> (s h) v k"), reads=[I["st_rwkv"]], writes=[Ss])
    prologue(n, HOFF + T, NCH, prev_sb=prev)
    mk.dma("sp", O["shift_s"].t[l], pS.t[0:n, :], reads=[pS], writes=[O["shift_s"]])
    wdec = arena.f32([NS, 384])
    k.act(wdec.a, logw[0:n, :], AF.Exp)
    tm = arena.f32([NS, 6, 392])
    srcs = [xs[0:n, 0:384], wdec.a, k2[0:n, :], vv[0:n, :], kk[0:n, :], ka[0:n, :]]
    for j, sref in enumerate(srcs):
        k.cp("dve" if j % 2 == 0 else "pool", tm[:, :, j * 64:(j + 1) * 64], sref.rr("p (h d) -> p h d", d=64))
    bo = S["bounceB"]
    mk.dma("sp", bo.t.rearrange("s (h c) -> s h c", h=6), tm.t, reads=[tm], writes=[bo])
    P = NS * 6
    sh = arena.f32([P, 392])
    mk.dma("sp", sh.t, bo.t.rearrange("s (h c) -> (s h) c", h=6), reads=[bo], writes=[sh])
    tmp = arena.f32([P, 64, 64])
    r_ = sh[:, 0:64]; w_ = sh[:, 64:128]; k2_ = sh[:, 128:192]; v_ = sh[:, 192:256]; kk_ = sh[:, 256:320]; ka_ = sh[:, 320:384]
    sa = arena.f32([P, 64]); ys = arena.f32([P, 64])
    rowb = lambda ref: ref.un(1).bc([P, 64, 64])
    colb = lambda ref: ref.un(2).bc([P, 64, 64])
    k.tt("dve", tmp.a, Ss.a, rowb(kk_), ALU.mult)
    k.red(sa.a, tmp.a)
    k.ts("dve", sa.a, sa.a, -1.0, ALU.mult)
    k.tt("dve", Ss.a, Ss.a, rowb(w_), ALU.mult)
    k.tt("pool", tmp.a, colb(sa.a), rowb(ka_), ALU.mult)
    k.tt("dve", Ss.a, Ss.a, tmp.a, ALU.add)
    k.tt("pool", tmp.a, colb(v_), rowb(k2_), ALU.mult)
    k.tt("dve", Ss.a, Ss.a, tmp.a, ALU.add)
    mk.dma("sp", O["rwkv_s"].t[l].rearrange("s h v k -> (s h) v k"), Ss.t, reads=[Ss], writes=[O["rwkv_s"]])
    k.tt("dve", tmp.a, Ss.a, rowb(r_), ALU.mult)
    k.red(ys.a, tmp.a)
    b2 = S["bounce2"]
    mk.dma("sp", b2.t, ys.t, reads=[ys], writes=[b2])
    mk.dma("sp", Y.t[0:n, :], b2.t.rearrange("(s h) d -> s (h d)", h=6), reads=[b2], writes=[Y])
    epilogue(n, NCH, T)
    arena.off = m0
    mk.barrier()


def hgrn_phase(S, l):
    k = S["k"]; mk = S["mk"]; arena = S["arena"]; I = S["I"]; O = S["O"]; banks = S["banks"]; hT = S["hT"]
    identF = S["identF"]
    m0 = arena.off
    Wc = arena.bf16([128, 8, C_COLS]); Wo = arena.bf16([128, 2, 1024])
    load_w(S, Wc, I["w_in"].t[l, :, C0:C0 + C_COLS], I["w_in"])
    load_w(S, Wo, I["w_out"].t[l, 768:1024, :], I["w_out"])
    lbr = arena.f32([128, 4, 256]); mx = arena.f32([128, 256]); oml = arena.f32([128, 256]); ssum = arena.f32([128, 256])
    mk.dma("sp", lbr.t, I["hgrn_lower_bounds"].t.rearrange("(o l) c -> o l c", o=1).to_broadcast([128, 4, 256]),
           reads=[I["hgrn_lower_bounds"]], writes=[lbr])
    k.tt("dve", mx.a, lbr[:, 0, :], lbr[:, 1, :], ALU.max)
    k.tt("dve", mx.a, mx.a, lbr[:, 2, :], ALU.max)
    k.tt("dve", mx.a, mx.a, lbr[:, 3, :], ALU.max)
    for j in range(4):
        k.tt("dve", lbr[:, j, :], lbr[:, j, :], mx.a, ALU.subtract)
    k.act(lbr.a, lbr.a, AF.Exp)
    k.tt("dve", ssum.a, lbr[:, 0, :], lbr[:, 1, :], ALU.add)
    k.tt("dve", ssum.a, ssum.a, lbr[:, 2, :], ALU.add)
    k.tt("dve", ssum.a, ssum.a, lbr[:, 3, :], ALU.add)
    k.recip(ssum.a, ssum.a)
    k.memset("pool", oml.a, 0.0)
    for j in range(1, l + 1):
        k.tt("dve", oml.a, oml.a, lbr[:, j, :], ALU.add)
    k.tt("dve", oml.a, oml.a, ssum.a, ALU.mult)
    k.ts("dve", oml.a, oml.a, -1.0, ALU.mult, 1.0, ALU.add)
    gnw = arena.f32([128, 64])
    mk.dma("sp", gnw.t, I["hgrn_norm_w"].t[l:l + 1, :].to_broadcast([128, 64]), reads=[I["hgrn_norm_w"]], writes=[gnw])
    Um1 = arena.f32([128, 128]); Um2 = arena.f32([128, 128]); bdm = arena.f32([128, 128])
    k.memset("pool", Um1.a, 0.0)
    k.memset("pool", Um1[0:32, 0:64], 1.0)
    k.memset("pool", Um1[0:96, 64:128], 1.0)
    k.tt("pool", Um1.a, S["Utri"].a, Um1.a, ALU.subtract)
    k.memset("pool", Um2.a, 0.0)
    k.memset("pool", Um2[0:64, :], 1.0)
    k.tt("pool", Um2.a, S["Utri"].a, Um2.a, ALU.subtract)
    k.cp("pool", bdm.a, S["cmT"].a)
    k.memset("pool", bdm[0:64, 64:128], 0.0)
    St = arena.f32([64, 4, 64]); k.memset("pool", St.a, 0.0)
    Sth = [Buf(St.t[:, h, :], f"hgS{h}") for h in range(4)]
    S["yT"] = arena.bf16([128, 8, 128])
    S["xt_pool"] = Pool_([arena.f32([128, D]) for _ in range(2)])
    qs = arena.f32([128, 256]); kf = arena.f32([128, 256]); logf = arena.f32([128, 256]); iv = arena.f32([128, 256]); gs = arena.f32([128, 256])
    t1 = arena.f32([128, 256]); t2 = arena.f32([128, 256])
    o = arena.f32([128, 256]); y32 = arena.f32([128, 256]); yb = arena.bf16([128, 256]); rs = arena.f32([128, 4])
    mchunk = arena.off
    q1 = arena.bf16([128, 256]); k1 = arena.bf16([128, 256]); q2 = arena.bf16([128, 256]); k2 = arena.bf16([128, 256])
    qS = arena.bf16([128, 256]); kE = arena.bf16([128, 256]); ee = arena.f32([128, 256]); ivb = arena.bf16([128, 256])
    Stb = arena.bf16([64, 4, 64]); k.memset("pool", Stb.a, 0.0)
    AT4 = arena.bf16([128, 4, 128]); AT4f = arena.f32([128, 4, 128])
    k.memset("pool", q2.a, 0.0); k.memset("pool", k2.a, 0.0)
    q1T = arena.bf16([64, 4, 128]); k1T = arena.bf16([64, 4, 128]); q2T = arena.bf16([64, 4, 128]); k2T = arena.bf16([64, 4, 128]); qST = arena.bf16([64, 4, 128])
    eend = arena.f32([64, 4])
    ATp = Pool_([arena.f32([128, 128]) for _ in range(4)])
    bankpool = Pool_([banks[i] for i in (2, 3, 4, 5, 6, 7, 0, 1)])

    class PSP:
        def get(self, w=128, p=128):
            b = bankpool.get()
            return Ref(b, b.t[0:p, 0:w])
    psp = PSP()

    def h3(ref, n):
        return ref[0:n, :].rr("p (h d) -> p h d", d=64)

    def b3(ref, n):
        return ref[0:n, :].un(2).bc([n, 4, 64])

    def prologue(n, col0):
        for gi in range(2):
            for c in range(8):
                k.mm(banks[gi][0:n, :], hT[:, c, col0:col0 + n], Wc[:, c, gi * 512:(gi + 1) * 512], start=(c == 0), stop=(c == 7))
        k.cp("act", t1[0:n, :], banks[0][0:n, 0:256])
        k.sigmoid(t2[0:n, :], t1[0:n, :], t2[0:n, :])
        k.stt(qs[0:n, :], t1[0:n, :], 0.125, t2[0:n, :], ALU.mult, ALU.mult)
        k.sigmoid(t2[0:n, :], banks[0][0:n, 256:512], t2[0:n, :], sign=-1.0)
        k.tt("pool", kf[0:n, :], t2[0:n, :], oml[0:n, :], ALU.mult)
        k.act(logf[0:n, :], kf[0:n, :], AF.Ln, bias=1.0, scale=-1.0)
        k.cp("act", iv[0:n, :], banks[1][0:n, 0:256])
        k.cp("act", t1[0:n, :], banks[1][0:n, 256:512])
        k.sigmoid(t2[0:n, :], t1[0:n, :], t2[0:n, :])
        k.tt("pool", gs[0:n, :], t1[0:n, :], t2[0:n, :], ALU.mult)

    def epilogue(n, ti, row0):
        k.tt("dve", t1[0:n, :], o[0:n, :], o[0:n, :], ALU.mult)
        k.red(rs[0:n, :], h3(t1, n))
        k.rsqrt_ln(rs[0:n, :], rs[0:n, :], 1.0 / 64, 1e-6)
        k.tt("dve", h3(y32, n), h3(o, n), b3(rs, n), ALU.mult)
        k.tt("dve", h3(y32, n), h3(y32, n), gnw[0:n, :].un(1).bc([n, 4, 64]), ALU.mult)
        k.tt("dve", yb[0:n, :], y32[0:n, :], gs[0:n, :], ALU.mult)
        if S["dbg"] and l == 0:
            k.tt("pool", y32[0:n, :], y32[0:n, :], gs[0:n, :], ALU.mult)
            mk.dma("sp", O["dbg_yc"].t[row0:row0 + n, :], y32.t[0:n, :], reads=[y32], writes=[O["dbg_yc"]])
        out_proj(S, yb, n, Wo, 2, ti)

    import os
    for ci in range(int(os.environ.get("GDN_NCH", NCH))):
        n = 128
        prologue(n, HOFF + ci * 128)
        pL1 = psp.get(256); pL2 = psp.get(256); pL = psp.get(256); pLe = psp.get(256)
        k.mm(pL1.a, Um1.a, logf.a)
        k.mm(pL2.a, Um2.a, logf.a)
        k.mm(pL.a, S["Utri"].a, logf.a)
        k.mm(pLe.a, S["sm"].a, logf.a)
        k.ts("dve", ee.a, pL1.a, 80.0, ALU.min)
        k.act(ee.a, ee.a, AF.Exp)
        k.tt("dve", q1.a, qs.a, ee.a, ALU.mult)
        k.ts("dve", ee.a, pL1.a, -1.0, ALU.mult, 80.0, ALU.min)
        k.act(ee.a, ee.a, AF.Exp)
        k.tt("dve", k1.a, kf.a, ee.a, ALU.mult)
        k.act(t1[64:128, :], pL2[64:128, :], AF.Exp)
        k.tt("pool", q2[64:128, :], qs[64:128, :], t1[64:128, :], ALU.mult)
        k.act(t1[0:64, :], pL2[0:64, :], AF.Exp, scale=-1.0)
        k.tt("pool", k2[0:64, :], kf[0:64, :], t1[0:64, :], ALU.mult)
        k.act(t2.a, pL.a, AF.Exp)
        k.tt("dve", qS.a, qs.a, t2.a, ALU.mult)
        k.act(t2.a, pLe.a, AF.Exp)
        k.tt("dve", kE.a, kf.a, t2.a, ALU.mult)
        pe_ = psp.get(4, 64)
        for h in range(4):
            k.mm(pe_[:, h:h + 1], logf[:, h * 64:(h + 1) * 64], S["ones"][:, 0:1])
        k.act(eend.a, pe_.a, AF.Exp)
        k.cp("act", ivb.a, iv.a)
        for (src, dst, eng) in ((q1, q1T, "act"), (k1, k1T, "dve"), (q2, q2T, "act"), (k2, k2T, "dve"), (qS, qST, "act")):
            pb_ = bankpool.get()
            ptr = Ref(pb_, pb_.t[:, :].bitcast(BF16))
            for h in range(4):
                k.tr(ptr[0:64, h * 128:(h + 1) * 128], src[:, h * 64:(h + 1) * 64], S["identB"].a)
            k.cp(eng, dst.a, ptr[0:64, 0:512].rr("p (h t) -> p h t", t=128))
        bdm4 = bdm.a.un(1).bc([128, 4, 128])
        grp_mm(S, psp, lambda h: [(k1T[:, h, :], q1T[:, h, :])], lambda g, ps: k.tt("dve", AT4f.a, ps, bdm4, ALU.mult), ngrp=1, per=4)
        grp_mm(S, psp, lambda h: [(k2T[:, h, :], q2T[:, h, :])], lambda g, ps: k.tt("dve", AT4.a, AT4f.a, ps, ALU.add), ngrp=1, per=4)
        hs_ = lambda h: slice(h * 64, (h + 1) * 64)
        grp_mm(S, psp, lambda h: [(qST[:, h, :], Stb[:, h, :]), (AT4[:, h, :], ivb[:, hs_(h)])],
               lambda g, ps: k.cp("act", o.a.rr("p (h d) -> p h d", d=64), ps), w=64, ngrp=1, per=4)

        def st_upd(g, ps):
            k.tt("dve", St.a, St.a, eend.a.un(2).bc([64, 4, 64]), ALU.mult)
            k.tt("dve", St.a, St.a, ps, ALU.add)
            k.cp("act", Stb.a, St.a)
        grp_mm(S, psp, lambda h: [(kE[:, hs_(h)], ivb[:, hs_(h)])], st_upd, w=64, p=64, ngrp=1, per=4)
        epilogue(n, ci, ci * 128)
    mk.dma("sp", O["hgrn_p"].t[l].rearrange("h d v -> d h v"), St.t, reads=[St], writes=[O["hgrn_p"]])
    mk.barrier()
    arena.off = mchunk
    n = NS
    P = NS * 4
    Ss = arena.f32([P, 64, 64])
    mk.dma("act", Ss.t, I["st_hgrn"].t[l].rearrange("s h d v -> (s h) d v"), reads=[I["st_hgrn"]], writes=[Ss])
    prologue(n, HOFF + T)
    fdec = arena.f32([NS, 256])
    k.act(fdec.a, logf[0:n, :], AF.Exp)
    tm = arena.f32([NS, 4, 256])
    for j, sref in enumerate([qs[0:n, :], kf[0:n, :], fdec.a, iv[0:n, :]]):
        k.cp("dve" if j % 2 == 0 else "pool", tm[:, :, j * 64:(j + 1) * 64], sref.rr("p (h d) -> p h d", d=64))
    bo = S["bounceC"]
    bo_v = bo.t
    mk.dma("sp", bo_v.rearrange("s (h c) -> s h c", h=4), tm.t, reads=[tm], writes=[bo])
    P = NS * 4
    sh = arena.f32([P, 256])
    mk.dma("sp", sh.t, bo_v.rearrange("s (h c) -> (s h) c", h=4), reads=[bo], writes=[sh])
    tmp = arena.f32([P, 64, 64])
    q_ = sh[:, 0:64]; k_ = sh[:, 64:128]; f_ = sh[:, 128:192]; i_ = sh[:, 192:256]
    os_ = arena.f32([P, 64])
    colb = lambda ref: ref.un(2).bc([P, 64, 64])
    rowb = lambda ref: ref.un(1).bc([P, 64, 64])
    k.tt("dve", Ss.a, Ss.a, colb(f_), ALU.mult)
    k.tt("pool", tmp.a, colb(k_), rowb(i_), ALU.mult)
    k.tt("dve", Ss.a, Ss.a, tmp.a, ALU.add)
    mk.dma("sp", O["hgrn_s"].t[l].rearrange("s h d v -> (s h) d v"), Ss.t, reads=[Ss], writes=[O["hgrn_s"]])
    k.tt("dve", tmp.a, Ss.a, colb(q_), ALU.mult)
    k.red(os_.a, tmp.a.rr("p d v -> p v d"))
    b2 = S["bounce2C"]
    mk.dma("sp", b2.t, os_.t, reads=[os_], writes=[b2])
    mk.dma("sp", o.t[0:n, :], b2.t.rearrange("(s h) d -> s (h d)", h=4), reads=[b2], writes=[o])
    epilogue(n, NCH, T)
    arena.off = m0
    mk.barrier()


def ffn_phase(S, l, norm_phase):
    k = S["k"]; mk = S["mk"]; arena = S["arena"]; I = S["I"]; O = S["O"]; banks = S["banks"]; hT = S["hT"]
    norm_phase("norm_ffn", l)
    m0 = arena.off
    quarters = [(0, 6), (6, 6), (12, 5), (17, 5)]
    wb = []
    for j in range(2):
        wb.append((arena.bf16([128, 8, 768]), arena.bf16([128, 8, 768]), arena.bf16([128, 6, 1024])))
    uTs = [arena.bf16([128, 6, 512]) for _ in range(2)]
    xts = Pool_([arena.f32([128, D]) for _ in range(3)])
    sgs = Pool_([arena.f32([128, 512]) for _ in range(2)])
    tgs = Pool_([arena.f32([128, 512]) for _ in range(2)])
    bankpool = Pool_([banks[i] for i in range(8)])
    blocks = [(0, 512), (512, 512), (1024, 512), (1536, 512), (T, NS)]

    def loadq(qi):
        f0, nf = quarters[qi]
        Wg, Wu, Wd = wb[qi % 2]
        mk.dma("pool", Wg.t[:, :, 0:nf * 128], I["w_gate"].t[l, :, f0 * 128:(f0 + nf) * 128].rearrange("(c p) n -> p c n", p=128),
               reads=[I["w_gate"]], writes=[Wg])
        mk.dma("pool", Wu.t[:, :, 0:nf * 128], I["w_up"].t[l, :, f0 * 128:(f0 + nf) * 128].rearrange("(c p) n -> p c n", p=128),
               reads=[I["w_up"]], writes=[Wu])
        mk.dma("pool", Wd.t[:, 0:nf, :], I["w_down"].t[l, f0 * 128:(f0 + nf) * 128, :].rearrange("(c p) n -> p c n", p=128),
               reads=[I["w_down"]], writes=[Wd])
    loadq(0)

    def gateup(qi, bi, uT):
        f0_, nf = quarters[qi]
        Wg, Wu, Wd = wb[qi % 2]
        t0, nb = blocks[bi]
        col0 = HOFF + t0
        for f in range(nf):
            pg = bankpool.get(); pu = bankpool.get()
            for c in range(8):
                k.mm(pg[:, 0:nb], Wg[:, c, f * 128:(f + 1) * 128], hT[:, c, col0:col0 + nb], start=(c == 0), stop=(c == 7))
            for c in range(8):
                k.mm(pu[:, 0:nb], Wu[:, c, f * 128:(f + 1) * 128], hT[:, c, col0:col0 + nb], start=(c == 0), stop=(c == 7))
            sg = sgs.get(); tg = tgs.get()
            k.cp("dve", tg[:, 0:nb], pg[:, 0:nb])
            k.sigmoid(sg[:, 0:nb], tg[:, 0:nb], sg[:, 0:nb])
            k.tt("dve", tg[:, 0:nb], tg[:, 0:nb], sg[:, 0:nb], ALU.mult)
            k.tt("dve", uT[:, f, 0:nb], tg[:, 0:nb], pu[:, 0:nb], ALU.mult)

    def down(qi, bi, uT):
        f0_, nf = quarters[qi]
        Wg, Wu, Wd = wb[qi % 2]
        t0, nb = blocks[bi]
        for ts_ in range(0, nb, 128):
            n = min(128, nb - ts_)
            ti = (t0 + ts_) // 128
            xt = xts.get()
            mk.dma("sp", xt.t[0:n, :], S["xres"][t0 + ts_:t0 + ts_ + n, :], reads=[S["xb"][ti]], writes=[xt])
            for dh in range(2):
                pd = bankpool.get()
                for f in range(nf):
                    k.mm(pd[0:n, :], uT[:, f, ts_:ts_ + n], Wd[:, f, dh * 512:(dh + 1) * 512], start=(f == 0), stop=(f == nf - 1))
                k.tt("dve", xt[0:n, dh * 512:(dh + 1) * 512], xt[0:n, dh * 512:(dh + 1) * 512], pd[0:n, :], ALU.add)
            mk.dma("sp", S["xres"][t0 + ts_:t0 + ts_ + n, :], xt.t[0:n, :], reads=[xt], writes=[S["xb"][ti]])

    items = [(qi, bi) for qi in range(4) for bi in range(len(blocks))]
    loadq(1)
    gateup(items[0][0], items[0][1], uTs[0])
    for idx, (qi, bi) in enumerate(items):
        if idx + 1 < len(items):
            gateup(items[idx + 1][0], items[idx + 1][1], uTs[(idx + 1) % 2])
        down(qi, bi, uTs[idx % 2])
        if bi == len(blocks) - 1 and qi + 2 < 4:
            loadq(qi + 2)
    arena.off = m0
    mk.barrier()


def final_phase(S):
    k = S["k"]; mk = S["mk"]; arena = S["arena"]; I = S["I"]; O = S["O"]
    m0 = arena.off
    wbc = arena.f32([128, D])
    mk.dma("sp", wbc.t, I["norm_final"].t[0:1, :].to_broadcast([128, D]), reads=[I["norm_final"]], writes=[wbc])
    ss = arena.f32([128, 32])
    k.memset("pool", ss.a, 0.0)
    xts = [arena.f32([128, D]) for _ in range(3)]
    junk = arena.bf16([128, D])
    ys = [arena.f32([128, D]) for _ in range(2)]
    tiles = S["tiles"]
    for i, (t0, n) in enumerate(tiles):
        xt = xts[i % 3]
        mk.dma("sp", xt.t[0:n, :], S["xres"][t0:t0 + n, :], reads=[S["xb"][i]], writes=[xt])
        k.act(junk[0:n, :], xt[0:n, :], AF.Square, accum=ss[0:n, i:i + 1])
    k.rsqrt_ln(ss.a, ss.a, 1.0 / D, 1e-6)
    for i, (t0, n) in enumerate(tiles):
        xt = xts[i % 3]
        mk.dma("sp", xt.t[0:n, :], S["xres"][t0:t0 + n, :], reads=[S["xb"][i]], writes=[xt])
        y = ys[i % 2]
        k.stt(y[0:n, :], xt[0:n, :], ss[0:n, i:i + 1], wbc[0:n, :], ALU.mult, ALU.mult)
        mk.dma("sp", O["y"].t[t0:t0 + n, :], y.t[0:n, :], reads=[y], writes=[O["y"]])
    arena.off = m0


_NC_CACHE = {}


def kernel(**inputs):
    if "nc" not in _NC_CACHE:
        _NC_CACHE["nc"] = build()
    nc = _NC_CACHE["nc"]
    f = lambda a: np.ascontiguousarray(np.asarray(a), dtype=np.float32)
    shared = {}
    for name in ("norm_mix", "w_in", "w_out", "gdn_conv_w", "gdn_a_log", "gdn_dt_bias", "gdn_norm_w", "rwkv_mu", "rwkv_w0",
                 "rwkv_w2", "rwkv_a0", "rwkv_a2", "rwkv_g2", "rwkv_v0", "rwkv_v1", "rwkv_v2", "rwkv_k_k", "rwkv_k_a",
                 "rwkv_ln_w", "rwkv_ln_b", "hgrn_lower_bounds", "hgrn_norm_w", "norm_ffn", "w_gate", "w_up", "w_down"):
        shared[name] = f(inputs[name])
    shared["rwkv_r_k"] = f(inputs["rwkv_r_k"]).reshape(DEPTH, 384)
    shared["norm_final"] = f(inputs["norm_final"]).reshape(1, D)
    xp = f(inputs["x_prompt"]); xs = f(inputs["x_sample"])
    sg = f(inputs["state_gdn"]); sc = f(inputs["state_gdn_conv"]); sr = f(inputs["state_rwkv"])
    ssh = f(inputs["state_rwkv_shift"]); shg = f(inputs["state_hgrn"])
    in_maps = []
    for c in range(8):
        sl = slice(NS * c, NS * (c + 1))
        m = dict(shared)
        m["xin"] = np.ascontiguousarray(np.concatenate([xp[c], xs[sl, 0]], 0))
        m["st_gdn"] = np.ascontiguousarray(sg[:, sl]); m["st_conv"] = np.ascontiguousarray(sc[:, sl])
        m["st_rwkv"] = np.ascontiguousarray(sr[:, sl]); m["st_shift"] = np.ascontiguousarray(ssh[:, sl])
        m["st_hgrn"] = np.ascontiguousarray(shg[:, sl])
        in_maps.append(m)
    res = run_bass_kernel_spmd(nc, in_maps, core_ids=list(range(8)))
    R = res.results
    y_prompt = np.stack([R[c]["y"][:T] for c in range(8)], 0).astype(np.float32)
    y_sample = np.concatenate([R[c]["y"][T:] for c in range(8)], 0)[:, None, :].astype(np.float32)

    def pstack(name):
        return np.stack([R[c][name] for c in range(8)], 1).astype(np.float32)

    def scat(name):
        return np.concatenate([R[c][name] for c in range(8)], 1).astype(np.float32)
    return (y_prompt, y_sample, pstack("gdn_p"), scat("gdn_s"), pstack("conv_p"), scat("conv_s"),
            pstack("rwkv_p"), scat("rwkv_s"), pstack("shift_p"), scat("shift_s"), pstack("hgrn_p"), scat("hgrn_s"))
```
